# Optimizing a Trainium2 kernel written in Bass

```python
import math
import jax, jax.numpy as jnp
from jax import lax
import numpy as np

D_MODEL = 2048
BATCH = 8
SEQ = 2048
DEPTH = 2

GRID_W = 64
CTX_LEN = 256
EXPAND = 2
D_INNER = EXPAND * D_MODEL
N_DIR = 2
N_MIXERS = 2
S5_GROUP = 16
S5_GROUPS = D_INNER // S5_GROUP
S5_STATE = 64
S5_CHUNK = 128
S5_DT_MIN = 1e-3
S5_DT_MAX = 1e-1
RWKV_HEAD = 64
RWKV_HEADS = D_INNER // RWKV_HEAD
LORA_DECAY = 96
LORA_AAA = 96
N_LERP = 6
RMS_EPS = 1e-6
GN_EPS = 64e-5
NORM_EPS = 1e-12

kernel_name = "hybrid_s5_rwkv7_prefix_dit"

F32 = jnp.float32


def rmsnorm(x, g):
    xf = x.astype(F32)
    y = xf * lax.rsqrt(jnp.mean(xf * xf, axis=-1, keepdims=True) + RMS_EPS)
    return (y * g.astype(F32)).astype(x.dtype)


def modulation(cond, ada_w, ada_b):
    m = (jax.nn.silu(cond) @ ada_w + ada_b).reshape(cond.shape[0], 1, 3 * D_MODEL)
    return jnp.split(m, 3, axis=-1)


def s5_discretize(a_re, a_im, log_step, b_re, b_im):
    a_re, a_im = a_re.astype(F32), a_im.astype(F32)
    b_re, b_im = b_re.astype(F32), b_im.astype(F32)
    dt = jnp.exp(log_step.astype(F32))[:, None]
    mag = jnp.exp(dt * a_re)
    ab_re = mag * jnp.cos(dt * a_im)
    ab_im = mag * jnp.sin(dt * a_im)
    den = a_re * a_re + a_im * a_im
    nr, ni = ab_re - 1.0, ab_im
    f_re = ((nr * a_re + ni * a_im) / den)[..., None]
    f_im = ((ni * a_re - nr * a_im) / den)[..., None]
    bb_re = f_re * b_re - f_im * b_im
    bb_im = f_re * b_im + f_im * b_re
    return ab_re, ab_im, bb_re, bb_im


def _complex_affine_combine(e1, e2):
    a1r, a1i, b1r, b1i = e1
    a2r, a2i, b2r, b2i = e2
    return (a2r * a1r - a2i * a1i, a2r * a1i + a2i * a1r,
            a2r * b1r - a2i * b1i + b2r, a2r * b1i + a2i * b1r + b2i)


def s5_chunked_scan(u, h0_re, h0_im, disc, c_re, c_im):
    ab_re, ab_im, bb_re, bb_im = disc
    c_re, c_im = c_re.astype(F32), c_im.astype(F32)
    b, l = u.shape[0], u.shape[1]
    n_chunks = l // S5_CHUNK
    uc = u.reshape(b, n_chunks, S5_CHUNK, S5_GROUPS, S5_GROUP).transpose(1, 2, 0, 3, 4)
    a_re = jnp.broadcast_to(ab_re, (S5_CHUNK, b, S5_GROUPS, S5_STATE))
    a_im = jnp.broadcast_to(ab_im, (S5_CHUNK, b, S5_GROUPS, S5_STATE))

    def chunk_step(carry, u_blk):
        h_re, h_im = carry
        bu_re = jnp.einsum('tbgi,gni->tbgn', u_blk, bb_re)
        bu_im = jnp.einsum('tbgi,gni->tbgn', u_blk, bb_im)
        ac_re, ac_im, hl_re, hl_im = lax.associative_scan(
            _complex_affine_combine, (a_re, a_im, bu_re, bu_im), axis=0)
        hs_re = hl_re + ac_re * h_re - ac_im * h_im
        hs_im = hl_im + ac_re * h_im + ac_im * h_re
        y = (jnp.einsum('tbgn,gin->tbgi', hs_re, c_re)
             - jnp.einsum('tbgn,gin->tbgi', hs_im, c_im))
        return (hs_re[-1], hs_im[-1]), y

    (h_re, h_im), ys = lax.scan(chunk_step, (h0_re, h0_im), uc)
    y = ys.transpose(2, 0, 1, 3, 4).reshape(b, l, S5_GROUPS, S5_GROUP)
    return y, h_re, h_im


def s5_direction(u, h0_re, h0_im, disc, c_re, c_im, reverse):
    b, l, _ = u.shape
    uu = u.astype(F32).reshape(b, l, S5_GROUPS, S5_GROUP)
    if reverse:
        uu = jnp.flip(uu, axis=1)
    y, h_re, h_im = s5_chunked_scan(uu, h0_re, h0_im, disc, c_re, c_im)
    if reverse:
        y = jnp.flip(y, axis=1)
    return y.reshape(b, l, D_INNER), h_re, h_im


def s5_output(y_ssm, u, z, d_skip, glu_w, glu_b, out_w):
    y = jax.nn.gelu(y_ssm + d_skip.astype(F32) * u.astype(F32)).astype(u.dtype)
    y = y * jax.nn.sigmoid(y @ glu_w + glu_b)
    return (y * jax.nn.silu(z)) @ out_w


def s5_mixer(h_lat, h_ctx, need_ctx_out, in_w, a_re, a_im, log_step, b_re, b_im,
             c_re, c_im, d_skip, glu_w, glu_b, out_w):
    u_lat, z_lat = jnp.split(h_lat @ in_w, 2, axis=-1)
    if need_ctx_out:
        u_ctx, z_ctx = jnp.split(h_ctx @ in_w, 2, axis=-1)
    else:
        u_ctx = h_ctx @ in_w[:, :D_INNER]
    h0 = jnp.zeros((h_ctx.shape[0], S5_GROUPS, S5_STATE), F32)
    ys_lat, ys_ctx = [], []
    for d in range(N_DIR):
        disc = s5_discretize(a_re[d], a_im[d], log_step[d], b_re[d], b_im[d])
        rev = d == 1
        y_c, hc_re, hc_im = s5_direction(u_ctx, h0, h0, disc, c_re[d], c_im[d], rev)
        y_l, _, _ = s5_direction(u_lat, hc_re, hc_im, disc, c_re[d], c_im[d], rev)
        ys_lat.append(y_l)
        ys_ctx.append(y_c)
    o_lat = s5_output(ys_lat[0] + ys_lat[1], u_lat, z_lat, d_skip, glu_w, glu_b, out_w)
    o_ctx = None
    if need_ctx_out:
        o_ctx = s5_output(ys_ctx[0] + ys_ctx[1], u_ctx, z_ctx, d_skip, glu_w, glu_b, out_w)
    return o_lat, o_ctx


def q_shift(h):
    b, l, d = h.shape
    rows = l // GRID_W
    g = h.reshape(b, rows, GRID_W, d)
    q = d // 4
    left = jnp.pad(g[:, :, :-1, :q], ((0, 0), (0, 0), (1, 0), (0, 0)))
    right = jnp.pad(g[:, :, 1:, q:2 * q], ((0, 0), (0, 0), (0, 1), (0, 0)))
    up = jnp.pad(g[:, :-1, :, 2 * q:3 * q], ((0, 0), (1, 0), (0, 0), (0, 0)))
    down = jnp.pad(g[:, 1:, :, 3 * q:], ((0, 0), (0, 1), (0, 0), (0, 0)))
    return jnp.concatenate([left, right, up, down], axis=-1).reshape(b, l, d)


def seq_shift(h):
    half = h.shape[-1] // 2
    prev = jnp.pad(h[:, :-1, :half], ((0, 0), (1, 0), (0, 0)))
    nxt = jnp.pad(h[:, 1:, half:], ((0, 0), (0, 1), (0, 0)))
    return jnp.concatenate([prev, nxt], axis=-1)


def rwkv_stream(h, shifted, with_gate, mu, in_w, w0, w1, w2, a0, a1, a2, k_k, k_a):
    b, l, _ = h.shape
    delta = shifted - h
    n_proj = 4 if with_gate else 3
    proj = [(h + delta * mu[i]) @ in_w[i] for i in range(n_proj)]
    z = proj[3] if with_gate else None
    xw = h + delta * mu[4]
    xa = h + delta * mu[5]

    def heads(t):
        return t.astype(F32).reshape(b, l, RWKV_HEADS, RWKV_HEAD)

    r, k, v = heads(proj[0]), heads(proj[1]), heads(proj[2])
    kk = k * k_k.astype(F32).reshape(RWKV_HEADS, RWKV_HEAD)
    kk = kk / jnp.maximum(jnp.sqrt(jnp.sum(kk * kk, axis=-1, keepdims=True)), NORM_EPS)
    k_a_h = k_a.astype(F32).reshape(RWKV_HEADS, RWKV_HEAD)
    dirs = []
    for d in range(N_DIR):
        w_log = -jax.nn.softplus(-heads(w0[d] + jnp.tanh(xw @ w1[d]) @ w2[d])) - 0.5
        decay = jnp.exp(-jnp.exp(w_log))
        a_rate = jax.nn.sigmoid(heads(a0[d] + (xa @ a1[d]) @ a2[d]))
        k_d = k * (1.0 + (a_rate - 1.0) * k_a_h)
        dirs.append((decay, k_d, -kk, kk * a_rate))
    return r, v, z, dirs


def rwkv_scan(r, decay, k, v, a, bvec, s0):
    def step(s, inp):
        r_t, w_t, k_t, v_t, a_t, b_t = inp
        sa = jnp.einsum('bhvk,bhk->bhv', s, a_t)
        s = (s * w_t[:, :, None, :] + sa[..., None] * b_t[:, :, None, :]
             + v_t[..., None] * k_t[:, :, None, :])
        return s, jnp.einsum('bhvk,bhk->bhv', s, r_t)

    xs = tuple(jnp.moveaxis(t, 1, 0) for t in (r, decay, k, v, a, bvec))
    s, ys = lax.scan(step, s0, xs)
    return jnp.moveaxis(ys, 0, 1), s


def rwkv_direction(r, v, dir_terms, s0, reverse):
    decay, k_d, a_vec, b_vec = dir_terms
    seqs = (r, decay, k_d, v, a_vec, b_vec)
    if reverse:
        seqs = tuple(jnp.flip(t, axis=1) for t in seqs)
    y, s = rwkv_scan(*seqs, s0)
    if reverse:
        y = jnp.flip(y, axis=1)
    return y, s


def rwkv_output(y_sum, r, k_sum, v, z, r_k, ln_w, ln_b, out_w):
    b, l = y_sum.shape[0], y_sum.shape[1]
    mu_ = jnp.mean(y_sum, axis=-1, keepdims=True)
    var = jnp.mean(jnp.square(y_sum - mu_), axis=-1, keepdims=True)
    yn = ((y_sum - mu_) * lax.rsqrt(var + GN_EPS)).reshape(b, l, D_INNER)
    yn = yn * ln_w.astype(F32) + ln_b.astype(F32)
    bonus = jnp.sum(r * k_sum * r_k.astype(F32), axis=-1, keepdims=True) * v
    y = (yn + bonus.reshape(b, l, D_INNER)).astype(z.dtype)
    return (y * jax.nn.silu(z)) @ out_w


def rwkv_mixer(h_lat, h_ctx, need_ctx_out, mu, in_w, w0, w1, w2, a0, a1, a2,
               k_k, k_a, r_k, ln_w, ln_b, out_w):
    lora = (mu, in_w, w0, w1, w2, a0, a1, a2, k_k, k_a)
    r_l, v_l, z_l, dirs_l = rwkv_stream(h_lat, q_shift(h_lat), True, *lora)
    r_c, v_c, z_c, dirs_c = rwkv_stream(h_ctx, seq_shift(h_ctx), need_ctx_out, *lora)
    s0 = jnp.zeros((h_ctx.shape[0], RWKV_HEADS, RWKV_HEAD, RWKV_HEAD), F32)
    ys_lat, ys_ctx = [], []
    for d in range(N_DIR):
        rev = d == 1
        y_c, s_c = rwkv_direction(r_c, v_c, dirs_c[d], s0, rev)
        y_l, _ = rwkv_direction(r_l, v_l, dirs_l[d], s_c, rev)
        ys_lat.append(y_l)
        ys_ctx.append(y_c)
    o_lat = rwkv_output(ys_lat[0] + ys_lat[1], r_l, dirs_l[0][1] + dirs_l[1][1], v_l, z_l,
                        r_k, ln_w, ln_b, out_w)
    o_ctx = None
    if need_ctx_out:
        o_ctx = rwkv_output(ys_ctx[0] + ys_ctx[1], r_c, dirs_c[0][1] + dirs_c[1][1], v_c,
                            z_c, r_k, ln_w, ln_b, out_w)
    return o_lat, o_ctx


def setup_inputs(seed: int = 0) -> dict:
    key = jax.random.key(seed)
    ks = iter(jax.random.split(key, 64))

    def nrm(shape, std):
        return jax.random.normal(next(ks), shape, F32) * std

    def gain(shape):
        return 1.0 + nrm(shape, 0.02)

    D, E, G, N, GC = D_MODEL, D_INNER, S5_GROUPS, S5_STATE, S5_GROUP
    p = {}
    p["x"] = nrm((BATCH, SEQ, D), 1.0)
    p["c"] = nrm((BATCH, D), 1.0)
    p["ctx"] = nrm((BATCH, CTX_LEN, D), 1.0)
    p["c_ctx"] = nrm((D,), 1.0)
    p["l0_norm_g"] = gain((D,))
    p["l0_ada_w"] = nrm((D, 3 * D), 0.5 * D ** -0.5)
    p["l0_ada_b"] = nrm((3 * D,), 0.01)
    p["l0_in_w"] = nrm((D, 2 * E), D ** -0.5)
    n_idx = jnp.arange(N, dtype=F32)
    p["l0_a_re"] = -0.5 + nrm((N_DIR, G, N), 0.01)
    p["l0_a_im"] = math.pi * n_idx + nrm((N_DIR, G, N), 0.01)
    p["l0_log_step"] = jax.random.uniform(next(ks), (N_DIR, G), F32,
                                          math.log(S5_DT_MIN), math.log(S5_DT_MAX))
    p["l0_b_re"] = nrm((N_DIR, G, N, GC), (2.0 * GC) ** -0.5)
    p["l0_b_im"] = nrm((N_DIR, G, N, GC), (2.0 * GC) ** -0.5)
    p["l0_c_re"] = nrm((N_DIR, G, GC, N), (2.0 * N) ** -0.5)
    p["l0_c_im"] = nrm((N_DIR, G, GC, N), (2.0 * N) ** -0.5)
    p["l0_d"] = nrm((E,), 1.0)
    p["l0_glu_w"] = nrm((E, E), E ** -0.5)
    p["l0_glu_b"] = nrm((E,), 0.01)
    p["l0_out_w"] = nrm((E, D), E ** -0.5)
    p["l1_norm_g"] = gain((D,))
    p["l1_ada_w"] = nrm((D, 3 * D), 0.5 * D ** -0.5)
    p["l1_ada_b"] = nrm((3 * D,), 0.01)
    p["l1_mu"] = jax.random.uniform(next(ks), (N_LERP, D), F32)
    p["l1_in_w"] = nrm((4, D, E), D ** -0.5)
    p["l1_w0"] = jax.random.uniform(next(ks), (N_DIR, E), F32, -6.0, 0.0)
    p["l1_w1"] = nrm((N_DIR, D, LORA_DECAY), D ** -0.5)
    p["l1_w2"] = nrm((N_DIR, LORA_DECAY, E), 0.1 * LORA_DECAY ** -0.5)
    p["l1_a0"] = nrm((N_DIR, E), 0.1)
    p["l1_a1"] = nrm((N_DIR, D, LORA_AAA), D ** -0.5)
    p["l1_a2"] = nrm((N_DIR, LORA_AAA, E), 0.1 * LORA_AAA ** -0.5)
    p["l1_k_k"] = 0.85 + nrm((E,), 0.02)
    p["l1_k_a"] = gain((E,))
    p["l1_r_k"] = nrm((RWKV_HEADS, RWKV_HEAD), 0.1)
    p["l1_ln_w"] = gain((E,))
    p["l1_ln_b"] = nrm((E,), 0.01)
    p["l1_out_w"] = nrm((E, D), E ** -0.5)
    p["final_norm_g"] = gain((D,))
    return p


def reference(x, c, ctx, c_ctx,
              l0_norm_g, l0_ada_w, l0_ada_b, l0_in_w, l0_a_re, l0_a_im, l0_log_step,
              l0_b_re, l0_b_im, l0_c_re, l0_c_im, l0_d, l0_glu_w, l0_glu_b, l0_out_w,
              l1_norm_g, l1_ada_w, l1_ada_b, l1_mu, l1_in_w, l1_w0, l1_w1, l1_w2,
              l1_a0, l1_a1, l1_a2, l1_k_k, l1_k_a, l1_r_k, l1_ln_w, l1_ln_b, l1_out_w,
              final_norm_g):
    mixers = (s5_mixer, rwkv_mixer)
    layers = (
        ((l0_norm_g, l0_ada_w, l0_ada_b),
         (l0_in_w, l0_a_re, l0_a_im, l0_log_step, l0_b_re, l0_b_im, l0_c_re, l0_c_im,
          l0_d, l0_glu_w, l0_glu_b, l0_out_w)),
        ((l1_norm_g, l1_ada_w, l1_ada_b),
         (l1_mu, l1_in_w, l1_w0, l1_w1, l1_w2, l1_a0, l1_a1, l1_a2, l1_k_k, l1_k_a,
          l1_r_k, l1_ln_w, l1_ln_b, l1_out_w)),
    )
    x_lat, x_ctx = x, ctx
    for i in range(DEPTH):
        (norm_g, ada_w, ada_b), mixer_params = layers[i]
        mixer = mixers[i % N_MIXERS]
        need_ctx_out = i < DEPTH - 1
        sh_l, sc_l, g_l = modulation(c, ada_w, ada_b)
        sh_c, sc_c, g_c = modulation(c_ctx[None], ada_w, ada_b)
        h_lat = rmsnorm(x_lat, norm_g) * (1.0 + sc_l) + sh_l
        h_ctx = rmsnorm(x_ctx, norm_g) * (1.0 + sc_c) + sh_c
        o_lat, o_ctx = mixer(h_lat, h_ctx, need_ctx_out, *mixer_params)
        x_lat = x_lat + g_l * o_lat
        if need_ctx_out:
            x_ctx = x_ctx + g_c * o_ctx
    return rmsnorm(x_lat, final_norm_g)
```

```python
import contextlib, math
import numpy as np
import concourse.bass as bass
import concourse.mybir as mybir
from concourse.bass_utils import run_bass_kernel_spmd

F32 = mybir.dt.float32
BF16 = mybir.dt.bfloat16
ALU = mybir.AluOpType
AF = mybir.ActivationFunctionType
AX = mybir.AxisListType

D = 2048
E = 4096
T = 2304
TC = 256
KC = 16
EC = 32
NCH = 18
PI = math.pi
RMS_EPS = 1e-6
GN_EPS = 64e-5

C_IDENT, C_ONES, C_TRIF, C_TRIB, C_SELF, C_SELB = 0, 128, 256, 384, 512, 640
C_S1 = 768
C_MBD = 772
C_W = 780
C2_M4, C2_MN, C2_BONES, C2_RESET = 0, 1024, 1280, 1408
C2_W = 1536


class Ev:
    __slots__ = ("sem", "val", "eng", "key")

    def __init__(self, sem, val, eng, key):
        self.sem, self.val, self.eng, self.key = sem, val, eng, key


class Buf:
    __slots__ = ("t", "w", "r", "name", "excl")

    def __init__(self, t=None, name=""):
        self.t, self.w, self.r, self.name, self.excl = t, None, {}, name, False

    def __getitem__(self, idx):
        return self.t[idx]


class Sched:
    SAME_ENGINE_SYNC = True
    NDMASEM = 24

    def __init__(self, nc, es):
        self.nc = nc
        self.stack = [es]
        self.engs = {"pe": nc.tensor, "act": nc.scalar, "dve": nc.vector, "pool": nc.gpsimd, "sp": nc.sync}
        self.sem = {k: es.enter_context(nc.semaphore("s_" + k)) for k in ("pe", "act", "dve", "pool")}
        self.cnt = {k: 0 for k in self.sem}
        self.dsem = [es.enter_context(nc.semaphore("d%d" % i)) for i in range(2 * self.NDMASEM)]
        self.duse = [0] * (2 * self.NDMASEM)
        self.dnext = {"sp": 0, "pool": 0, "act": 0}
        self.seen = {k: {} for k in self.engs}
        self.nins = 0
        self.uid = 0
        self.scope_bufs = [[]]
        self.pending_free = {}

    def push(self):
        es = contextlib.ExitStack()
        es.__enter__()
        self.stack.append(es)
        self.scope_bufs.append([])

    def pop(self):
        for b in self.scope_bufs.pop():
            evs = list(b.r.values()) + ([b.w] if b.w is not None else [])
            for ev in evs:
                old = self.pending_free.get(ev.key)
                if old is None or old.val < ev.val:
                    self.pending_free[ev.key] = ev
        es = self.stack.pop()
        es.__exit__(None, None, None)

    def _newbuf(self, t, nm):
        b = Buf(t, nm)
        b.r = dict(self.pending_free)
        self.scope_bufs[-1].append(b)
        return b

    def sbuf(self, name, shape, dt=F32):
        self.uid += 1
        nm = "%s_%d" % (name, self.uid)
        return self._newbuf(self.stack[-1].enter_context(self.nc.sbuf_tensor(nm, list(shape), dt)), nm)

    def psum(self, name, shape, dt=F32):
        b = self._newbuf(self.stack[-1].enter_context(self.nc.psum_tensor(name, list(shape), dt)), name)
        b.excl = True
        return b

    def dram(self, name, shape, dt=F32, kind="Internal"):
        return self.nc.dram_tensor(name, list(shape), dt, kind=kind).ap()

    def _wait(self, engname, ev):
        seen = self.seen[engname]
        if seen.get(ev.key, 0) >= ev.val:
            return
        self.engs[engname].wait_ge(ev.sem, ev.val)
        seen[ev.key] = ev.val

    def emit(self, engname, fn, reads=(), writes=(), dma=False):
        waits = []
        for b in reads:
            if b.w is not None:
                waits.append(b.w)
            if b.excl:
                waits.extend(ev for ev in b.r.values() if ev.eng != engname)
        for b in writes:
            if b.w is not None:
                waits.append(b.w)
            waits.extend(b.r.values())
        for ev in waits:
            if ev.eng == engname and not dma:
                if engname == "pe" or not self.SAME_ENGINE_SYNC:
                    continue
            self._wait(engname, ev)
        eng = self.engs[engname]
        if dma:
            i = self.dnext[engname] + (self.NDMASEM if engname == "pool" else 0)
            self.dnext[engname] = (self.dnext[engname] + 1) % self.NDMASEM
            s = self.dsem[i]
            if self.duse[i] > 0:
                self._wait(engname, Ev(s, 16 * self.duse[i], "dma", ("d", i)))
            ins = fn(eng)
            self.duse[i] += 1
            ins.then_inc(s, 16)
            ev = Ev(s, 16 * self.duse[i], "dma", ("d", i))
        else:
            ins = fn(eng)
            self.cnt[engname] += 1
            ins.then_inc(self.sem[engname], 1)
            ev = Ev(self.sem[engname], self.cnt[engname], engname, engname)
        for b in reads:
            b.r[ev.key] = ev
        for b in writes:
            b.w = ev
            b.r = {}
        self.nins += 1
        return ev

    def dma(self, out_b, out_ap, in_b, in_ap, q="sp", slow=False):
        outs = out_b if isinstance(out_b, (list, tuple)) else [out_b]
        ins = in_b if isinstance(in_b, (list, tuple)) else [in_b]
        if slow:
            f = lambda e: e.dma_start(out=out_ap, in_=in_ap, allow_slow_non_contiguous=True)
        else:
            f = lambda e: e.dma_start(out=out_ap, in_=in_ap)
        return self.emit(q, f, reads=ins, writes=outs, dma=True)

    def finish(self, bufs):
        for b in bufs:
            if b.w is not None:
                self._wait("sp", b.w)

    def act(self, ob, o, ib, i, func, bias=None, scale=None, eng="act", extra_reads=()):
        kw = {}
        if bias is not None:
            kw["bias"] = bias
        if scale is not None:
            kw["scale"] = scale
        return self.emit(eng, lambda e: e.activation(out=o, in_=i, func=func, **kw), [ib] + list(extra_reads), [ob])

    def tt(self, eng, ob, o, ab, a, bb, b, op):
        return self.emit(eng, lambda e: e.tensor_tensor(out=o, in0=a, in1=b, op=op), [ab, bb], [ob])

    def ts(self, eng, ob, o, ab, a, s1, s2, op0, op1=None, extra_reads=()):
        if op1 is None:
            f = lambda e: e.tensor_scalar(out=o, in0=a, scalar1=s1, scalar2=None, op0=op0)
        else:
            f = lambda e: e.tensor_scalar(out=o, in0=a, scalar1=s1, scalar2=s2, op0=op0, op1=op1)
        return self.emit(eng, f, [ab] + list(extra_reads), [ob])

    def stt(self, eng, ob, o, ab, a, sc, bb, b, op0, op1, extra_reads=()):
        return self.emit(eng, lambda e: e.scalar_tensor_tensor(out=o, in0=a, scalar=sc, in1=b, op0=op0, op1=op1),
                         [ab, bb] + list(extra_reads), [ob])

    def copy(self, eng, ob, o, ib, i):
        if eng == "act":
            return self.emit(eng, lambda e: e.activation(out=o, in_=i, func=AF.Copy), [ib], [ob])
        return self.emit(eng, lambda e: e.tensor_copy(out=o, in_=i), [ib], [ob])

    def mm(self, ob, o, lb, lhsT, rb, rhs, start=True, stop=True):
        return self.emit("pe", lambda e: e.matmul(out=o, lhsT=lhsT, rhs=rhs, start=start, stop=stop), [lb, rb], [ob])

    def tr(self, ob, o, ib, i, identb, ident):
        return self.emit("pe", lambda e: e.transpose(out=o, in_=i, identity=ident), [ib, identb], [ob])


class DramT:
    def __init__(self, S, name, shape, dt, ru, cu, kind="Internal"):
        self.ap = S.dram(name, shape, dt, kind=kind)
        self.ru, self.cu = ru, cu
        self.nr, self.ncu = (shape[0] + ru - 1) // ru, (shape[1] + cu - 1) // cu
        self.units = [[Buf(None, "%s_%d_%d" % (name, i, j)) for j in range(self.ncu)] for i in range(self.nr)]

    def bufs(self, r0, r1, c0, c1):
        out = []
        for i in range(r0 // self.ru, (r1 - 1) // self.ru + 1):
            for j in range(c0 // self.cu, (c1 - 1) // self.cu + 1):
                out.append(self.units[i][j])
        return out


def lat_blocks():
    return [(0, 256, 1)] + [(256 + 512 * i, 512, 0) for i in range(4)]


def build_program(dbg=None):
    dbg = dbg or ""
    nc = bass.Bass("TRN2", target_bir_lowering=False)
    es = contextlib.ExitStack()
    with es:
        S = Sched(nc, es)
        inp = {}

        def ext(name, shape):
            b = Buf(nc.dram_tensor(name, list(shape), F32, kind="ExternalInput").ap(), name)
            inp[name] = b
            return b

        x_in = ext("x", [2048, D])
        ctx_in = ext("ctx", [TC, D])
        cond_in = ext("cond", [2, D])
        cst_in = ext("cst", [128, C_W])
        cst2_in = ext("cst2", [128, C2_W])
        W = {}
        for nm, shp in [
            ("l0_norm_g", [D]), ("l0_ada_w", [D, 3 * D]), ("l0_ada_b", [3 * D]), ("l0_in_w", [D, 2 * E]),
            ("l0_a_re", [2, 256, 64]), ("l0_a_im", [2, 256, 64]), ("l0_log_step", [2, 256]),
            ("l0_b_re", [2, 256, 64, 16]), ("l0_b_im", [2, 256, 64, 16]),
            ("l0_c_re", [2, 256, 16, 64]), ("l0_c_im", [2, 256, 16, 64]),
            ("l0_d", [E]), ("l0_glu_w", [E, E]), ("l0_glu_b", [E]), ("l0_out_w", [E, D]),
            ("l1_norm_g", [D]), ("l1_ada_w", [D, 3 * D]), ("l1_ada_b", [3 * D]), ("l1_mu", [6, D]),
            ("l1_in_w", [4, D, E]), ("l1_w0", [2, E]), ("l1_w1", [2, D, 96]), ("l1_w2", [2, 96, E]),
            ("l1_a0", [2, E]), ("l1_a1", [2, D, 96]), ("l1_a2", [2, 96, E]), ("l1_k_k", [E]), ("l1_k_a", [E]),
            ("l1_r_k", [64, 64]), ("l1_ln_w", [E]), ("l1_ln_b", [E]), ("l1_out_w", [E, D]), ("final_norm_g", [D]),
        ]:
            W[nm] = ext(nm, shp)
        out_t = Buf(nc.dram_tensor("out", [2048, D], F32, kind="ExternalOutput").ap(), "out")

        xT = DramT(S, "xT", [D, T], F32, D, 256)
        uT = DramT(S, "uT", [E, T], F32, 128, T)
        szT = DramT(S, "szT", [E, T], BF16, 128, T)
        vT = DramT(S, "vT", [E, T], BF16, 128, T)
        s5c = DramT(S, "s5c", [4 * 512, 64], F32, 512, 64)

        cst = S.sbuf("cst", [128, C_W])
        S.dma(cst, cst[:], cst_in, cst_in[:, :])
        ident = cst[:, C_IDENT:C_IDENT + 128]
        ones = cst[:, C_ONES:C_ONES + 128]
        def mkps(n=8):
            return [S.psum("ps%d_%d" % (i, S.uid), [128, 512]) for i in range(n)]

        dbg_out = {}

        def stage_transpose_in():
            S.push()
            ps = mkps()
            xin = [S.sbuf("xin", [128, D]) for _ in range(2)]
            xtt = [S.sbuf("xtt", [128, KC, 128]) for _ in range(2)]
            xTv = xT.ap.rearrange("(kc p) t -> p kc t", p=128)
            for i in range(NCH):
                src_b, src = (ctx_in, ctx_in[i * 128:(i + 1) * 128, :]) if i < 2 else (x_in, x_in[(i - 2) * 128:(i - 1) * 128, :])
                xi, xo = xin[i % 2], xtt[i % 2]
                S.dma(xi, xi[:], src_b, src)
                for q in range(4):
                    pb = ps[(i * 4 + q) % 8]
                    for j in range(4):
                        kc = 4 * q + j
                        S.tr(pb, pb[:, j * 128:(j + 1) * 128], xi, xi[:, kc * 128:(kc + 1) * 128], cst, ident)
                    S.copy("act" if q % 2 == 0 else "dve", xo, xo[:, 4 * q:4 * q + 4, :],
                           pb, pb[:, :].rearrange("p (a b) -> p a b", a=4))
                S.dma(xT.bufs(0, D, i * 128, (i + 1) * 128), xTv[:, :, i * 128:(i + 1) * 128], xo, xo[:])
            S.pop()

        def stage_ada(pre, modA, modB, modG):
            S.push()
            ps = mkps()
            condT = S.sbuf("condT", [128, 2, KC])
            for r in range(2):
                S.dma(condT, condT[:, r, :], cond_in, cond_in.t[r, :].rearrange("(kc p) -> p kc", p=128), slow=True)
            S.act(condT, condT[:], condT, condT[:], AF.Silu)
            adab = S.sbuf("adab", [128, 48])
            S.dma(adab, adab[:], W[pre + "ada_b"], W[pre + "ada_b"].t.rearrange("(jc p) -> p jc", p=128), slow=True)
            normg = S.sbuf("normg", [128, KC])
            S.dma(normg, normg[:], W[pre + "norm_g"], W[pre + "norm_g"].t.rearrange("(kc p) -> p kc", p=128), slow=True)
            wb = [S.sbuf("adaw", [128, KC, 512]) for _ in range(2)]
            wv = W[pre + "ada_w"].t.rearrange("(kc p) j -> p kc j", p=128)
            pA = ps[7]
            for grp in range(12):
                w = wb[grp % 2]
                S.dma(w, w[:], W[pre + "ada_w"], wv[:, :, grp * 512:(grp + 1) * 512])
                for jj in range(4):
                    jc = grp * 4 + jj
                    for kc in range(KC):
                        S.mm(pA, pA[:, jc * 2:jc * 2 + 2], w, w[:, kc, jj * 128:(jj + 1) * 128], condT, condT[:, :, kc],
                             start=(kc == 0), stop=(kc == KC - 1))
            mod = S.sbuf("mod", [128, 48, 2])
            S.tt("dve", mod, mod[:], pA, pA[:, 0:96].rearrange("p (j r) -> p j r", r=2),
                 adab, adab[:, :].unsqueeze(2).to_broadcast([128, 48, 2]), ALU.add)
            S.copy("dve", modB, modB[:], mod, mod[:, 0:16, :])
            S.copy("dve", modG, modG[:], mod, mod[:, 32:48, :])
            S.stt("dve", modA, modA[:], mod, mod[:, 16:32, :], 1.0, normg,
                  normg[:, :].unsqueeze(2).to_broadcast([128, KC, 2]), ALU.add, ALU.mult)
            S.pop()

        def stage_norm(hT, modA, modB):
            S.push()
            ps = mkps()
            xb = [S.sbuf("xb", [128, KC, 256]) for _ in range(2)]
            sq = S.sbuf("sq", [128, KC, 256])
            tmp = S.sbuf("ntmp", [128, KC, 256])
            rstd = S.sbuf("rstd", [128, 256])
            xTv = xT.ap.rearrange("(kc p) t -> p kc t", p=128)
            for bi in range(T // 256):
                c0 = bi * 256
                r = 1 if bi == 0 else 0
                x = xb[bi % 2]
                S.dma(x, x[:], xT.bufs(0, D, c0, c0 + 256), xTv[:, :, c0:c0 + 256])
                S.act(sq, sq[:], x, x[:], AF.Square)
                pb = ps[bi % 2]
                for kc in range(KC):
                    S.mm(pb, pb[:, 0:256], cst, ones, sq, sq[:, kc, :], start=(kc == 0), stop=(kc == KC - 1))
                S.ts("dve", rstd, rstd[:], pb, pb[:, 0:256], 1.0 / D, RMS_EPS, ALU.mult, ALU.add)
                S.act(rstd, rstd[:], rstd, rstd[:], AF.Sqrt)
                S.emit("dve", lambda e: e.reciprocal(out=rstd[:], in_=rstd[:]), [rstd], [rstd])
                for kc in range(KC):
                    S.stt("dve", tmp, tmp[:, kc, :], x, x[:, kc, :], modA[:, kc, r:r + 1], rstd, rstd[:],
                          ALU.mult, ALU.mult, extra_reads=[modA])
                    S.act(hT, hT[:, kc, c0:c0 + 256], tmp, tmp[:, kc, :], AF.Identity, bias=modB[:, kc, r:r + 1],
                          extra_reads=[modB])
            S.pop()

        def proj_chunk(hT, wt, pbanks, evac):
            for tb in range(5):
                c0 = tb * 512
                n = min(512, T - c0)
                pb = pbanks[tb % len(pbanks)]
                for kc in range(KC):
                    S.mm(pb, pb[:, 0:n], wt, wt[:, kc, :], hT, hT[:, kc, c0:c0 + n], start=(kc == 0), stop=(kc == KC - 1))
                evac(tb, c0, n, pb)

        def stage_inproj_l0(hT):
            S.push()
            ps = mkps()
            wb = [S.sbuf("inw", [128, KC, 128], BF16) for _ in range(2)]
            ust = [S.sbuf("ust", [128, T]) for _ in range(2)]
            zst = [S.sbuf("zst", [128, T], BF16) for _ in range(2)]
            wv = W["l0_in_w"].t.rearrange("(kc p) e -> p kc e", p=128)
            for ec in range(2 * EC):
                w = wb[ec % 2]
                S.dma(w, w[:], W["l0_in_w"], wv[:, :, ec * 128:(ec + 1) * 128], q="pool")
                if ec < EC:
                    st = ust[ec % 2]
                    proj_chunk(hT, w, ps[0:4], lambda tb, c0, n, pb: S.copy("act" if tb % 2 else "dve", st, st[:, c0:c0 + n], pb, pb[:, 0:n]))
                    S.dma(uT.bufs(ec * 128, ec * 128 + 128, 0, T), uT.ap[ec * 128:(ec + 1) * 128, :], st, st[:])
                else:
                    st = zst[ec % 2]
                    proj_chunk(hT, w, ps[0:4], lambda tb, c0, n, pb: S.act(st, st[:, c0:c0 + n], pb, pb[:, 0:n], AF.Silu))
                    r0 = (ec - EC) * 128
                    S.dma(szT.bufs(r0, r0 + 128, 0, T), szT.ap[r0:r0 + 128, :], st, st[:])
            S.pop()

        MAGIC = 12582912.0

        def sin_arg(eng, ob, o, ib, i, mul, add, tb, t):
            if isinstance(mul, float):
                S.ts(eng, ob, o, ib, i, mul, add, ALU.mult, ALU.add)
            else:
                S.ts(eng, ob, o, ib, i, mul, None, ALU.mult, extra_reads=[cst])
                if add != 0.0:
                    S.ts(eng, ob, o, ob, o, add, None, ALU.add)
            S.ts(eng, tb, t, ob, o, 1.0 / (2 * PI), MAGIC, ALU.mult, ALU.add)
            S.ts(eng, tb, t, tb, t, -MAGIC, None, ALU.add)
            S.stt(eng, ob, o, tb, t, -2 * PI, ob, o, ALU.mult, ALU.add)

        SIN_SC = 0.999999

        def rr(*gens):
            gens = list(gens)
            while gens:
                for g in list(gens):
                    try:
                        next(g)
                        yield
                    except StopIteration:
                        gens.remove(g)

        def stage_s5(ngb=EC):
            S.push()
            pBu = S.psum("pBu", [128, 1024]); pG = S.psum("pG", [128, 1024]); pT = S.psum("pT", [128, 1024])
            pG2 = S.psum("pG2", [128, 1024])
            pY = pG2
            pX = pG
            Bbar = [S.sbuf("Bbar%d" % c, [128, 64, 64]) for c in range(2)]
            Cn = S.sbuf("Cn", [128, 64, 128], BF16)
            dsk = S.sbuf("dsk", [128, EC])
            S.dma(dsk, dsk[:], W["l0_d"], W["l0_d"].t.rearrange("(gb p) -> p gb", p=128), slow=True)
            tri = [S.sbuf("tri%d" % d, [128, 128], BF16) for d in range(2)]
            S.copy("dve", tri[0], tri[0][:], cst, cst[:, C_TRIF:C_TRIF + 128])
            S.copy("dve", tri[1], tri[1][:], cst, cst[:, C_TRIB:C_TRIB + 128])
            sel = [cst[:, C_SELF:C_SELF + 128], cst[:, C_SELB:C_SELB + 128]]
            stop_k = int(dbg.split(":")[1]) if (dbg and ":" in dbg) else 99

            def ckpt(k):
                if k != stop_k:
                    return False
                o_ = Buf(nc.dram_tensor("dbg_ck", [128, EC], F32, kind="ExternalOutput").ap(), "dbg_ck")
                S.dma(o_, o_[:, :], dsk, dsk[:]); S.finish([o_])
                return True
            if ckpt(1):
                S.pop(); return
            S.push()
            cv = lambda ap3: ap3.rearrange("d g n -> (d g) n").rearrange("(q p) n -> p q n", p=128)
            are = S.sbuf("are", [128, 4, 64]); aim = S.sbuf("aim", [128, 4, 64])
            S.dma(are, are[:], W["l0_a_re"], cv(W["l0_a_re"].t))
            S.dma(aim, aim[:], W["l0_a_im"], cv(W["l0_a_im"].t))
            ls = S.sbuf("ls", [128, 4])
            S.dma(ls, ls[:], W["l0_log_step"], W["l0_log_step"].t.rearrange("d g -> (d g)").rearrange("(q p) -> p q", p=128), slow=True)
            S.act(ls, ls[:], ls, ls[:], AF.Exp)
            dtb = ls[:, :].unsqueeze(2).to_broadcast([128, 4, 64])
            dtar = S.sbuf("dtar", [128, 4, 64]); dtai = S.sbuf("dtai", [128, 4, 64])
            S.tt("dve", dtar, dtar[:], are, are[:], ls, dtb, ALU.mult)
            S.tt("dve", dtai, dtai[:], aim, aim[:], ls, dtb, ALU.mult)
            mag = S.sbuf("mag", [128, 4, 64]); sn = S.sbuf("sn", [128, 4, 64]); cs = S.sbuf("cs", [128, 4, 64])
            S.act(mag, mag[:], dtar, dtar[:], AF.Exp)
            rt = S.sbuf("rt", [128, 4, 64])
            sin_arg("dve", sn, sn[:], dtai, dtai[:], 1.0, 0.0, rt, rt[:])
            S.act(sn, sn[:], sn, sn[:], AF.Sin, scale=SIN_SC)
            sin_arg("dve", cs, cs[:], dtai, dtai[:], 1.0, 0.5 * PI, rt, rt[:])
            S.act(cs, cs[:], cs, cs[:], AF.Sin, scale=SIN_SC)
            nr = S.sbuf("nr", [128, 4, 64]); ni = S.sbuf("ni", [128, 4, 64])
            S.tt("dve", nr, nr[:], mag, mag[:], cs, cs[:], ALU.mult)
            S.ts("dve", nr, nr[:], nr, nr[:], -1.0, None, ALU.add)
            S.tt("dve", ni, ni[:], mag, mag[:], sn, sn[:], ALU.mult)
            den = S.sbuf("den", [128, 4, 64]); t0 = S.sbuf("t0", [128, 4, 64])
            S.tt("dve", den, den[:], are, are[:], are, are[:], ALU.mult)
            S.tt("dve", t0, t0[:], aim, aim[:], aim, aim[:], ALU.mult)
            S.tt("dve", den, den[:], den, den[:], t0, t0[:], ALU.add)
            S.emit("dve", lambda e: e.reciprocal(out=den[:], in_=den[:]), [den], [den])
            fre = S.sbuf("fre", [128, 4, 64]); fim = S.sbuf("fim", [128, 4, 64])
            S.tt("dve", fre, fre[:], nr, nr[:], are, are[:], ALU.mult)
            S.tt("dve", t0, t0[:], ni, ni[:], aim, aim[:], ALU.mult)
            S.tt("dve", fre, fre[:], fre, fre[:], t0, t0[:], ALU.add)
            S.tt("dve", fre, fre[:], fre, fre[:], den, den[:], ALU.mult)
            S.tt("dve", fim, fim[:], ni, ni[:], are, are[:], ALU.mult)
            S.tt("dve", t0, t0[:], nr, nr[:], aim, aim[:], ALU.mult)
            S.tt("dve", fim, fim[:], fim, fim[:], t0, t0[:], ALU.subtract)
            S.tt("dve", fim, fim[:], fim, fim[:], den, den[:], ALU.mult)
            if ckpt(2):
                S.pop(); S.pop(); return
            for k, tsrc in enumerate((dtar, dtai, fre, fim)):
                S.dma(s5c.bufs(k * 512, k * 512 + 512, 0, 64),
                      s5c.ap[k * 512:(k + 1) * 512, :].rearrange("(q p) n -> p q n", p=128), tsrc, tsrc[:])
            if ckpt(3):
                S.pop(); S.pop(); return
            S.pop()
            S.push()
            Bn = S.sbuf("Bn", [128, 512, 16])
            for q in range(16):
                S.dma(Bn, Bn[0:64, q * 32:(q + 1) * 32, :], W["l0_b_re"], W["l0_b_re"].t.rearrange("d g n j -> n (d g) j")[:, q * 32:(q + 1) * 32, :])
                S.dma(Bn, Bn[64:128, q * 32:(q + 1) * 32, :], W["l0_b_im"], W["l0_b_im"].t.rearrange("d g n j -> n (d g) j")[:, q * 32:(q + 1) * 32, :])
            Bre = S.sbuf("Bre", [128, 64, 64]); Bim = S.sbuf("Bim", [128, 64, 64])
            Fre = S.sbuf("Fre", [128, 64, 64]); Fim = S.sbuf("Fim", [128, 64, 64])
            for gp in range(8):
                for dst, k in ((Fre, 2), (Fim, 3)):
                    S.dma(dst, dst[gp * 16:(gp + 1) * 16, :, :], s5c.bufs(k * 512, k * 512 + 512, 0, 64),
                          bass.AP(s5c.ap.tensor, s5c.ap.offset + (k * 512 + gp) * 64, [[0, 16], [512, 64], [1, 64]]))
            if stop_k == 4:
                S.finish([Bn, Fre, Fim])
            if stop_k == 41:
                S.finish([Bn])
            if stop_k == 42:
                S.finish([Fre, Fim])
            if ckpt(4) or ckpt(41) or ckpt(42):
                S.pop(); S.pop(); return
            for q in range(16):
                pb = pY if q % 2 else pX
                for j in range(4):
                    dgb = q * 4 + j
                    S.tr(pb, pb[:, j * 128:(j + 1) * 128], Bn, Bn[:, dgb * 8:(dgb + 1) * 8, :].rearrange('p a b -> p (a b)'), cst, ident)
                pv = pb[:, 0:512].rearrange("p (a c n) -> p a c n", a=4, c=2)
                S.copy("act", Bre, Bre[:, q * 4:q * 4 + 4, :], pb, pv[:, :, 0, :])
                S.copy("dve", Bim, Bim[:, q * 4:q * 4 + 4, :], pb, pv[:, :, 1, :])
            if stop_k == 45:
                S.finish([Bre, Bim])
            if ckpt(45):
                S.pop(); S.pop(); return
            tA = S.sbuf("tA", [128, 64, 64])
            S.tt("dve", Bbar[0], Bbar[0][:], Fre, Fre[:], Bre, Bre[:], ALU.mult)
            S.tt("pool", tA, tA[:], Fim, Fim[:], Bim, Bim[:], ALU.mult)
            S.tt("dve", Bbar[0], Bbar[0][:], Bbar[0], Bbar[0][:], tA, tA[:], ALU.subtract)
            S.tt("dve", Bbar[1], Bbar[1][:], Fre, Fre[:], Bim, Bim[:], ALU.mult)
            S.tt("pool", tA, tA[:], Fim, Fim[:], Bre, Bre[:], ALU.mult)
            S.tt("dve", Bbar[1], Bbar[1][:], Bbar[1], Bbar[1][:], tA, tA[:], ALU.add)
            if ckpt(5):
                S.pop(); S.pop(); return
            S.pop()
            S.push()
            Cnat = S.sbuf("Cnat", [128, 64, 2, 64])
            for q in range(4):
                S.dma(Cnat, Cnat[:, q * 16:(q + 1) * 16, 0, :], W["l0_c_re"], W["l0_c_re"].t.rearrange("d (gb gp) i n -> (gp i) (d gb) n", gp=8)[:, q * 16:(q + 1) * 16, :])
                S.dma(Cnat, Cnat[:, q * 16:(q + 1) * 16, 1, :], W["l0_c_im"], W["l0_c_im"].t.rearrange("d (gb gp) i n -> (gp i) (d gb) n", gp=8)[:, q * 16:(q + 1) * 16, :])
            for q in range(16):
                pb = pY if q % 2 else pX
                for j in range(4):
                    dgb = q * 4 + j
                    S.tr(pb, pb[:, j * 128:(j + 1) * 128], Cnat, Cnat[:, dgb, :, :].rearrange('p a b -> p (a b)'), cst, ident)
                pv = pb[:, 0:512].rearrange("p (a m) -> p a m", a=4)
                S.copy("act", Cn, Cn[0:64, q * 4:q * 4 + 4, :], pb, pv[0:64, :, :])
                S.ts("dve", Cn, Cn[64:128, q * 4:q * 4 + 4, :], pb, pv[64:128, :, :], -1.0, None, ALU.mult)
            S.pop()

            u32 = [S.sbuf("u32", [128, T]) for _ in range(2)]
            ubf = S.sbuf("ubf", [128, T], BF16)
            yacc = S.sbuf("yacc", [128, T])
            vst = [S.sbuf("vst", [128, T], BF16) for _ in range(2)]
            ex = S.sbuf("ex", [128, 512]); ex2 = S.sbuf("ex2", [128, 512])
            v3 = lambda t: t[:, :].rearrange("p (g n) -> p g n", g=8)
            v4 = lambda ap2: ap2.rearrange("p (g c n) -> p g c n", g=8, c=2)
            flat = lambda ap4: ap4.rearrange("p g c n -> p (g c n)")
            DB = []
            for d in range(2):
                B_ = {}
                for nm_ in ("drebc", "dimbc", "magm", "magp", "snT", "csT", "Lmr", "Lmi", "Lpr", "Lpi"):
                    B_[nm_] = S.sbuf(nm_, [128, 512])
                B_["BBD"] = S.sbuf("BBD", [128, 8, 2, 64], BF16)
                B_["CBD"] = S.sbuf("CBD", [128, 8, 128], BF16)
                S.emit("dve", lambda e, c_=B_["CBD"]: e.memset(c_[:], 0.0), [], [B_["CBD"]])
                B_["tt"] = [S.sbuf("s5t%d" % k, [128, 8, 64]) for k in range(4)]
                B_["X"] = S.sbuf("X", [128, 8, 2, 64], BF16)
                B_["Hc"] = [S.sbuf("Hc", [128, 8, 2, 64]) for _ in range(2)]
                B_["HTc"] = [S.sbuf("HTc", [128, 8, 128], BF16) for _ in range(2)]
                B_["pBu"] = pBu if d == 0 else pT
                B_["pG"] = pG if d == 0 else pG2
                DB.append(B_)

            def s5_dir(gb, d, u):
                B_ = DB[d]
                drebc, dimbc, magm, magp, snT, csT = (B_[k] for k in ("drebc", "dimbc", "magm", "magp", "snT", "csT"))
                Lmr, Lmi, Lpr, Lpi, BBD, CBD, tt_, X, Hc, HTc = (B_[k] for k in ("Lmr", "Lmi", "Lpr", "Lpi", "BBD", "CBD", "tt", "X", "Hc", "HTc"))
                pBu_, pG_ = B_["pBu"], B_["pG"]
                Bu4 = v4(pBu_[:, :]); G4 = v4(pG_[:, :])
                s1 = cst[:, C_S1 + 2 * d:C_S1 + 2 * d + 1]
                ns1 = cst[:, C_S1 + 2 * d + 1:C_S1 + 2 * d + 2]
                row0 = d * 256 + gb * 8
                for dst, k in ((drebc, 0), (dimbc, 1)):
                    S.dma(dst, dst[:], s5c.bufs(k * 512, k * 512 + 512, 0, 64),
                          bass.AP(s5c.ap.tensor, s5c.ap.offset + (k * 512 + row0) * 64, [[0, 128], [1, 512]]))
                S.act(magm, magm[:], drebc, drebc[:], AF.Exp, scale=ns1, extra_reads=[cst])
                S.act(magp, magp[:], drebc, drebc[:], AF.Exp, scale=s1, extra_reads=[cst])
                yield
                sin_arg("dve", snT, snT[:], dimbc, dimbc[:], s1, 0.0, Lmr, Lmr[:])
                S.act(snT, snT[:], snT, snT[:], AF.Sin, scale=SIN_SC)
                yield
                sin_arg("dve", csT, csT[:], dimbc, dimbc[:], s1, 0.5 * PI, Lmr, Lmr[:])
                S.act(csT, csT[:], csT, csT[:], AF.Sin, scale=SIN_SC)
                yield
                S.tt("dve", Lmr, Lmr[:], magm, magm[:], csT, csT[:], ALU.mult)
                S.stt("dve", Lmi, Lmi[:], magm, magm[:], -1.0, snT, snT[:], ALU.mult, ALU.mult)
                S.tt("pool", Lpr, Lpr[:], magp, magp[:], csT, csT[:], ALU.mult)
                S.tt("pool", Lpi, Lpi[:], magp, magp[:], snT, snT[:], ALU.mult)
                yield
                for c in range(2):
                    for g2 in range(8):
                        S.ts("dve" if g2 % 2 else "pool", BBD, BBD[:, g2, c, :], Bbar[c], Bbar[c][:, d * 32 + gb, :],
                             cst[:, C_MBD + g2:C_MBD + g2 + 1], None, ALU.mult, extra_reads=[cst])
                    yield
                cfull = CBD[:]
                S.copy("pool", CBD, bass.AP(cfull.tensor, cfull.offset, [[cfull.ap[0][0], 128], [144, 8], [1, 16]]),
                       Cn, Cn[:, d * 32 + gb, :].rearrange("p (g i) -> p g i", g=8))
                order = list(range(NCH)) if d == 0 else [1, 0] + list(range(NCH - 1, 1, -1))
                for ci, ch in enumerate(order):
                    c0 = ch * 128
                    for h in range(2):
                        S.mm(pBu_, pBu_[:, h * 512:(h + 1) * 512], ubf, ubf[:, c0:c0 + 128], BBD, flat(BBD[:, 4 * h:4 * h + 4, :, :]))
                    yield
                    S.tt("dve", tt_[0], tt_[0][:], Lmr, v3(Lmr), pBu_, Bu4[:, :, 0, :], ALU.mult)
                    S.tt("dve", tt_[1], tt_[1][:], Lmi, v3(Lmi), pBu_, Bu4[:, :, 1, :], ALU.mult)
                    S.tt("pool", X, X[:, :, 0, :], tt_[0], tt_[0][:], tt_[1], tt_[1][:], ALU.subtract)
                    yield
                    S.tt("dve", tt_[2], tt_[2][:], Lmr, v3(Lmr), pBu_, Bu4[:, :, 1, :], ALU.mult)
                    S.tt("dve", tt_[3], tt_[3][:], Lmi, v3(Lmi), pBu_, Bu4[:, :, 0, :], ALU.mult)
                    S.tt("pool", X, X[:, :, 1, :], tt_[2], tt_[2][:], tt_[3], tt_[3][:], ALU.add)
                    yield
                    hp, hc = Hc[(ci + 1) % 2], Hc[ci % 2]
                    first = ci == 0
                    for h in range(2):
                        S.mm(pG_, pG_[:, h * 512:(h + 1) * 512], tri[d], tri[d][:], X, flat(X[:, 4 * h:4 * h + 4, :, :]), start=True, stop=first)
                        if not first:
                            S.mm(pG_, pG_[:, h * 512:(h + 1) * 512], cst, sel[d], hp, flat(hp[:, 4 * h:4 * h + 4, :, :]), start=False, stop=True)
                    yield
                    S.tt("dve", tt_[0], tt_[0][:], Lpr, v3(Lpr), pG_, G4[:, :, 0, :], ALU.mult)
                    S.tt("dve", tt_[1], tt_[1][:], Lpi, v3(Lpi), pG_, G4[:, :, 1, :], ALU.mult)
                    S.tt("pool", hc, hc[:, :, 0, :], tt_[0], tt_[0][:], tt_[1], tt_[1][:], ALU.subtract)
                    yield
                    S.tt("dve", tt_[2], tt_[2][:], Lpr, v3(Lpr), pG_, G4[:, :, 1, :], ALU.mult)
                    S.tt("dve", tt_[3], tt_[3][:], Lpi, v3(Lpi), pG_, G4[:, :, 0, :], ALU.mult)
                    S.tt("pool", hc, hc[:, :, 1, :], tt_[2], tt_[2][:], tt_[3], tt_[3][:], ALU.add)
                    yield
                    ht = HTc[ci % 2]
                    for gp in range(8):
                        S.tr(pBu_, pBu_[:, gp * 128:(gp + 1) * 128], hc, hc[:, gp, :, :].rearrange("p c n -> p (c n)"), cst, ident)
                    yield
                    S.copy("act", ht, ht[:, :, :], pBu_, pBu_[:, :].rearrange("p (a b) -> p a b", a=8))
                    for gp in range(8):
                        S.mm(pG_, pG_[:, 0:128], CBD, CBD[:, gp, :], ht, ht[:, gp, :], start=(gp == 0), stop=(gp == 7))
                    yield
                    S.tt("dve", yacc, yacc[:, c0:c0 + 128], yacc, yacc[:, c0:c0 + 128], pG_, pG_[:, 0:128], ALU.add)
                    yield

            def rr(*gens):
                gens = list(gens)
                while gens:
                    for g in list(gens):
                        try:
                            next(g)
                            yield
                        except StopIteration:
                            gens.remove(g)

            for gb in range(ngb):
                u = u32[gb % 2]
                S.dma(u, u[:], uT.bufs(gb * 128, gb * 128 + 128, 0, T), uT.ap[gb * 128:(gb + 1) * 128, :])
                S.copy("pool", ubf, ubf[:], u, u[:])
                S.emit("pool", lambda e: e.memset(yacc[:], 0.0), [], [yacc])
                for _ in rr(s5_dir(gb, 0, u), s5_dir(gb, 1, u)):
                    pass
                v = vst[gb % 2]
                for tb in range(5):
                    c0 = tb * 512
                    n = min(512, T - c0)
                    S.stt("dve", ex, ex[:, 0:n], u, u[:, c0:c0 + n], dsk[:, gb:gb + 1], yacc, yacc[:, c0:c0 + n],
                          ALU.mult, ALU.add, extra_reads=[dsk])
                    S.act(ex2, ex2[:, 0:n], ex, ex[:, 0:n], AF.Square)
                    S.ts("dve", ex2, ex2[:, 0:n], ex2, ex2[:, 0:n], 0.044715, 1.0, ALU.mult, ALU.add)
                    S.tt("pool", ex2, ex2[:, 0:n], ex2, ex2[:, 0:n], ex, ex[:, 0:n], ALU.mult)
                    S.act(ex2, ex2[:, 0:n], ex2, ex2[:, 0:n], AF.Sigmoid, scale=2.0 * math.sqrt(2.0 / PI))
                    S.tt("pool", v, v[:, c0:c0 + n], ex, ex[:, 0:n], ex2, ex2[:, 0:n], ALU.mult)
                S.dma(vT.bufs(gb * 128, gb * 128 + 128, 0, T), vT.ap[gb * 128:(gb + 1) * 128, :], v, v[:])
            S.pop()

        def stage_glu_out_l0(modG):
            S.push()
            ps = mkps()
            vb = S.sbuf("vb", [128, EC, 512], BF16)
            zb = S.sbuf("zb", [128, EC, 512], BF16)
            gb_ = S.sbuf("gb", [128, EC, 512], BF16)
            xb = S.sbuf("xbo", [128, KC, 512])
            sg = S.sbuf("sg", [128, 512])
            gw = [S.sbuf("gw", [128, EC, 128], BF16) for _ in range(2)]
            glub = S.sbuf("glub", [128, EC])
            S.dma(glub, glub[:], W["l0_glu_b"], W["l0_glu_b"].t.rearrange("(e p) -> p e", p=128), slow=True)
            gv = W["l0_glu_w"].t.rearrange("(ei p) e -> p ei e", p=128)
            ov = W["l0_out_w"].t.rearrange("(ei p) e -> p ei e", p=128)
            xTv = xT.ap.rearrange("(kc p) t -> p kc t", p=128)
            vTv = vT.ap.rearrange("(e p) t -> p e t", p=128)
            zTv = szT.ap.rearrange("(e p) t -> p e t", p=128)
            wi = 0
            for (c0, n, r) in lat_blocks():
                S.dma(vb, vb[:, :, 0:n], vT.bufs(0, E, 0, T), vTv[:, :, c0:c0 + n])
                S.dma(zb, zb[:, :, 0:n], szT.bufs(0, E, 0, T), zTv[:, :, c0:c0 + n])
                S.dma(xb, xb[:, :, 0:n], xT.bufs(0, D, c0, c0 + n), xTv[:, :, c0:c0 + n])
                for eo in range(EC):
                    w = gw[wi % 2]; wi += 1
                    S.dma(w, w[:], W["l0_glu_w"], gv[:, :, eo * 128:(eo + 1) * 128], q="pool")
                    pb = ps[eo % 4]
                    for ei in range(EC):
                        S.mm(pb, pb[:, 0:n], w, w[:, ei, :], vb, vb[:, ei, 0:n], start=(ei == 0), stop=(ei == EC - 1))
                    S.act(sg, sg[:, 0:n], pb, pb[:, 0:n], AF.Sigmoid, bias=glub[:, eo:eo + 1], extra_reads=[glub])
                    S.tt("dve", sg, sg[:, 0:n], sg, sg[:, 0:n], vb, vb[:, eo, 0:n], ALU.mult)
                    S.tt("dve", gb_, gb_[:, eo, 0:n], sg, sg[:, 0:n], zb, zb[:, eo, 0:n], ALU.mult)
                for dc in range(KC):
                    w = gw[wi % 2]; wi += 1
                    S.dma(w, w[:], W["l0_out_w"], ov[:, :, dc * 128:(dc + 1) * 128], q="pool")
                    pb = ps[4 + dc % 4]
                    for ei in range(EC):
                        S.mm(pb, pb[:, 0:n], w, w[:, ei, :], gb_, gb_[:, ei, 0:n], start=(ei == 0), stop=(ei == EC - 1))
                    S.stt("dve", xb, xb[:, dc, 0:n], pb, pb[:, 0:n], modG[:, dc, r:r + 1], xb, xb[:, dc, 0:n],
                          ALU.mult, ALU.add, extra_reads=[modG])
                S.dma(xT.bufs(0, D, c0, c0 + n), xTv[:, :, c0:c0 + n], xb, xb[:, :, 0:n])
            S.pop()

        cst2 = S.sbuf("cst2", [128, C2_W])
        S.dma(cst2, cst2[:], cst2_in, cst2_in[:, :])
        bones = cst2[:, C2_BONES:C2_BONES + 128]
        rT = DramT(S, "rT", [E, T], F32, 128, T)
        kT = DramT(S, "kT", [E, T], F32, 128, T)
        v1T = DramT(S, "v1T", [E, T], F32, 128, T)
        vtm = DramT(S, "vtm", [T, E], F32, T, 128)
        lwT = [DramT(S, "lwT%d" % d, [E, T], F32, 128, T) for d in range(2)]
        arT = [DramT(S, "arT%d" % d, [E, T], F32, 128, T) for d in range(2)]
        gT = vT

        def colvec(name, src_ap_1d, n):
            t_ = S.sbuf(name, [128, n])
            return t_

        def stage_l1_proj(hT):
            S.push()
            ps = mkps()
            xi = S.sbuf("xi", [128, KC, T], BF16)
            mu = S.sbuf("mu", [128, 6, KC])
            for i in range(6):
                S.dma(mu, mu[:, i, :], W["l1_mu"], W["l1_mu"].t[i, :].rearrange("(kc p) -> p kc", p=128), slow=True)
            omu = S.sbuf("omu", [128, 6, KC])
            S.ts("dve", omu, omu[:], mu, mu[:], -1.0, 1.0, ALU.mult, ALU.add)
            wb = [S.sbuf("l1w", [128, KC, 128], BF16) for _ in range(2)]
            st32 = [S.sbuf("st32", [128, T])] * 2
            stbf = [S.sbuf("stbf", [128, T], BF16)] * 2

            def build_xi(i):
                for kc in range(KC):
                    eng = "dve" if kc % 2 else "pool"
                    S.ts(eng, xi, xi[:, kc, :], hT, hT[:, kc, :], omu[:, i, kc:kc + 1], None, ALU.mult, extra_reads=[omu])
                    m = mu[:, i, kc:kc + 1]
                    xl = xi[:, kc, TC:T].rearrange("p (r c) -> p r c", c=64)
                    hl = hT[:, kc, TC:T].rearrange("p (r c) -> p r c", c=64)
                    q = kc // 4
                    if q == 0:
                        o_, s_ = xl[:, :, 1:64], hl[:, :, 0:63]
                    elif q == 1:
                        o_, s_ = xl[:, :, 0:63], hl[:, :, 1:64]
                    elif q == 2:
                        o_, s_ = xl[:, 1:32, :], hl[:, 0:31, :]
                    else:
                        o_, s_ = xl[:, 0:31, :], hl[:, 1:32, :]
                    S.stt("dve", xi, o_, hT, s_, m, xi, o_, ALU.mult, ALU.add, extra_reads=[mu])
                    if kc < 8:
                        o_, s_ = xi[:, kc, 1:TC], hT[:, kc, 0:TC - 1]
                    else:
                        o_, s_ = xi[:, kc, 0:TC - 1], hT[:, kc, 1:TC]
                    S.stt("dve", xi, o_, hT, s_, m, xi, o_, ALU.mult, ALU.add, extra_reads=[mu])

            cnt = [0]

            def fm_proj(widx, dst, silu=False):
                wv = W["l1_in_w"].t[widx].rearrange("(kc p) e -> p kc e", p=128)
                for ec in range(EC):
                    w = wb[cnt[0] % 2]
                    S.dma(w, w[:], W["l1_in_w"], wv[:, :, ec * 128:(ec + 1) * 128], q="pool")
                    if silu:
                        st = stbf[cnt[0] % 2]
                        proj_chunk(xi, w, ps[0:4], lambda tb, c0, n, pb: S.act(st, st[:, c0:c0 + n], pb, pb[:, 0:n], AF.Silu))
                    else:
                        st = st32[cnt[0] % 2]
                        proj_chunk(xi, w, ps[0:4], lambda tb, c0, n, pb: S.copy("act" if tb % 2 else "dve", st, st[:, c0:c0 + n], pb, pb[:, 0:n]))
                    S.dma(dst.bufs(ec * 128, ec * 128 + 128, 0, T), dst.ap[ec * 128:(ec + 1) * 128, :], st, st[:])
                    cnt[0] += 1

            build_xi(0); fm_proj(0, rT)
            build_xi(1); fm_proj(1, kT)
            build_xi(2); fm_proj(2, v1T)
            S.push()
            wt = [S.sbuf("wvt", [128, KC, 512], BF16)] * 2
            vs = [S.sbuf("vs", [128, 512]) for _ in range(2)]
            wv = W["l1_in_w"].t[2].rearrange("(kc p) e -> p kc e", p=128)
            for cb in range(8):
                w = wt[cb % 2]
                S.dma(w, w[:], W["l1_in_w"], wv[:, :, cb * 512:(cb + 1) * 512], q="pool")
                for tt in range(NCH):
                    pb = ps[4 + tt % 4]
                    for kc in range(KC):
                        S.mm(pb, pb[:, :], xi, xi[:, kc, tt * 128:(tt + 1) * 128], w, w[:, kc, :], start=(kc == 0), stop=(kc == KC - 1))
                    o = vs[tt % 2]
                    S.copy("act" if tt % 2 else "dve", o, o[:], pb, pb[:, :])
                    S.dma(vtm.bufs(0, T, cb * 512, cb * 512 + 512), vtm.ap[tt * 128:(tt + 1) * 128, cb * 512:(cb + 1) * 512], o, o[:])
            S.pop()
            build_xi(3); fm_proj(3, szT, silu=True)
            S.push()
            l1 = S.sbuf("lora1", [128, KC, 96], BF16)
            l2 = S.sbuf("lora2", [96, E], BF16)
            lmid = S.sbuf("lmid", [96, T], BF16)
            b0 = S.sbuf("lb0", [128, EC])
            for (i, n1, n2, n0, dst, is_w) in ((4, "l1_w1", "l1_w2", "l1_w0", lwT, True), (5, "l1_a1", "l1_a2", "l1_a0", arT, False)):
                build_xi(i)
                for d in range(2):
                    S.dma(l1, l1[:], W[n1], W[n1].t[d].rearrange("(kc p) r -> p kc r", p=128), q="pool")
                    S.dma(l2, l2[:], W[n2], W[n2].t[d], q="pool")
                    S.dma(b0, b0[:], W[n0], W[n0].t[d, :].rearrange("(e p) -> p e", p=128), slow=True)
                    for tb in range(5):
                        c0 = tb * 512
                        n = min(512, T - c0)
                        pb = ps[tb % 4]
                        for kc in range(KC):
                            S.mm(pb, pb[0:96, 0:n], l1, l1[:, kc, :], xi, xi[:, kc, c0:c0 + n], start=(kc == 0), stop=(kc == KC - 1))
                        S.act(lmid, lmid[:, c0:c0 + n], pb, pb[0:96, 0:n], AF.Tanh if is_w else AF.Copy)
                    for ec in range(EC):
                        st = st32[cnt[0] % 2]
                        cnt[0] += 1
                        for tb in range(5):
                            c0 = tb * 512
                            n = min(512, T - c0)
                            pb = ps[4 + tb % 4]
                            S.mm(pb, pb[:, 0:n], l2, l2[:, ec * 128:(ec + 1) * 128], lmid, lmid[:, c0:c0 + n])
                            S.act(st, st[:, c0:c0 + n], pb, pb[:, 0:n], AF.Sigmoid, bias=b0[:, ec:ec + 1], extra_reads=[b0])
                        if is_w:
                            S.ts("dve", st, st[:], st, st[:], -math.exp(-0.5), None, ALU.mult)
                        S.dma(dst[d].bufs(ec * 128, ec * 128 + 128, 0, T), dst[d].ap[ec * 128:(ec + 1) * 128, :], st, st[:])
            S.pop()
            S.pop()

        def stage_l1_scan(nec=EC):
            S.push()
            L = 128
            def vec(nm, src):
                t_ = S.sbuf(nm, [128, EC])
                S.dma(t_, t_[:], W[src], W[src].t.rearrange("(e p) -> p e", p=128), slow=True)
                return t_
            kkv = vec("kkv", "l1_k_k"); kav = vec("kav", "l1_k_a"); lnw = vec("lnw", "l1_ln_w"); lnb = vec("lnb", "l1_ln_b")
            rkv = S.sbuf("rkv", [128, EC])
            S.dma(rkv, rkv[:], W["l1_r_k"], W["l1_r_k"].t.rearrange("(e h) k -> (h k) e", h=2), slow=True)
            KKN = S.sbuf("KKN", [128, T]); BON = S.sbuf("BON", [128, T]); YACC = S.sbuf("YACC", [128, T])
            RT = S.sbuf("RT", [128, T]); KT = S.sbuf("KT", [128, T]); AR = S.sbuf("AR", [128, T]); LW = S.sbuf("LW", [128, T])
            CIN = S.sbuf("CIN", [128, T]); TMP = S.sbuf("TMP", [128, T]); AT = S.sbuf("AT", [128, T]); BT = S.sbuf("BT", [128, T])
            GL = S.sbuf("GL", [128, NCH])
            Vp = S.sbuf("Vp", [128, NCH, 128])
            VpA = S.sbuf("VpA", [128, NCH, 128]); VpB = S.sbuf("VpB", [128, NCH, 128])
            S.emit("pool", lambda e: e.memset(VpA[:], 0.0), [], [VpA])
            S.emit("pool", lambda e: e.memset(VpB[:], 0.0), [], [VpB])
            szb = S.sbuf("szb", [128, T], BF16)
            gst = S.sbuf("gst", [128, T], BF16)
            Sbd = [S.sbuf("Sbd", [128, 128]) for _ in range(2)]
            for s_ in Sbd:
                S.emit("dve", lambda e, s_=s_: e.memset(s_[:], 0.0), [], [s_])
            M4 = [[S.sbuf("M4", [128, 4, 128]) for _ in range(2)] for _ in range(2)]
            XN = [[S.sbuf("XN", [128, 128]) for _ in range(4)] for _ in range(2)]
            Wm = [[S.sbuf("Wm", [128, 128]) for _ in range(4)] for _ in range(2)]
            RHS = S.sbuf("RHS", [128, 128]); Us = S.sbuf("Us", [128, 128])
            UpA = S.sbuf("UpA", [128, 128]); UpB = S.sbuf("UpB", [128, 128])
            S.emit("dve", lambda e: e.memset(UpA[:], 0.0), [], [UpA])
            S.emit("dve", lambda e: e.memset(UpB[:], 0.0), [], [UpB])
            bk = S.sbuf("bk", [128, 2, 128]); bkT = S.sbuf("bkT", [128, 2, 128])
            pM = [S.psum("pM%d" % h, [128, 512]) for h in range(2)]
            pN = S.psum("pN", [128, 512]); pC = [S.psum("pC%d" % h, [128, 512]) for h in range(2)]
            pR = S.psum("pR", [128, 512]); pYy = S.psum("pYy", [128, 512]); pS = S.psum("pS", [128, 512])
            v3 = lambda t_: t_[:, :].rearrange("p (c l) -> p c l", l=L)

            for ec in range(nec):
                rows = (ec * 128, ec * 128 + 128)
                S.dma(KT, KT[:], kT.bufs(*rows, 0, T), kT.ap[rows[0]:rows[1], :])
                S.dma(RT, RT[:], rT.bufs(*rows, 0, T), rT.ap[rows[0]:rows[1], :])
                S.dma(AT, AT[:], v1T.bufs(*rows, 0, T), v1T.ap[rows[0]:rows[1], :])
                S.dma(AR, AR[:], arT[0].bufs(*rows, 0, T), arT[0].ap[rows[0]:rows[1], :])
                S.dma(LW, LW[:], arT[1].bufs(*rows, 0, T), arT[1].ap[rows[0]:rows[1], :])
                S.dma(Vp, Vp[:], vtm.bufs(0, T, rows[0], rows[1]), vtm.ap.rearrange("(c p) e -> p c e", p=128)[:, :, rows[0]:rows[1]])
                S.dma(szb, szb[:], szT.bufs(*rows, 0, T), szT.ap[rows[0]:rows[1], :])
                S.copy("pool", VpA, VpA[:, :, 0:64], Vp, Vp[:, :, 0:64])
                S.copy("pool", VpB, VpB[:, :, 64:128], Vp, Vp[:, :, 64:128])
                S.ts("dve", KKN, KKN[:], KT, KT[:], kkv[:, ec:ec + 1], None, ALU.mult, extra_reads=[kkv])
                S.act(TMP, TMP[:], KKN, KKN[:], AF.Square)
                for tb in range(5):
                    c0 = tb * 512; n = min(512, T - c0)
                    S.mm(pR, pR[:, 0:n], cst2, bones, TMP, TMP[:, c0:c0 + n])
                    S.act(CIN, CIN[:, c0:c0 + n], pR, pR[:, 0:n], AF.Sqrt)
                S.ts("dve", CIN, CIN[:], CIN, CIN[:], 1e-12, None, ALU.max)
                S.emit("dve", lambda e: e.reciprocal(out=CIN[:], in_=CIN[:]), [CIN], [CIN])
                S.tt("dve", KKN, KKN[:], KKN, KKN[:], CIN, CIN[:], ALU.mult)
                S.tt("pool", TMP, TMP[:], AR, AR[:], LW, LW[:], ALU.add)
                S.ts("dve", TMP, TMP[:], TMP, TMP[:], -2.0, None, ALU.add)
                S.ts("dve", TMP, TMP[:], TMP, TMP[:], kav[:, ec:ec + 1], None, ALU.mult, extra_reads=[kav])
                S.ts("dve", TMP, TMP[:], TMP, TMP[:], 2.0, None, ALU.add)
                S.tt("pool", TMP, TMP[:], TMP, TMP[:], KT, KT[:], ALU.mult)
                S.stt("dve", TMP, TMP[:], TMP, TMP[:], rkv[:, ec:ec + 1], RT, RT[:], ALU.mult, ALU.mult, extra_reads=[rkv])
                for tb in range(5):
                    c0 = tb * 512; n = min(512, T - c0)
                    S.mm(pR, pR[:, 0:n], cst2, bones, TMP, TMP[:, c0:c0 + n])
                    S.tt("dve", BON, BON[:, c0:c0 + n], AT, AT[:, c0:c0 + n], pR, pR[:, 0:n], ALU.mult)

                for d in range(2):
                    if not (d == 0):
                        S.dma(KT, KT[:], kT.bufs(*rows, 0, T), kT.ap[rows[0]:rows[1], :])
                        S.dma(RT, RT[:], rT.bufs(*rows, 0, T), rT.ap[rows[0]:rows[1], :])
                    if d == 1:
                        S.dma(AR, AR[:], arT[1].bufs(*rows, 0, T), arT[1].ap[rows[0]:rows[1], :])
                    S.dma(LW, LW[:], lwT[d].bufs(*rows, 0, T), lwT[d].ap[rows[0]:rows[1], :])
                    for ch_ in range(NCH):
                        S.emit("dve", lambda e, ch_=ch_: e.tensor_tensor_scan(out=CIN[:, ch_ * L:(ch_ + 1) * L], data0=ones,
                                                                             data1=LW[:, ch_ * L:(ch_ + 1) * L], initial=0.0,
                                                                             op0=ALU.mult, op1=ALU.add), [LW, cst], [CIN])
                    if d == 1:
                        S.tt("dve", TMP, TMP[:], LW, LW[:], CIN, CIN[:], ALU.subtract)
                        S.tt("dve", CIN, v3(CIN), TMP, v3(TMP), CIN, v3(CIN)[:, :, L - 1:L].to_broadcast([128, NCH, L]), ALU.add)
                    S.tt("pool", BT, BT[:], KKN, KKN[:], AR, AR[:], ALU.mult)
                    S.ts("dve", TMP, TMP[:], AR, AR[:], -1.0, None, ALU.add)
                    S.ts("dve", TMP, TMP[:], TMP, TMP[:], kav[:, ec:ec + 1], None, ALU.mult, extra_reads=[kav])
                    S.ts("dve", TMP, TMP[:], TMP, TMP[:], 1.0, None, ALU.add)
                    S.tt("pool", KT, KT[:], KT, KT[:], TMP, TMP[:], ALU.mult)
                    S.act(TMP, TMP[:], CIN, CIN[:], AF.Exp, scale=-1.0)
                    S.tt("dve", BT, BT[:], BT, BT[:], TMP, TMP[:], ALU.mult)
                    S.tt("pool", KT, KT[:], KT, KT[:], TMP, TMP[:], ALU.mult)
                    S.act(TMP, TMP[:], CIN, CIN[:], AF.Exp)
                    S.tt("dve", RT, RT[:], RT, RT[:], TMP, TMP[:], ALU.mult)
                    S.tt("pool", TMP, TMP[:], CIN, CIN[:], LW, LW[:], ALU.subtract)
                    S.act(TMP, TMP[:], TMP, TMP[:], AF.Exp)
                    S.stt("dve", AT, AT[:], KKN, KKN[:], -1.0, TMP, TMP[:], ALU.mult, ALU.mult)
                    endc = (L - 1) if d == 0 else 0
                    S.act(GL, GL[:], CIN, v3(CIN)[:, :, endc], AF.Exp)
                    cur = 0
                    for h in range(2):
                        pb_ = 64 * h
                        S.emit("dve", lambda e, h=h, pb_=pb_: e.memset(Sbd[0][pb_:pb_ + 64, pb_:pb_ + 64], 0.0), [], [Sbd[0]])
                    order = list(range(NCH)) if d == 0 else [1, 0] + list(range(NCH - 1, 1, -1))
                    m4c = cst2[:, C2_M4 + d * 512:C2_M4 + (d + 1) * 512].rearrange("p (a t) -> p a t", a=4)
                    mNc = cst2[:, C2_MN + d * 128:C2_MN + (d + 1) * 128]
                    Wfin = {}

                    def pre_head(h, ci, ch):
                        cols = slice(ch * L, ch * L + L)
                        hp = slice(64 * h, 64 * h + 64)
                        m4 = M4[h][ci % 2]
                        S.mm(pM[h], pM[h][:, 0:128], BT, BT[hp, cols], AT, AT[hp, cols])
                        S.mm(pM[h], pM[h][:, 128:256], BT, BT[hp, cols], RT, RT[hp, cols])
                        S.mm(pM[h], pM[h][:, 256:384], KT, KT[hp, cols], AT, AT[hp, cols])
                        S.mm(pM[h], pM[h][:, 384:512], KT, KT[hp, cols], RT, RT[hp, cols])
                        S.mm(pN, pN[:, h * 128:(h + 1) * 128], AT, AT[hp, cols], BT, BT[hp, cols])
                        yield
                        S.tt("dve", m4, m4[:], cst2, m4c, pM[h], pM[h][:, :].rearrange("p (a t) -> p a t", a=4), ALU.mult)
                        Xc, Nc = XN[h][0], XN[h][1]
                        S.tt("dve", Nc, Nc[:], cst2, mNc, pN, pN[:, h * 128:(h + 1) * 128], ALU.mult)
                        yield
                        S.copy("act", Xc, Xc[:], m4, m4[:, 0, :])
                        Wc = Wm[h][2 * (ci % 2)]
                        S.tt("pool", Wc, Wc[:], m4, m4[:, 0, :], cst, ident, ALU.add)
                        yield
                        xi_, wi_ = 0, 0
                        for j in range(6):
                            Xn, Nn = XN[h][2 - 2 * xi_ + 0], XN[h][2 - 2 * xi_ + 1]
                            pc = pC[h]
                            S.mm(pc, pc[:, 0:128], Xc, Xc[:], Nc, Nc[:])
                            if j < 5:
                                S.mm(pc, pc[:, 128:256], Nc, Nc[:], Xc, Xc[:])
                            yield
                            S.copy("act", Nn, Nn[:], pc, pc[:, 0:128])
                            if j < 5:
                                S.copy("act", Xn, Xn[:], pc, pc[:, 128:256])
                            yield
                            S.mm(pc, pc[:, 256:384], Nn, Nn[:], Wc, Wc[:])
                            yield
                            Wn = Wm[h][2 * (ci % 2) + 1 - wi_]
                            S.tt("dve", Wn, Wn[:], Wc, Wc[:], pc, pc[:, 256:384], ALU.add)
                            yield
                            Xc, Nc, Wc = Xn, Nn, Wn
                            xi_, wi_ = 1 - xi_, 1 - wi_
                        Wfin[(h, ci)] = Wc

                    def seq(ci, ch):
                        cols = slice(ch * L, ch * L + L)
                        S0, S1 = Sbd[ci % 2], Sbd[(ci + 1) % 2]
                        for h in range(2):
                            hp = slice(64 * h, 64 * h + 64)
                            m4 = M4[h][ci % 2]
                            S.mm(pR, pR[:, 64 * h:64 * h + 64], AT, AT[hp, cols], S0, S0[hp, 64 * h:64 * h + 64], start=True, stop=False)
                            S.mm(pR, pR[:, 64 * h:64 * h + 64], m4, m4[:, 2, :], Vp, Vp[:, ch, 64 * h:64 * h + 64], start=False, stop=True)
                        yield
                        S.copy("act", RHS, RHS[:], pR, pR[:, 0:128])
                        yield
                        for h in range(2):
                            Wh = Wfin[(h, ci)]
                            S.mm(pR, pR[:, 128 + 64 * h:128 + 64 * h + 64], Wh, Wh[:], RHS, RHS[:, 64 * h:64 * h + 64])
                        yield
                        S.copy("act", Us, Us[:], pR, pR[:, 128:256])
                        S.copy("dve", UpA, UpA[:, 0:64], pR, pR[:, 128:192])
                        S.copy("dve", UpB, UpB[:, 64:128], pR, pR[:, 192:256])
                        yield
                        S.mm(pYy, pYy[:, 0:128], S0, S0[:], RT, RT[:, cols], start=True, stop=False)
                        S.mm(pYy, pYy[:, 0:128], VpA, VpA[:, ch, :], M4[0][ci % 2], M4[0][ci % 2][:, 3, :], start=False, stop=False)
                        S.mm(pYy, pYy[:, 0:128], VpB, VpB[:, ch, :], M4[1][ci % 2], M4[1][ci % 2][:, 3, :], start=False, stop=False)
                        yield
                        S.mm(pYy, pYy[:, 0:128], UpA, UpA[:], M4[0][ci % 2], M4[0][ci % 2][:, 1, :], start=False, stop=False)
                        S.mm(pYy, pYy[:, 0:128], UpB, UpB[:], M4[1][ci % 2], M4[1][ci % 2][:, 1, :], start=False, stop=True)
                        yield
                        if d == 0:
                            S.copy("act", YACC, YACC[:, cols], pYy, pYy[:, 0:128])
                        else:
                            S.tt("dve", YACC, YACC[:, cols], YACC, YACC[:, cols], pYy, pYy[:, 0:128], ALU.add)
                        yield
                        S.ts("pool", bk, bk[:, 0, :], BT, BT[:, cols], GL[:, ch:ch + 1], None, ALU.mult, extra_reads=[GL])
                        S.ts("pool", bk, bk[:, 1, :], KT, KT[:, cols], GL[:, ch:ch + 1], None, ALU.mult, extra_reads=[GL])
                        yield
                        S.tr(pS, pS[:, 0:128], bk, bk[:, 0, :], cst, ident)
                        S.tr(pS, pS[:, 128:256], bk, bk[:, 1, :], cst, ident)
                        yield
                        S.copy("act", bkT, bkT[:], pS, pS[:, 0:256].rearrange("p (a b) -> p a b", a=2))
                        yield
                        S.mm(pS, pS[:, 256:384], bkT, bkT[:, 0, :], Us, Us[:], start=True, stop=False)
                        S.mm(pS, pS[:, 256:384], bkT, bkT[:, 1, :], Vp, Vp[:, ch, :], start=False, stop=True)
                        yield
                        for h in range(2):
                            hp = slice(64 * h, 64 * h + 64)
                            S.stt("dve", S1, S1[hp, 64 * h:64 * h + 64], S0, S0[hp, 64 * h:64 * h + 64], GL[hp, ch:ch + 1],
                                  pS, pS[hp, 256 + 64 * h:256 + 64 * h + 64], ALU.mult, ALU.add, extra_reads=[GL])
                        yield

                    for _ in rr(pre_head(0, 0, order[0]), pre_head(1, 0, order[0])):
                        pass
                    for ci, ch in enumerate(order):
                        gens = [seq(ci, ch)]
                        if ci + 1 < len(order):
                            gens += [pre_head(0, ci + 1, order[ci + 1]), pre_head(1, ci + 1, order[ci + 1])]
                        for _ in rr(*gens):
                            pass
                for tb in range(5):
                    c0 = tb * 512; n = min(512, T - c0)
                    S.mm(pR, pR[:, 0:n], cst2, bones, YACC, YACC[:, c0:c0 + n])
                    S.stt("dve", TMP, TMP[:, c0:c0 + n], pR, pR[:, 0:n], -1.0 / 64, YACC, YACC[:, c0:c0 + n], ALU.mult, ALU.add)
                    S.act(CIN, CIN[:, c0:c0 + n], TMP, TMP[:, c0:c0 + n], AF.Square)
                    S.mm(pYy, pYy[:, 0:n], cst2, bones, CIN, CIN[:, c0:c0 + n])
                    S.ts("dve", CIN, CIN[:, c0:c0 + n], pYy, pYy[:, 0:n], 1.0 / 64, GN_EPS, ALU.mult, ALU.add)
                    S.act(CIN, CIN[:, c0:c0 + n], CIN, CIN[:, c0:c0 + n], AF.Sqrt)
                    S.emit("dve", lambda e, c0=c0, n=n: e.reciprocal(out=CIN[:, c0:c0 + n], in_=CIN[:, c0:c0 + n]), [CIN], [CIN])
                    S.stt("dve", TMP, TMP[:, c0:c0 + n], TMP, TMP[:, c0:c0 + n], lnw[:, ec:ec + 1], CIN, CIN[:, c0:c0 + n],
                          ALU.mult, ALU.mult, extra_reads=[lnw])
                    S.stt("dve", TMP, TMP[:, c0:c0 + n], TMP, TMP[:, c0:c0 + n], lnb[:, ec:ec + 1], BON, BON[:, c0:c0 + n],
                          ALU.add, ALU.add, extra_reads=[lnb])
                    S.tt("pool", gst, gst[:, c0:c0 + n], TMP, TMP[:, c0:c0 + n], szb, szb[:, c0:c0 + n], ALU.mult)
                S.dma(gT.bufs(*rows, 0, T), gT.ap[rows[0]:rows[1], :], gst, gst[:])
            S.pop()

        def stage_l1_out(modG):
            S.push()
            ps = mkps()
            gb_ = S.sbuf("g1", [128, EC, 512], BF16)
            xb = S.sbuf("xb1", [128, KC, 512])
            sq = S.sbuf("sq1", [128, KC, 512])
            rstd = S.sbuf("rstd1", [128, 512])
            ot = [S.sbuf("ot", [128, D]) for _ in range(2)]
            gw = [S.sbuf("gw1", [128, EC, 128], BF16) for _ in range(2)]
            fng = S.sbuf("fng", [128, KC])
            S.dma(fng, fng[:], W["final_norm_g"], W["final_norm_g"].t.rearrange("(kc p) -> p kc", p=128), slow=True)
            ov = W["l1_out_w"].t.rearrange("(ei p) e -> p ei e", p=128)
            xTv = xT.ap.rearrange("(kc p) t -> p kc t", p=128)
            gTv = gT.ap.rearrange("(e p) t -> p e t", p=128)
            wi = 0
            oi = 0
            for (c0, n, r) in lat_blocks()[1:]:
                S.dma(gb_, gb_[:, :, 0:n], gT.bufs(0, E, 0, T), gTv[:, :, c0:c0 + n])
                S.dma(xb, xb[:, :, 0:n], xT.bufs(0, D, c0, c0 + n), xTv[:, :, c0:c0 + n])
                for dc in range(KC):
                    w = gw[wi % 2]; wi += 1
                    S.dma(w, w[:], W["l1_out_w"], ov[:, :, dc * 128:(dc + 1) * 128], q="pool")
                    pb = ps[dc % 4]
                    for ei in range(EC):
                        S.mm(pb, pb[:, 0:n], w, w[:, ei, :], gb_, gb_[:, ei, 0:n], start=(ei == 0), stop=(ei == EC - 1))
                    S.stt("dve", xb, xb[:, dc, 0:n], pb, pb[:, 0:n], modG[:, dc, r:r + 1], xb, xb[:, dc, 0:n],
                          ALU.mult, ALU.add, extra_reads=[modG])
                S.act(sq, sq[:, :, 0:n], xb, xb[:, :, 0:n], AF.Square)
                pb = ps[4]
                for kc in range(KC):
                    S.mm(pb, pb[:, 0:n], cst, ones, sq, sq[:, kc, 0:n], start=(kc == 0), stop=(kc == KC - 1))
                S.ts("dve", rstd, rstd[:, 0:n], pb, pb[:, 0:n], 1.0 / D, RMS_EPS, ALU.mult, ALU.add)
                S.act(rstd, rstd[:, 0:n], rstd, rstd[:, 0:n], AF.Sqrt)
                S.emit("dve", lambda e, n=n: e.reciprocal(out=rstd[:, 0:n], in_=rstd[:, 0:n]), [rstd], [rstd])
                for kc in range(KC):
                    S.stt("dve", xb, xb[:, kc, 0:n], xb, xb[:, kc, 0:n], fng[:, kc:kc + 1], rstd, rstd[:, 0:n],
                          ALU.mult, ALU.mult, extra_reads=[fng])
                for tt in range(n // 128):
                    o = ot[oi % 2]; oi += 1
                    for q in range(4):
                        pb = ps[5 + (q % 3)]
                        for j in range(4):
                            kc = 4 * q + j
                            S.tr(pb, pb[:, j * 128:(j + 1) * 128], xb, xb[:, kc, tt * 128:(tt + 1) * 128], cst, ident)
                        S.copy("act" if q % 2 == 0 else "dve", o, o[:, q * 512:(q + 1) * 512], pb, pb[:, :])
                    t0 = c0 - TC + tt * 128
                    S.dma(out_t, out_t[t0:t0 + 128, :], o, o[:])
            S.pop()

        dbg = dbg or ""
        stage_transpose_in()
        modA = S.sbuf("modA", [128, KC, 2]); modB = S.sbuf("modB", [128, KC, 2]); modG = S.sbuf("modG", [128, KC, 2])
        l1_only = dbg.startswith("l1")
        if not l1_only:
            stage_ada("l0_", modA, modB, modG)
            S.push()
            hT = S.sbuf("hT", [128, KC, T], BF16)
            stage_norm(hT, modA, modB)
            if dbg == "h0":
                ho = Buf(nc.dram_tensor("dbg_h", [D, T], BF16, kind="ExternalOutput").ap(), "dbg_h")
                S.dma(ho, ho.t.rearrange("(kc p) t -> p kc t", p=128), hT, hT[:])
                S.finish([ho])
            stage_inproj_l0(hT)
            S.pop()
        if dbg in ("s5pre", "s5") or dbg.startswith("s5pre:"):
            stage_s5(1 if dbg == "s5" else 0)
            if dbg == "s5":
                o_ = Buf(nc.dram_tensor("dbg_v", [128, T], BF16, kind="ExternalOutput").ap(), "dbg_v")
                S.push(); tb_ = S.sbuf("dbgv", [128, T], BF16)
                S.dma(tb_, tb_[:], vT.bufs(0, 128, 0, T), vT.ap[0:128, :]); S.dma(o_, o_[:, :], tb_, tb_[:]); S.pop()
                S.finish([o_])
        elif dbg != "h0" and not l1_only:
            stage_s5()
            stage_glu_out_l0(modG)
            if dbg == "l0":
                xo = Buf(nc.dram_tensor("dbg_x", [D, T], F32, kind="ExternalOutput").ap(), "dbg_x")
                S.push()
                tb_ = S.sbuf("dbgb", [128, KC, 256])
                xTv = xT.ap.rearrange("(kc p) t -> p kc t", p=128)
                for bi in range(9):
                    S.dma(tb_, tb_[:], xT.bufs(0, D, bi * 256, bi * 256 + 256), xTv[:, :, bi * 256:(bi + 1) * 256])
                    S.dma(xo, xo.t.rearrange("(kc p) t -> p kc t", p=128)[:, :, bi * 256:(bi + 1) * 256], tb_, tb_[:])
                S.pop()
                S.finish([xo])
        if dbg in ("", "l1", "l1scan"):
            stage_ada("l1_", modA, modB, modG)
            S.push()
            hT = S.sbuf("hT1", [128, KC, T], BF16)
            stage_norm(hT, modA, modB)
            stage_l1_proj(hT)
            S.pop()
            if dbg == "l1scan":
                stage_l1_scan(1)
                o_ = Buf(nc.dram_tensor("dbg_g", [128, T], BF16, kind="ExternalOutput").ap(), "dbg_g")
                S.push(); tb_ = S.sbuf("dbgg", [128, T], BF16)
                S.dma(tb_, tb_[:], gT.bufs(0, 128, 0, T), gT.ap[0:128, :]); S.dma(o_, o_[:, :], tb_, tb_[:]); S.pop()
                S.finish([o_])
            else:
                stage_l1_scan()
                stage_l1_out(modG)
                S.finish([out_t])
        print("instructions:", S.nins, S.cnt)
    return nc


def make_consts():
    c = np.zeros((128, C_W), np.float32)
    s = np.arange(128)
    c[:, C_IDENT:C_IDENT + 128] = np.eye(128)
    c[:, C_ONES:C_ONES + 128] = 1.0
    c[:, C_TRIF:C_TRIF + 128] = (s[:, None] <= s[None, :])
    c[:, C_TRIB:C_TRIB + 128] = (s[:, None] >= s[None, :])
    c[127, C_SELF:C_SELF + 128] = 1.0
    c[0, C_SELB:C_SELB + 128] = 1.0
    c[:, C_S1 + 0] = s + 1
    c[:, C_S1 + 1] = -(s + 1)
    c[:, C_S1 + 2] = 128 - s
    c[:, C_S1 + 3] = -(128 - s)
    c[:, C_MBD:C_MBD + 8] = ((s[:, None] // 16) == np.arange(8)[None, :])
    return c


WEIGHT_NAMES = ["l0_norm_g", "l0_ada_w", "l0_ada_b", "l0_in_w", "l0_a_re", "l0_a_im", "l0_log_step", "l0_b_re",
                "l0_b_im", "l0_c_re", "l0_c_im", "l0_d", "l0_glu_w", "l0_glu_b", "l0_out_w", "l1_norm_g", "l1_ada_w",
                "l1_ada_b", "l1_mu", "l1_in_w", "l1_w0", "l1_w1", "l1_w2", "l1_a0", "l1_a1", "l1_a2", "l1_k_k",
                "l1_k_a", "l1_r_k", "l1_ln_w", "l1_ln_b", "l1_out_w", "final_norm_g"]


def make_consts2():
    c = np.zeros((128, C2_W), np.float32)
    s = np.arange(128)
    lt = (s[:, None] < s[None, :]).astype(np.float32)
    le = (s[:, None] <= s[None, :]).astype(np.float32)
    gt = (s[:, None] > s[None, :]).astype(np.float32)
    ge = (s[:, None] >= s[None, :]).astype(np.float32)
    c[:, C2_M4:C2_M4 + 512] = np.concatenate([lt, le, lt, le], 1)
    c[:, C2_M4 + 512:C2_M4 + 1024] = np.concatenate([gt, ge, gt, ge], 1)
    c[:, C2_MN:C2_MN + 128] = gt
    c[:, C2_MN + 128:C2_MN + 256] = lt
    c[:, C2_BONES:C2_BONES + 128] = ((s[:, None] // 64) == (s[None, :] // 64))
    c[:, C2_RESET:C2_RESET + 128] = (s[None, :] != 0)
    return c


def make_in_maps(inputs, cores):
    cst = make_consts()
    cst2 = make_consts2()
    maps = []
    for b in cores:
        m = {"x": np.ascontiguousarray(inputs["x"][b]), "ctx": np.ascontiguousarray(inputs["ctx"][b]),
             "cond": np.ascontiguousarray(np.stack([inputs["c"][b], inputs["c_ctx"]], 0)), "cst": cst, "cst2": cst2}
        for nm in WEIGHT_NAMES:
            m[nm] = np.ascontiguousarray(inputs[nm], dtype=np.float32)
        maps.append(m)
    return maps


def kernel(**inputs):
    nc = build_program()
    maps = make_in_maps(inputs, list(range(8)))
    res = run_bass_kernel_spmd(nc, maps, core_ids=list(range(8)))
    return np.stack([r["out"] for r in res.results], 0).astype(np.float32)
```

```python
import contextlib, math
import numpy as np
import concourse.bass as bass
import concourse.mybir as mybir
from concourse.bass_utils import run_bass_kernel_spmd

F32 = mybir.dt.float32
BF16 = mybir.dt.bfloat16
ALU = mybir.AluOpType
AF = mybir.ActivationFunctionType
AX = mybir.AxisListType

D = 2048
E = 4096
T = 2304
TC = 256
KC = 16
EC = 32
NCH = 18
PI = math.pi
RMS_EPS = 1e-6
GN_EPS = 64e-5

C_IDENT, C_ONES, C_TRIF, C_TRIB, C_SELF, C_SELB = 0, 128, 256, 384, 512, 640
C_S1 = 768
C_MBD = 772
C_W = 780
C2_M4, C2_MN, C2_BONES, C2_RESET = 0, 1024, 1280, 1408
C2_W = 1536


class Ev:
    __slots__ = ("sem", "val", "eng", "key")

    def __init__(self, sem, val, eng, key):
        self.sem, self.val, self.eng, self.key = sem, val, eng, key


class Buf:
    __slots__ = ("t", "w", "r", "name", "excl")

    def __init__(self, t=None, name=""):
        self.t, self.w, self.r, self.name, self.excl = t, None, {}, name, False

    def __getitem__(self, idx):
        return self.t[idx]


class Sched:
    SAME_ENGINE_SYNC = True
    NDMASEM = 24

    def __init__(self, nc, es):
        self.nc = nc
        self.stack = [es]
        self.engs = {"pe": nc.tensor, "act": nc.scalar, "dve": nc.vector, "pool": nc.gpsimd, "sp": nc.sync}
        self.sem = {k: es.enter_context(nc.semaphore("s_" + k)) for k in ("pe", "act", "dve", "pool")}
        self.cnt = {k: 0 for k in self.sem}
        self.dsem = [es.enter_context(nc.semaphore("d%d" % i)) for i in range(2 * self.NDMASEM)]
        self.duse = [0] * (2 * self.NDMASEM)
        self.dnext = {"sp": 0, "pool": 0, "act": 0}
        self.seen = {k: {} for k in self.engs}
        self.nins = 0
        self.uid = 0
        self.scope_bufs = [[]]
        self.pending_free = {}

    def push(self):
        es = contextlib.ExitStack()
        es.__enter__()
        self.stack.append(es)
        self.scope_bufs.append([])

    def pop(self):
        for b in self.scope_bufs.pop():
            evs = list(b.r.values()) + ([b.w] if b.w is not None else [])
            for ev in evs:
                old = self.pending_free.get(ev.key)
                if old is None or old.val < ev.val:
                    self.pending_free[ev.key] = ev
        es = self.stack.pop()
        es.__exit__(None, None, None)

    def _newbuf(self, t, nm):
        b = Buf(t, nm)
        b.r = dict(self.pending_free)
        self.scope_bufs[-1].append(b)
        return b

    def sbuf(self, name, shape, dt=F32):
        self.uid += 1
        nm = "%s_%d" % (name, self.uid)
        return self._newbuf(self.stack[-1].enter_context(self.nc.sbuf_tensor(nm, list(shape), dt)), nm)

    def psum(self, name, shape, dt=F32):
        b = self._newbuf(self.stack[-1].enter_context(self.nc.psum_tensor(name, list(shape), dt)), name)
        b.excl = True
        return b

    def dram(self, name, shape, dt=F32, kind="Internal"):
        return self.nc.dram_tensor(name, list(shape), dt, kind=kind).ap()

    def _wait(self, engname, ev):
        seen = self.seen[engname]
        if seen.get(ev.key, 0) >= ev.val:
            return
        self.engs[engname].wait_ge(ev.sem, ev.val)
        seen[ev.key] = ev.val

    def emit(self, engname, fn, reads=(), writes=(), dma=False):
        waits = []
        for b in reads:
            if b.w is not None:
                waits.append(b.w)
            if b.excl:
                waits.extend(ev for ev in b.r.values() if ev.eng != engname)
        for b in writes:
            if b.w is not None:
                waits.append(b.w)
            waits.extend(b.r.values())
        for ev in waits:
            if ev.eng == engname and not dma:
                if engname == "pe" or not self.SAME_ENGINE_SYNC:
                    continue
            self._wait(engname, ev)
        eng = self.engs[engname]
        if dma:
            i = self.dnext[engname] + (self.NDMASEM if engname == "pool" else 0)
            self.dnext[engname] = (self.dnext[engname] + 1) % self.NDMASEM
            s = self.dsem[i]
            if self.duse[i] > 0:
                self._wait(engname, Ev(s, 16 * self.duse[i], "dma", ("d", i)))
            ins = fn(eng)
            self.duse[i] += 1
            ins.then_inc(s, 16)
            ev = Ev(s, 16 * self.duse[i], "dma", ("d", i))
        else:
            ins = fn(eng)
            self.cnt[engname] += 1
            ins.then_inc(self.sem[engname], 1)
            ev = Ev(self.sem[engname], self.cnt[engname], engname, engname)
        for b in reads:
            b.r[ev.key] = ev
        for b in writes:
            b.w = ev
            b.r = {}
        self.nins += 1
        return ev

    def dma(self, out_b, out_ap, in_b, in_ap, q="sp", slow=False):
        outs = out_b if isinstance(out_b, (list, tuple)) else [out_b]
        ins = in_b if isinstance(in_b, (list, tuple)) else [in_b]
        if slow:
            f = lambda e: e.dma_start(out=out_ap, in_=in_ap, allow_slow_non_contiguous=True)
        else:
            f = lambda e: e.dma_start(out=out_ap, in_=in_ap)
        return self.emit(q, f, reads=ins, writes=outs, dma=True)

    def finish(self, bufs):
        for b in bufs:
            if b.w is not None:
                self._wait("sp", b.w)

    def act(self, ob, o, ib, i, func, bias=None, scale=None, eng="act", extra_reads=()):
        kw = {}
        if bias is not None:
            kw["bias"] = bias
        if scale is not None:
            kw["scale"] = scale
        return self.emit(eng, lambda e: e.activation(out=o, in_=i, func=func, **kw), [ib] + list(extra_reads), [ob])

    def tt(self, eng, ob, o, ab, a, bb, b, op):
        return self.emit(eng, lambda e: e.tensor_tensor(out=o, in0=a, in1=b, op=op), [ab, bb], [ob])

    def ts(self, eng, ob, o, ab, a, s1, s2, op0, op1=None, extra_reads=()):
        if op1 is None:
            f = lambda e: e.tensor_scalar(out=o, in0=a, scalar1=s1, scalar2=None, op0=op0)
        else:
            f = lambda e: e.tensor_scalar(out=o, in0=a, scalar1=s1, scalar2=s2, op0=op0, op1=op1)
        return self.emit(eng, f, [ab] + list(extra_reads), [ob])

    def stt(self, eng, ob, o, ab, a, sc, bb, b, op0, op1, extra_reads=()):
        return self.emit(eng, lambda e: e.scalar_tensor_tensor(out=o, in0=a, scalar=sc, in1=b, op0=op0, op1=op1),
                         [ab, bb] + list(extra_reads), [ob])

    def copy(self, eng, ob, o, ib, i):
        if eng == "act":
            return self.emit(eng, lambda e: e.activation(out=o, in_=i, func=AF.Copy), [ib], [ob])
        return self.emit(eng, lambda e: e.tensor_copy(out=o, in_=i), [ib], [ob])

    def mm(self, ob, o, lb, lhsT, rb, rhs, start=True, stop=True):
        return self.emit("pe", lambda e: e.matmul(out=o, lhsT=lhsT, rhs=rhs, start=start, stop=stop), [lb, rb], [ob])

    def tr(self, ob, o, ib, i, identb, ident):
        return self.emit("pe", lambda e: e.transpose(out=o, in_=i, identity=ident), [ib, identb], [ob])


class DramT:
    def __init__(self, S, name, shape, dt, ru, cu, kind="Internal"):
        self.ap = S.dram(name, shape, dt, kind=kind)
        self.ru, self.cu = ru, cu
        self.nr, self.ncu = (shape[0] + ru - 1) // ru, (shape[1] + cu - 1) // cu
        self.units = [[Buf(None, "%s_%d_%d" % (name, i, j)) for j in range(self.ncu)] for i in range(self.nr)]

    def bufs(self, r0, r1, c0, c1):
        out = []
        for i in range(r0 // self.ru, (r1 - 1) // self.ru + 1):
            for j in range(c0 // self.cu, (c1 - 1) // self.cu + 1):
                out.append(self.units[i][j])
        return out


def lat_blocks():
    return [(0, 256, 1)] + [(256 + 512 * i, 512, 0) for i in range(4)]


def build_program(dbg=None):
    dbg = dbg or ""
    nc = bass.Bass("TRN2", target_bir_lowering=False)
    es = contextlib.ExitStack()
    with es:
        S = Sched(nc, es)
        inp = {}

        def ext(name, shape):
            b = Buf(nc.dram_tensor(name, list(shape), F32, kind="ExternalInput").ap(), name)
            inp[name] = b
            return b

        x_in = ext("x", [2048, D])
        ctx_in = ext("ctx", [TC, D])
        cond_in = ext("cond", [2, D])
        cst_in = ext("cst", [128, C_W])
        cst2_in = ext("cst2", [128, C2_W])
        W = {}
        for nm, shp in [
            ("l0_norm_g", [D]), ("l0_ada_w", [D, 3 * D]), ("l0_ada_b", [3 * D]), ("l0_in_w", [D, 2 * E]),
            ("l0_a_re", [2, 256, 64]), ("l0_a_im", [2, 256, 64]), ("l0_log_step", [2, 256]),
            ("l0_b_re", [2, 256, 64, 16]), ("l0_b_im", [2, 256, 64, 16]),
            ("l0_c_re", [2, 256, 16, 64]), ("l0_c_im", [2, 256, 16, 64]),
            ("l0_d", [E]), ("l0_glu_w", [E, E]), ("l0_glu_b", [E]), ("l0_out_w", [E, D]),
            ("l1_norm_g", [D]), ("l1_ada_w", [D, 3 * D]), ("l1_ada_b", [3 * D]), ("l1_mu", [6, D]),
            ("l1_in_w", [4, D, E]), ("l1_w0", [2, E]), ("l1_w1", [2, D, 96]), ("l1_w2", [2, 96, E]),
            ("l1_a0", [2, E]), ("l1_a1", [2, D, 96]), ("l1_a2", [2, 96, E]), ("l1_k_k", [E]), ("l1_k_a", [E]),
            ("l1_r_k", [64, 64]), ("l1_ln_w", [E]), ("l1_ln_b", [E]), ("l1_out_w", [E, D]), ("final_norm_g", [D]),
        ]:
            W[nm] = ext(nm, shp)
        out_t = Buf(nc.dram_tensor("out", [2048, D], F32, kind="ExternalOutput").ap(), "out")

        xT = DramT(S, "xT", [D, T], F32, D, 256)
        uT = DramT(S, "uT", [E, T], F32, 128, T)
        szT = DramT(S, "szT", [E, T], BF16, 128, T)
        vT = DramT(S, "vT", [E, T], BF16, 128, T)
        s5c = DramT(S, "s5c", [4 * 512, 64], F32, 512, 64)

        cst = S.sbuf("cst", [128, C_W])
        S.dma(cst, cst[:], cst_in, cst_in[:, :])
        ident = cst[:, C_IDENT:C_IDENT + 128]
        ones = cst[:, C_ONES:C_ONES + 128]
        def mkps(n=8):
            return [S.psum("ps%d_%d" % (i, S.uid), [128, 512]) for i in range(n)]

        dbg_out = {}

        def stage_transpose_in():
            S.push()
            ps = mkps()
            xin = [S.sbuf("xin", [128, D]) for _ in range(2)]
            xtt = [S.sbuf("xtt", [128, KC, 128]) for _ in range(2)]
            xTv = xT.ap.rearrange("(kc p) t -> p kc t", p=128)
            for i in range(NCH):
                src_b, src = (ctx_in, ctx_in[i * 128:(i + 1) * 128, :]) if i < 2 else (x_in, x_in[(i - 2) * 128:(i - 1) * 128, :])
                xi, xo = xin[i % 2], xtt[i % 2]
                S.dma(xi, xi[:], src_b, src)
                for q in range(4):
                    pb = ps[(i * 4 + q) % 8]
                    for j in range(4):
                        kc = 4 * q + j
                        S.tr(pb, pb[:, j * 128:(j + 1) * 128], xi, xi[:, kc * 128:(kc + 1) * 128], cst, ident)
                    S.copy("act" if q % 2 == 0 else "dve", xo, xo[:, 4 * q:4 * q + 4, :],
                           pb, pb[:, :].rearrange("p (a b) -> p a b", a=4))
                S.dma(xT.bufs(0, D, i * 128, (i + 1) * 128), xTv[:, :, i * 128:(i + 1) * 128], xo, xo[:])
            S.pop()

        def stage_ada(pre, modA, modB, modG):
            S.push()
            ps = mkps()
            condT = S.sbuf("condT", [128, 2, KC])
            for r in range(2):
                S.dma(condT, condT[:, r, :], cond_in, cond_in.t[r, :].rearrange("(kc p) -> p kc", p=128), slow=True)
            S.act(condT, condT[:], condT, condT[:], AF.Silu)
            adab = S.sbuf("adab", [128, 48])
            S.dma(adab, adab[:], W[pre + "ada_b"], W[pre + "ada_b"].t.rearrange("(jc p) -> p jc", p=128), slow=True)
            normg = S.sbuf("normg", [128, KC])
            S.dma(normg, normg[:], W[pre + "norm_g"], W[pre + "norm_g"].t.rearrange("(kc p) -> p kc", p=128), slow=True)
            wb = [S.sbuf("adaw", [128, KC, 512]) for _ in range(2)]
            wv = W[pre + "ada_w"].t.rearrange("(kc p) j -> p kc j", p=128)
            pA = ps[7]
            for grp in range(12):
                w = wb[grp % 2]
                S.dma(w, w[:], W[pre + "ada_w"], wv[:, :, grp * 512:(grp + 1) * 512])
                for jj in range(4):
                    jc = grp * 4 + jj
                    for kc in range(KC):
                        S.mm(pA, pA[:, jc * 2:jc * 2 + 2], w, w[:, kc, jj * 128:(jj + 1) * 128], condT, condT[:, :, kc],
                             start=(kc == 0), stop=(kc == KC - 1))
            mod = S.sbuf("mod", [128, 48, 2])
            S.tt("dve", mod, mod[:], pA, pA[:, 0:96].rearrange("p (j r) -> p j r", r=2),
                 adab, adab[:, :].unsqueeze(2).to_broadcast([128, 48, 2]), ALU.add)
            S.copy("dve", modB, modB[:], mod, mod[:, 0:16, :])
            S.copy("dve", modG, modG[:], mod, mod[:, 32:48, :])
            S.stt("dve", modA, modA[:], mod, mod[:, 16:32, :], 1.0, normg,
                  normg[:, :].unsqueeze(2).to_broadcast([128, KC, 2]), ALU.add, ALU.mult)
            S.pop()

        def stage_norm(hT, modA, modB):
            S.push()
            ps = mkps()
            xb = [S.sbuf("xb", [128, KC, 256]) for _ in range(2)]
            sq = S.sbuf("sq", [128, KC, 256])
            tmp = S.sbuf("ntmp", [128, KC, 256])
            rstd = S.sbuf("rstd", [128, 256])
            xTv = xT.ap.rearrange("(kc p) t -> p kc t", p=128)
            for bi in range(T // 256):
                c0 = bi * 256
                r = 1 if bi == 0 else 0
                x = xb[bi % 2]
                S.dma(x, x[:], xT.bufs(0, D, c0, c0 + 256), xTv[:, :, c0:c0 + 256])
                S.act(sq, sq[:], x, x[:], AF.Square)
                pb = ps[bi % 2]
                for kc in range(KC):
                    S.mm(pb, pb[:, 0:256], cst, ones, sq, sq[:, kc, :], start=(kc == 0), stop=(kc == KC - 1))
                S.ts("dve", rstd, rstd[:], pb, pb[:, 0:256], 1.0 / D, RMS_EPS, ALU.mult, ALU.add)
                S.act(rstd, rstd[:], rstd, rstd[:], AF.Sqrt)
                S.emit("dve", lambda e: e.reciprocal(out=rstd[:], in_=rstd[:]), [rstd], [rstd])
                for kc in range(KC):
                    S.stt("dve", tmp, tmp[:, kc, :], x, x[:, kc, :], modA[:, kc, r:r + 1], rstd, rstd[:],
                          ALU.mult, ALU.mult, extra_reads=[modA])
                    S.act(hT, hT[:, kc, c0:c0 + 256], tmp, tmp[:, kc, :], AF.Identity, bias=modB[:, kc, r:r + 1],
                          extra_reads=[modB])
            S.pop()

        def proj_chunk(hT, wt, pbanks, evac):
            for tb in range(5):
                c0 = tb * 512
                n = min(512, T - c0)
                pb = pbanks[tb % len(pbanks)]
                for kc in range(KC):
                    S.mm(pb, pb[:, 0:n], wt, wt[:, kc, :], hT, hT[:, kc, c0:c0 + n], start=(kc == 0), stop=(kc == KC - 1))
                evac(tb, c0, n, pb)

        def stage_inproj_l0(hT):
            S.push()
            ps = mkps()
            wb = [S.sbuf("inw", [128, KC, 128], BF16) for _ in range(2)]
            ust = [S.sbuf("ust", [128, T]) for _ in range(2)]
            zst = [S.sbuf("zst", [128, T], BF16) for _ in range(2)]
            wv = W["l0_in_w"].t.rearrange("(kc p) e -> p kc e", p=128)
            for ec in range(2 * EC):
                w = wb[ec % 2]
                S.dma(w, w[:], W["l0_in_w"], wv[:, :, ec * 128:(ec + 1) * 128], q="pool")
                if ec < EC:
                    st = ust[ec % 2]
                    proj_chunk(hT, w, ps[0:4], lambda tb, c0, n, pb: S.copy("act" if tb % 2 else "dve", st, st[:, c0:c0 + n], pb, pb[:, 0:n]))
                    S.dma(uT.bufs(ec * 128, ec * 128 + 128, 0, T), uT.ap[ec * 128:(ec + 1) * 128, :], st, st[:])
                else:
                    st = zst[ec % 2]
                    proj_chunk(hT, w, ps[0:4], lambda tb, c0, n, pb: S.act(st, st[:, c0:c0 + n], pb, pb[:, 0:n], AF.Silu))
                    r0 = (ec - EC) * 128
                    S.dma(szT.bufs(r0, r0 + 128, 0, T), szT.ap[r0:r0 + 128, :], st, st[:])
            S.pop()

        MAGIC = 12582912.0

        def sin_arg(eng, ob, o, ib, i, mul, add, tb, t):
            if isinstance(mul, float):
                S.ts(eng, ob, o, ib, i, mul, add, ALU.mult, ALU.add)
            else:
                S.ts(eng, ob, o, ib, i, mul, None, ALU.mult, extra_reads=[cst])
                if add != 0.0:
                    S.ts(eng, ob, o, ob, o, add, None, ALU.add)
            S.ts(eng, tb, t, ob, o, 1.0 / (2 * PI), MAGIC, ALU.mult, ALU.add)
            S.ts(eng, tb, t, tb, t, -MAGIC, None, ALU.add)
            S.stt(eng, ob, o, tb, t, -2 * PI, ob, o, ALU.mult, ALU.add)

        SIN_SC = 0.999999

        def rr(*gens):
            gens = list(gens)
            while gens:
                for g in list(gens):
                    try:
                        next(g)
                        yield
                    except StopIteration:
                        gens.remove(g)

        def stage_s5(ngb=EC):
            S.push()
            pBu = S.psum("pBu", [128, 1024]); pG = S.psum("pG", [128, 1024]); pT = S.psum("pT", [128, 1024])
            pG2 = S.psum("pG2", [128, 1024])
            pY = pG2
            pX = pG
            Bbar = [S.sbuf("Bbar%d" % c, [128, 64, 64]) for c in range(2)]
            Cn = S.sbuf("Cn", [128, 64, 128], BF16)
            dsk = S.sbuf("dsk", [128, EC])
            S.dma(dsk, dsk[:], W["l0_d"], W["l0_d"].t.rearrange("(gb p) -> p gb", p=128), slow=True)
            tri = [S.sbuf("tri%d" % d, [128, 128], BF16) for d in range(2)]
            S.copy("dve", tri[0], tri[0][:], cst, cst[:, C_TRIF:C_TRIF + 128])
            S.copy("dve", tri[1], tri[1][:], cst, cst[:, C_TRIB:C_TRIB + 128])
            sel = [cst[:, C_SELF:C_SELF + 128], cst[:, C_SELB:C_SELB + 128]]
            stop_k = int(dbg.split(":")[1]) if (dbg and ":" in dbg) else 99

            def ckpt(k):
                if k != stop_k:
                    return False
                o_ = Buf(nc.dram_tensor("dbg_ck", [128, EC], F32, kind="ExternalOutput").ap(), "dbg_ck")
                S.dma(o_, o_[:, :], dsk, dsk[:]); S.finish([o_])
                return True
            if ckpt(1):
                S.pop(); return
            S.push()
            cv = lambda ap3: ap3.rearrange("d g n -> (d g) n").rearrange("(q p) n -> p q n", p=128)
            are = S.sbuf("are", [128, 4, 64]); aim = S.sbuf("aim", [128, 4, 64])
            S.dma(are, are[:], W["l0_a_re"], cv(W["l0_a_re"].t))
            S.dma(aim, aim[:], W["l0_a_im"], cv(W["l0_a_im"].t))
            ls = S.sbuf("ls", [128, 4])
            S.dma(ls, ls[:], W["l0_log_step"], W["l0_log_step"].t.rearrange("d g -> (d g)").rearrange("(q p) -> p q", p=128), slow=True)
            S.act(ls, ls[:], ls, ls[:], AF.Exp)
            dtb = ls[:, :].unsqueeze(2).to_broadcast([128, 4, 64])
            dtar = S.sbuf("dtar", [128, 4, 64]); dtai = S.sbuf("dtai", [128, 4, 64])
            S.tt("dve", dtar, dtar[:], are, are[:], ls, dtb, ALU.mult)
            S.tt("dve", dtai, dtai[:], aim, aim[:], ls, dtb, ALU.mult)
            mag = S.sbuf("mag", [128, 4, 64]); sn = S.sbuf("sn", [128, 4, 64]); cs = S.sbuf("cs", [128, 4, 64])
            S.act(mag, mag[:], dtar, dtar[:], AF.Exp)
            rt = S.sbuf("rt", [128, 4, 64])
            sin_arg("dve", sn, sn[:], dtai, dtai[:], 1.0, 0.0, rt, rt[:])
            S.act(sn, sn[:], sn, sn[:], AF.Sin, scale=SIN_SC)
            sin_arg("dve", cs, cs[:], dtai, dtai[:], 1.0, 0.5 * PI, rt, rt[:])
            S.act(cs, cs[:], cs, cs[:], AF.Sin, scale=SIN_SC)
            nr = S.sbuf("nr", [128, 4, 64]); ni = S.sbuf("ni", [128, 4, 64])
            S.tt("dve", nr, nr[:], mag, mag[:], cs, cs[:], ALU.mult)
            S.ts("dve", nr, nr[:], nr, nr[:], -1.0, None, ALU.add)
            S.tt("dve", ni, ni[:], mag, mag[:], sn, sn[:], ALU.mult)
            den = S.sbuf("den", [128, 4, 64]); t0 = S.sbuf("t0", [128, 4, 64])
            S.tt("dve", den, den[:], are, are[:], are, are[:], ALU.mult)
            S.tt("dve", t0, t0[:], aim, aim[:], aim, aim[:], ALU.mult)
            S.tt("dve", den, den[:], den, den[:], t0, t0[:], ALU.add)
            S.emit("dve", lambda e: e.reciprocal(out=den[:], in_=den[:]), [den], [den])
            fre = S.sbuf("fre", [128, 4, 64]); fim = S.sbuf("fim", [128, 4, 64])
            S.tt("dve", fre, fre[:], nr, nr[:], are, are[:], ALU.mult)
            S.tt("dve", t0, t0[:], ni, ni[:], aim, aim[:], ALU.mult)
            S.tt("dve", fre, fre[:], fre, fre[:], t0, t0[:], ALU.add)
            S.tt("dve", fre, fre[:], fre, fre[:], den, den[:], ALU.mult)
            S.tt("dve", fim, fim[:], ni, ni[:], are, are[:], ALU.mult)
            S.tt("dve", t0, t0[:], nr, nr[:], aim, aim[:], ALU.mult)
            S.tt("dve", fim, fim[:], fim, fim[:], t0, t0[:], ALU.subtract)
            S.tt("dve", fim, fim[:], fim, fim[:], den, den[:], ALU.mult)
            if ckpt(2):
                S.pop(); S.pop(); return
            for k, tsrc in enumerate((dtar, dtai, fre, fim)):
                S.dma(s5c.bufs(k * 512, k * 512 + 512, 0, 64),
                      s5c.ap[k * 512:(k + 1) * 512, :].rearrange("(q p) n -> p q n", p=128), tsrc, tsrc[:])
            if ckpt(3):
                S.pop(); S.pop(); return
            S.pop()
            S.push()
            Bn = S.sbuf("Bn", [128, 512, 16])
            for q in range(16):
                S.dma(Bn, Bn[0:64, q * 32:(q + 1) * 32, :], W["l0_b_re"], W["l0_b_re"].t.rearrange("d g n j -> n (d g) j")[:, q * 32:(q + 1) * 32, :])
                S.dma(Bn, Bn[64:128, q * 32:(q + 1) * 32, :], W["l0_b_im"], W["l0_b_im"].t.rearrange("d g n j -> n (d g) j")[:, q * 32:(q + 1) * 32, :])
            Bre = S.sbuf("Bre", [128, 64, 64]); Bim = S.sbuf("Bim", [128, 64, 64])
            Fre = S.sbuf("Fre", [128, 64, 64]); Fim = S.sbuf("Fim", [128, 64, 64])
            for gp in range(8):
                for dst, k in ((Fre, 2), (Fim, 3)):
                    S.dma(dst, dst[gp * 16:(gp + 1) * 16, :, :], s5c.bufs(k * 512, k * 512 + 512, 0, 64),
                          bass.AP(s5c.ap.tensor, s5c.ap.offset + (k * 512 + gp) * 64, [[0, 16], [512, 64], [1, 64]]))
            if stop_k == 4:
                S.finish([Bn, Fre, Fim])
            if stop_k == 41:
                S.finish([Bn])
            if stop_k == 42:
                S.finish([Fre, Fim])
            if ckpt(4) or ckpt(41) or ckpt(42):
                S.pop(); S.pop(); return
            for q in range(16):
                pb = pY if q % 2 else pX
                for j in range(4):
                    dgb = q * 4 + j
                    S.tr(pb, pb[:, j * 128:(j + 1) * 128], Bn, Bn[:, dgb * 8:(dgb + 1) * 8, :].rearrange('p a b -> p (a b)'), cst, ident)
                pv = pb[:, 0:512].rearrange("p (a c n) -> p a c n", a=4, c=2)
                S.copy("act", Bre, Bre[:, q * 4:q * 4 + 4, :], pb, pv[:, :, 0, :])
                S.copy("dve", Bim, Bim[:, q * 4:q * 4 + 4, :], pb, pv[:, :, 1, :])
            if stop_k == 45:
                S.finish([Bre, Bim])
            if ckpt(45):
                S.pop(); S.pop(); return
            tA = S.sbuf("tA", [128, 64, 64])
            S.tt("dve", Bbar[0], Bbar[0][:], Fre, Fre[:], Bre, Bre[:], ALU.mult)
            S.tt("pool", tA, tA[:], Fim, Fim[:], Bim, Bim[:], ALU.mult)
            S.tt("dve", Bbar[0], Bbar[0][:], Bbar[0], Bbar[0][:], tA, tA[:], ALU.subtract)
            S.tt("dve", Bbar[1], Bbar[1][:], Fre, Fre[:], Bim, Bim[:], ALU.mult)
            S.tt("pool", tA, tA[:], Fim, Fim[:], Bre, Bre[:], ALU.mult)
            S.tt("dve", Bbar[1], Bbar[1][:], Bbar[1], Bbar[1][:], tA, tA[:], ALU.add)
            if ckpt(5):
                S.pop(); S.pop(); return
            S.pop()
            S.push()
            Cnat = S.sbuf("Cnat", [128, 64, 2, 64])
            for q in range(4):
                S.dma(Cnat, Cnat[:, q * 16:(q + 1) * 16, 0, :], W["l0_c_re"], W["l0_c_re"].t.rearrange("d (gb gp) i n -> (gp i) (d gb) n", gp=8)[:, q * 16:(q + 1) * 16, :])
                S.dma(Cnat, Cnat[:, q * 16:(q + 1) * 16, 1, :], W["l0_c_im"], W["l0_c_im"].t.rearrange("d (gb gp) i n -> (gp i) (d gb) n", gp=8)[:, q * 16:(q + 1) * 16, :])
            for q in range(16):
                pb = pY if q % 2 else pX
                for j in range(4):
                    dgb = q * 4 + j
                    S.tr(pb, pb[:, j * 128:(j + 1) * 128], Cnat, Cnat[:, dgb, :, :].rearrange('p a b -> p (a b)'), cst, ident)
                pv = pb[:, 0:512].rearrange("p (a m) -> p a m", a=4)
                S.copy("act", Cn, Cn[0:64, q * 4:q * 4 + 4, :], pb, pv[0:64, :, :])
                S.ts("dve", Cn, Cn[64:128, q * 4:q * 4 + 4, :], pb, pv[64:128, :, :], -1.0, None, ALU.mult)
            S.pop()

            u32 = [S.sbuf("u32", [128, T]) for _ in range(2)]
            ubf = S.sbuf("ubf", [128, T], BF16)
            yacc = S.sbuf("yacc", [128, T])
            vst = [S.sbuf("vst", [128, T], BF16) for _ in range(2)]
            ex = S.sbuf("ex", [128, 512]); ex2 = S.sbuf("ex2", [128, 512])
            v3 = lambda t: t[:, :].rearrange("p (g n) -> p g n", g=8)
            v4 = lambda ap2: ap2.rearrange("p (g c n) -> p g c n", g=8, c=2)
            flat = lambda ap4: ap4.rearrange("p g c n -> p (g c n)")
            DB = []
            for d in range(2):
                B_ = {}
                for nm_ in ("drebc", "dimbc", "magm", "magp", "snT", "csT", "Lmr", "Lmi", "Lpr", "Lpi"):
                    B_[nm_] = S.sbuf(nm_, [128, 512])
                B_["BBD"] = S.sbuf("BBD", [128, 8, 2, 64], BF16)
                B_["CBD"] = S.sbuf("CBD", [128, 8, 128], BF16)
                S.emit("dve", lambda e, c_=B_["CBD"]: e.memset(c_[:], 0.0), [], [B_["CBD"]])
                B_["tt"] = [S.sbuf("s5t%d" % k, [128, 8, 64]) for k in range(4)]
                B_["X"] = S.sbuf("X", [128, 8, 2, 64], BF16)
                B_["Hc"] = [S.sbuf("Hc", [128, 8, 2, 64]) for _ in range(2)]
                B_["HTc"] = [S.sbuf("HTc", [128, 8, 128], BF16) for _ in range(2)]
                B_["pBu"] = pBu if d == 0 else pT
                B_["pG"] = pG if d == 0 else pG2
                DB.append(B_)

            def s5_dir(gb, d, u):
                B_ = DB[d]
                drebc, dimbc, magm, magp, snT, csT = (B_[k] for k in ("drebc", "dimbc", "magm", "magp", "snT", "csT"))
                Lmr, Lmi, Lpr, Lpi, BBD, CBD, tt_, X, Hc, HTc = (B_[k] for k in ("Lmr", "Lmi", "Lpr", "Lpi", "BBD", "CBD", "tt", "X", "Hc", "HTc"))
                pBu_, pG_ = B_["pBu"], B_["pG"]
                Bu4 = v4(pBu_[:, :]); G4 = v4(pG_[:, :])
                s1 = cst[:, C_S1 + 2 * d:C_S1 + 2 * d + 1]
                ns1 = cst[:, C_S1 + 2 * d + 1:C_S1 + 2 * d + 2]
                row0 = d * 256 + gb * 8
                for dst, k in ((drebc, 0), (dimbc, 1)):
                    S.dma(dst, dst[:], s5c.bufs(k * 512, k * 512 + 512, 0, 64),
                          bass.AP(s5c.ap.tensor, s5c.ap.offset + (k * 512 + row0) * 64, [[0, 128], [1, 512]]))
                S.act(magm, magm[:], drebc, drebc[:], AF.Exp, scale=ns1, extra_reads=[cst])
                S.act(magp, magp[:], drebc, drebc[:], AF.Exp, scale=s1, extra_reads=[cst])
                yield
                sin_arg("dve", snT, snT[:], dimbc, dimbc[:], s1, 0.0, Lmr, Lmr[:])
                S.act(snT, snT[:], snT, snT[:], AF.Sin, scale=SIN_SC)
                yield
                sin_arg("dve", csT, csT[:], dimbc, dimbc[:], s1, 0.5 * PI, Lmr, Lmr[:])
                S.act(csT, csT[:], csT, csT[:], AF.Sin, scale=SIN_SC)
                yield
                S.tt("dve", Lmr, Lmr[:], magm, magm[:], csT, csT[:], ALU.mult)
                S.stt("dve", Lmi, Lmi[:], magm, magm[:], -1.0, snT, snT[:], ALU.mult, ALU.mult)
                S.tt("pool", Lpr, Lpr[:], magp, magp[:], csT, csT[:], ALU.mult)
                S.tt("pool", Lpi, Lpi[:], magp, magp[:], snT, snT[:], ALU.mult)
                yield
                for c in range(2):
                    for g2 in range(8):
                        S.ts("dve" if g2 % 2 else "pool", BBD, BBD[:, g2, c, :], Bbar[c], Bbar[c][:, d * 32 + gb, :],
                             cst[:, C_MBD + g2:C_MBD + g2 + 1], None, ALU.mult, extra_reads=[cst])
                    yield
                cfull = CBD[:]
                S.copy("pool", CBD, bass.AP(cfull.tensor, cfull.offset, [[cfull.ap[0][0], 128], [144, 8], [1, 16]]),
                       Cn, Cn[:, d * 32 + gb, :].rearrange("p (g i) -> p g i", g=8))
                order = list(range(NCH)) if d == 0 else [1, 0] + list(range(NCH - 1, 1, -1))
                for ci, ch in enumerate(order):
                    c0 = ch * 128
                    for h in range(2):
                        S.mm(pBu_, pBu_[:, h * 512:(h + 1) * 512], ubf, ubf[:, c0:c0 + 128], BBD, flat(BBD[:, 4 * h:4 * h + 4, :, :]))
                    yield
                    S.tt("dve", tt_[0], tt_[0][:], Lmr, v3(Lmr), pBu_, Bu4[:, :, 0, :], ALU.mult)
                    S.tt("dve", tt_[1], tt_[1][:], Lmi, v3(Lmi), pBu_, Bu4[:, :, 1, :], ALU.mult)
                    S.tt("pool", X, X[:, :, 0, :], tt_[0], tt_[0][:], tt_[1], tt_[1][:], ALU.subtract)
                    yield
                    S.tt("dve", tt_[2], tt_[2][:], Lmr, v3(Lmr), pBu_, Bu4[:, :, 1, :], ALU.mult)
                    S.tt("dve", tt_[3], tt_[3][:], Lmi, v3(Lmi), pBu_, Bu4[:, :, 0, :], ALU.mult)
                    S.tt("pool", X, X[:, :, 1, :], tt_[2], tt_[2][:], tt_[3], tt_[3][:], ALU.add)
                    yield
                    hp, hc = Hc[(ci + 1) % 2], Hc[ci % 2]
                    first = ci == 0
                    for h in range(2):
                        S.mm(pG_, pG_[:, h * 512:(h + 1) * 512], tri[d], tri[d][:], X, flat(X[:, 4 * h:4 * h + 4, :, :]), start=True, stop=first)
                        if not first:
                            S.mm(pG_, pG_[:, h * 512:(h + 1) * 512], cst, sel[d], hp, flat(hp[:, 4 * h:4 * h + 4, :, :]), start=False, stop=True)
                    yield
                    S.tt("dve", tt_[0], tt_[0][:], Lpr, v3(Lpr), pG_, G4[:, :, 0, :], ALU.mult)
                    S.tt("dve", tt_[1], tt_[1][:], Lpi, v3(Lpi), pG_, G4[:, :, 1, :], ALU.mult)
                    S.tt("pool", hc, hc[:, :, 0, :], tt_[0], tt_[0][:], tt_[1], tt_[1][:], ALU.subtract)
                    yield
                    S.tt("dve", tt_[2], tt_[2][:], Lpr, v3(Lpr), pG_, G4[:, :, 1, :], ALU.mult)
                    S.tt("dve", tt_[3], tt_[3][:], Lpi, v3(Lpi), pG_, G4[:, :, 0, :], ALU.mult)
                    S.tt("pool", hc, hc[:, :, 1, :], tt_[2], tt_[2][:], tt_[3], tt_[3][:], ALU.add)
                    yield
                    ht = HTc[ci % 2]
                    for gp in range(8):
                        S.tr(pBu_, pBu_[:, gp * 128:(gp + 1) * 128], hc, hc[:, gp, :, :].rearrange("p c n -> p (c n)"), cst, ident)
                    yield
                    S.copy("act", ht, ht[:, :, :], pBu_, pBu_[:, :].rearrange("p (a b) -> p a b", a=8))
                    for gp in range(8):
                        S.mm(pG_, pG_[:, 0:128], CBD, CBD[:, gp, :], ht, ht[:, gp, :], start=(gp == 0), stop=(gp == 7))
                    yield
                    S.tt("dve", yacc, yacc[:, c0:c0 + 128], yacc, yacc[:, c0:c0 + 128], pG_, pG_[:, 0:128], ALU.add)
                    yield

            def rr(*gens):
                gens = list(gens)
                while gens:
                    for g in list(gens):
                        try:
                            next(g)
                            yield
                        except StopIteration:
                            gens.remove(g)

            for gb in range(ngb):
                u = u32[gb % 2]
                S.dma(u, u[:], uT.bufs(gb * 128, gb * 128 + 128, 0, T), uT.ap[gb * 128:(gb + 1) * 128, :])
                S.copy("pool", ubf, ubf[:], u, u[:])
                S.emit("pool", lambda e: e.memset(yacc[:], 0.0), [], [yacc])
                for _ in rr(s5_dir(gb, 0, u), s5_dir(gb, 1, u)):
                    pass
                v = vst[gb % 2]
                for tb in range(5):
                    c0 = tb * 512
                    n = min(512, T - c0)
                    S.stt("dve", ex, ex[:, 0:n], u, u[:, c0:c0 + n], dsk[:, gb:gb + 1], yacc, yacc[:, c0:c0 + n],
                          ALU.mult, ALU.add, extra_reads=[dsk])
                    S.act(ex2, ex2[:, 0:n], ex, ex[:, 0:n], AF.Square)
                    S.ts("dve", ex2, ex2[:, 0:n], ex2, ex2[:, 0:n], 0.044715, 1.0, ALU.mult, ALU.add)
                    S.tt("pool", ex2, ex2[:, 0:n], ex2, ex2[:, 0:n], ex, ex[:, 0:n], ALU.mult)
                    S.act(ex2, ex2[:, 0:n], ex2, ex2[:, 0:n], AF.Sigmoid, scale=2.0 * math.sqrt(2.0 / PI))
                    S.tt("pool", v, v[:, c0:c0 + n], ex, ex[:, 0:n], ex2, ex2[:, 0:n], ALU.mult)
                S.dma(vT.bufs(gb * 128, gb * 128 + 128, 0, T), vT.ap[gb * 128:(gb + 1) * 128, :], v, v[:])
            S.pop()

        def stage_glu_out_l0(modG):
            S.push()
            ps = mkps()
            vb = S.sbuf("vb", [128, EC, 512], BF16)
            zb = S.sbuf("zb", [128, EC, 512], BF16)
            gb_ = S.sbuf("gb", [128, EC, 512], BF16)
            xb = S.sbuf("xbo", [128, KC, 512])
            sg = S.sbuf("sg", [128, 512])
            gw = [S.sbuf("gw", [128, EC, 128], BF16) for _ in range(2)]
            glub = S.sbuf("glub", [128, EC])
            S.dma(glub, glub[:], W["l0_glu_b"], W["l0_glu_b"].t.rearrange("(e p) -> p e", p=128), slow=True)
            gv = W["l0_glu_w"].t.rearrange("(ei p) e -> p ei e", p=128)
            ov = W["l0_out_w"].t.rearrange("(ei p) e -> p ei e", p=128)
            xTv = xT.ap.rearrange("(kc p) t -> p kc t", p=128)
            vTv = vT.ap.rearrange("(e p) t -> p e t", p=128)
            zTv = szT.ap.rearrange("(e p) t -> p e t", p=128)
            wi = 0
            for (c0, n, r) in lat_blocks():
                S.dma(vb, vb[:, :, 0:n], vT.bufs(0, E, 0, T), vTv[:, :, c0:c0 + n])
                S.dma(zb, zb[:, :, 0:n], szT.bufs(0, E, 0, T), zTv[:, :, c0:c0 + n])
                S.dma(xb, xb[:, :, 0:n], xT.bufs(0, D, c0, c0 + n), xTv[:, :, c0:c0 + n])
                for eo in range(EC):
                    w = gw[wi % 2]; wi += 1
                    S.dma(w, w[:], W["l0_glu_w"], gv[:, :, eo * 128:(eo + 1) * 128], q="pool")
                    pb = ps[eo % 4]
                    for ei in range(EC):
                        S.mm(pb, pb[:, 0:n], w, w[:, ei, :], vb, vb[:, ei, 0:n], start=(ei == 0), stop=(ei == EC - 1))
                    S.act(sg, sg[:, 0:n], pb, pb[:, 0:n], AF.Sigmoid, bias=glub[:, eo:eo + 1], extra_reads=[glub])
                    S.tt("dve", sg, sg[:, 0:n], sg, sg[:, 0:n], vb, vb[:, eo, 0:n], ALU.mult)
                    S.tt("dve", gb_, gb_[:, eo, 0:n], sg, sg[:, 0:n], zb, zb[:, eo, 0:n], ALU.mult)
                for dc in range(KC):
                    w = gw[wi % 2]; wi += 1
                    S.dma(w, w[:], W["l0_out_w"], ov[:, :, dc * 128:(dc + 1) * 128], q="pool")
                    pb = ps[4 + dc % 4]
                    for ei in range(EC):
                        S.mm(pb, pb[:, 0:n], w, w[:, ei, :], gb_, gb_[:, ei, 0:n], start=(ei == 0), stop=(ei == EC - 1))
                    S.stt("dve", xb, xb[:, dc, 0:n], pb, pb[:, 0:n], modG[:, dc, r:r + 1], xb, xb[:, dc, 0:n],
                          ALU.mult, ALU.add, extra_reads=[modG])
                S.dma(xT.bufs(0, D, c0, c0 + n), xTv[:, :, c0:c0 + n], xb, xb[:, :, 0:n])
            S.pop()

        cst2 = S.sbuf("cst2", [128, C2_W])
        S.dma(cst2, cst2[:], cst2_in, cst2_in[:, :])
        bones = cst2[:, C2_BONES:C2_BONES + 128]
        rT = DramT(S, "rT", [E, T], F32, 128, T)
        kT = DramT(S, "kT", [E, T], F32, 128, T)
        v1T = DramT(S, "v1T", [E, T], F32, 128, T)
        vtm = DramT(S, "vtm", [T, E], F32, T, 128)
        lwT = [DramT(S, "lwT%d" % d, [E, T], F32, 128, T) for d in range(2)]
        arT = [DramT(S, "arT%d" % d, [E, T], F32, 128, T) for d in range(2)]
        gT = vT

        def colvec(name, src_ap_1d, n):
            t_ = S.sbuf(name, [128, n])
            return t_

        def stage_l1_proj(hT):
            S.push()
            ps = mkps()
            xi = S.sbuf("xi", [128, KC, T], BF16)
            mu = S.sbuf("mu", [128, 6, KC])
            for i in range(6):
                S.dma(mu, mu[:, i, :], W["l1_mu"], W["l1_mu"].t[i, :].rearrange("(kc p) -> p kc", p=128), slow=True)
            omu = S.sbuf("omu", [128, 6, KC])
            S.ts("dve", omu, omu[:], mu, mu[:], -1.0, 1.0, ALU.mult, ALU.add)
            wb = [S.sbuf("l1w", [128, KC, 128], BF16) for _ in range(2)]
            st32 = [S.sbuf("st32", [128, T])] * 2
            stbf = [S.sbuf("stbf", [128, T], BF16)] * 2

            def build_xi(i):
                for kc in range(KC):
                    eng = "dve" if kc % 2 else "pool"
                    S.ts(eng, xi, xi[:, kc, :], hT, hT[:, kc, :], omu[:, i, kc:kc + 1], None, ALU.mult, extra_reads=[omu])
                    m = mu[:, i, kc:kc + 1]
                    xl = xi[:, kc, TC:T].rearrange("p (r c) -> p r c", c=64)
                    hl = hT[:, kc, TC:T].rearrange("p (r c) -> p r c", c=64)
                    q = kc // 4
                    if q == 0:
                        o_, s_ = xl[:, :, 1:64], hl[:, :, 0:63]
                    elif q == 1:
                        o_, s_ = xl[:, :, 0:63], hl[:, :, 1:64]
                    elif q == 2:
                        o_, s_ = xl[:, 1:32, :], hl[:, 0:31, :]
                    else:
                        o_, s_ = xl[:, 0:31, :], hl[:, 1:32, :]
                    S.stt("dve", xi, o_, hT, s_, m, xi, o_, ALU.mult, ALU.add, extra_reads=[mu])
                    if kc < 8:
                        o_, s_ = xi[:, kc, 1:TC], hT[:, kc, 0:TC - 1]
                    else:
                        o_, s_ = xi[:, kc, 0:TC - 1], hT[:, kc, 1:TC]
                    S.stt("dve", xi, o_, hT, s_, m, xi, o_, ALU.mult, ALU.add, extra_reads=[mu])

            cnt = [0]

            def fm_proj(widx, dst, silu=False):
                wv = W["l1_in_w"].t[widx].rearrange("(kc p) e -> p kc e", p=128)
                for ec in range(EC):
                    w = wb[cnt[0] % 2]
                    S.dma(w, w[:], W["l1_in_w"], wv[:, :, ec * 128:(ec + 1) * 128], q="pool")
                    if silu:
                        st = stbf[cnt[0] % 2]
                        proj_chunk(xi, w, ps[0:4], lambda tb, c0, n, pb: S.act(st, st[:, c0:c0 + n], pb, pb[:, 0:n], AF.Silu))
                    else:
                        st = st32[cnt[0] % 2]
                        proj_chunk(xi, w, ps[0:4], lambda tb, c0, n, pb: S.copy("act" if tb % 2 else "dve", st, st[:, c0:c0 + n], pb, pb[:, 0:n]))
                    S.dma(dst.bufs(ec * 128, ec * 128 + 128, 0, T), dst.ap[ec * 128:(ec + 1) * 128, :], st, st[:])
                    cnt[0] += 1

            build_xi(0); fm_proj(0, rT)
            build_xi(1); fm_proj(1, kT)
            build_xi(2); fm_proj(2, v1T)
            S.push()
            wt = [S.sbuf("wvt", [128, KC, 512], BF16)] * 2
            vs = [S.sbuf("vs", [128, 512]) for _ in range(2)]
            wv = W["l1_in_w"].t[2].rearrange("(kc p) e -> p kc e", p=128)
            for cb in range(8):
                w = wt[cb % 2]
                S.dma(w, w[:], W["l1_in_w"], wv[:, :, cb * 512:(cb + 1) * 512], q="pool")
                for tt in range(NCH):
                    pb = ps[4 + tt % 4]
                    for kc in range(KC):
                        S.mm(pb, pb[:, :], xi, xi[:, kc, tt * 128:(tt + 1) * 128], w, w[:, kc, :], start=(kc == 0), stop=(kc == KC - 1))
                    o = vs[tt % 2]
                    S.copy("act" if tt % 2 else "dve", o, o[:], pb, pb[:, :])
                    S.dma(vtm.bufs(0, T, cb * 512, cb * 512 + 512), vtm.ap[tt * 128:(tt + 1) * 128, cb * 512:(cb + 1) * 512], o, o[:])
            S.pop()
            build_xi(3); fm_proj(3, szT, silu=True)
            S.push()
            l1 = S.sbuf("lora1", [128, KC, 96], BF16)
            l2 = S.sbuf("lora2", [96, E], BF16)
            lmid = S.sbuf("lmid", [96, T], BF16)
            b0 = S.sbuf("lb0", [128, EC])
            for (i, n1, n2, n0, dst, is_w) in ((4, "l1_w1", "l1_w2", "l1_w0", lwT, True), (5, "l1_a1", "l1_a2", "l1_a0", arT, False)):
                build_xi(i)
                for d in range(2):
                    S.dma(l1, l1[:], W[n1], W[n1].t[d].rearrange("(kc p) r -> p kc r", p=128), q="pool")
                    S.dma(l2, l2[:], W[n2], W[n2].t[d], q="pool")
                    S.dma(b0, b0[:], W[n0], W[n0].t[d, :].rearrange("(e p) -> p e", p=128), slow=True)
                    for tb in range(5):
                        c0 = tb * 512
                        n = min(512, T - c0)
                        pb = ps[tb % 4]
                        for kc in range(KC):
                            S.mm(pb, pb[0:96, 0:n], l1, l1[:, kc, :], xi, xi[:, kc, c0:c0 + n], start=(kc == 0), stop=(kc == KC - 1))
                        S.act(lmid, lmid[:, c0:c0 + n], pb, pb[0:96, 0:n], AF.Tanh if is_w else AF.Copy)
                    for ec in range(EC):
                        st = st32[cnt[0] % 2]
                        cnt[0] += 1
                        for tb in range(5):
                            c0 = tb * 512
                            n = min(512, T - c0)
                            pb = ps[4 + tb % 4]
                            S.mm(pb, pb[:, 0:n], l2, l2[:, ec * 128:(ec + 1) * 128], lmid, lmid[:, c0:c0 + n])
                            S.act(st, st[:, c0:c0 + n], pb, pb[:, 0:n], AF.Sigmoid, bias=b0[:, ec:ec + 1], extra_reads=[b0])
                        if is_w:
                            S.ts("dve", st, st[:], st, st[:], -math.exp(-0.5), None, ALU.mult)
                        S.dma(dst[d].bufs(ec * 128, ec * 128 + 128, 0, T), dst[d].ap[ec * 128:(ec + 1) * 128, :], st, st[:])
            S.pop()
            S.pop()

        def stage_l1_scan(nec=EC):
            S.push()
            L = 128
            def vec(nm, src):
                t_ = S.sbuf(nm, [128, EC])
                S.dma(t_, t_[:], W[src], W[src].t.rearrange("(e p) -> p e", p=128), slow=True)
                return t_
            kkv = vec("kkv", "l1_k_k"); kav = vec("kav", "l1_k_a"); lnw = vec("lnw", "l1_ln_w"); lnb = vec("lnb", "l1_ln_b")
            rkv = S.sbuf("rkv", [128, EC])
            S.dma(rkv, rkv[:], W["l1_r_k"], W["l1_r_k"].t.rearrange("(e h) k -> (h k) e", h=2), slow=True)
            KKN = S.sbuf("KKN", [128, T]); BON = S.sbuf("BON", [128, T]); YACC = S.sbuf("YACC", [128, T])
            RT = S.sbuf("RT", [128, T]); KT = S.sbuf("KT", [128, T]); AR = S.sbuf("AR", [128, T]); LW = S.sbuf("LW", [128, T])
            CIN = S.sbuf("CIN", [128, T]); TMP = S.sbuf("TMP", [128, T]); AT = S.sbuf("AT", [128, T]); BT = S.sbuf("BT", [128, T])
            GL = S.sbuf("GL", [128, NCH])
            Vp = S.sbuf("Vp", [128, NCH, 128])
            VpA = S.sbuf("VpA", [128, NCH, 128]); VpB = S.sbuf("VpB", [128, NCH, 128])
            S.emit("pool", lambda e: e.memset(VpA[:], 0.0), [], [VpA])
            S.emit("pool", lambda e: e.memset(VpB[:], 0.0), [], [VpB])
            szb = S.sbuf("szb", [128, T], BF16)
            gst = S.sbuf("gst", [128, T], BF16)
            Sbd = [S.sbuf("Sbd", [128, 128]) for _ in range(2)]
            for s_ in Sbd:
                S.emit("dve", lambda e, s_=s_: e.memset(s_[:], 0.0), [], [s_])
            M4 = [[S.sbuf("M4", [128, 4, 128]) for _ in range(3)] for _ in range(2)]
            XN = [[[S.sbuf("XN", [128, 2, 128]) for _ in range(2)] for _ in range(2)] for _ in range(2)]
            Wm = [[S.sbuf("Wm", [128, 128]) for _ in range(6)] for _ in range(2)]
            RHS = S.sbuf("RHS", [128, 128]); Us = S.sbuf("Us", [128, 128])
            UpA = S.sbuf("UpA", [128, 128]); UpB = S.sbuf("UpB", [128, 128])
            S.emit("dve", lambda e: e.memset(UpA[:], 0.0), [], [UpA])
            S.emit("dve", lambda e: e.memset(UpB[:], 0.0), [], [UpB])
            bk = S.sbuf("bk", [128, 2, 128]); bkT = S.sbuf("bkT", [128, 2, 128])
            pM = [S.psum("pM%d" % h, [128, 512]) for h in range(2)]
            pC = [[S.psum("pC%d%d" % (h, q), [128, 512]) for q in range(2)] for h in range(2)]
            pR = S.psum("pR", [128, 512]); pS = S.psum("pS", [128, 512])
            pYy = pR
            v3 = lambda t_: t_[:, :].rearrange("p (c l) -> p c l", l=L)

            for ec in range(nec):
                rows = (ec * 128, ec * 128 + 128)
                S.dma(KT, KT[:], kT.bufs(*rows, 0, T), kT.ap[rows[0]:rows[1], :])
                S.dma(RT, RT[:], rT.bufs(*rows, 0, T), rT.ap[rows[0]:rows[1], :])
                S.dma(AT, AT[:], v1T.bufs(*rows, 0, T), v1T.ap[rows[0]:rows[1], :])
                S.dma(AR, AR[:], arT[0].bufs(*rows, 0, T), arT[0].ap[rows[0]:rows[1], :])
                S.dma(LW, LW[:], arT[1].bufs(*rows, 0, T), arT[1].ap[rows[0]:rows[1], :])
                S.dma(Vp, Vp[:], vtm.bufs(0, T, rows[0], rows[1]), vtm.ap.rearrange("(c p) e -> p c e", p=128)[:, :, rows[0]:rows[1]])
                S.dma(szb, szb[:], szT.bufs(*rows, 0, T), szT.ap[rows[0]:rows[1], :])
                S.copy("pool", VpA, VpA[:, :, 0:64], Vp, Vp[:, :, 0:64])
                S.copy("pool", VpB, VpB[:, :, 64:128], Vp, Vp[:, :, 64:128])
                S.ts("dve", KKN, KKN[:], KT, KT[:], kkv[:, ec:ec + 1], None, ALU.mult, extra_reads=[kkv])
                S.act(TMP, TMP[:], KKN, KKN[:], AF.Square)
                for tb in range(5):
                    c0 = tb * 512; n = min(512, T - c0)
                    S.mm(pR, pR[:, 0:n], cst2, bones, TMP, TMP[:, c0:c0 + n])
                    S.act(CIN, CIN[:, c0:c0 + n], pR, pR[:, 0:n], AF.Sqrt)
                S.ts("dve", CIN, CIN[:], CIN, CIN[:], 1e-12, None, ALU.max)
                S.emit("dve", lambda e: e.reciprocal(out=CIN[:], in_=CIN[:]), [CIN], [CIN])
                S.tt("dve", KKN, KKN[:], KKN, KKN[:], CIN, CIN[:], ALU.mult)
                S.tt("pool", TMP, TMP[:], AR, AR[:], LW, LW[:], ALU.add)
                S.ts("dve", TMP, TMP[:], TMP, TMP[:], -2.0, None, ALU.add)
                S.ts("dve", TMP, TMP[:], TMP, TMP[:], kav[:, ec:ec + 1], None, ALU.mult, extra_reads=[kav])
                S.ts("dve", TMP, TMP[:], TMP, TMP[:], 2.0, None, ALU.add)
                S.tt("pool", TMP, TMP[:], TMP, TMP[:], KT, KT[:], ALU.mult)
                S.stt("dve", TMP, TMP[:], TMP, TMP[:], rkv[:, ec:ec + 1], RT, RT[:], ALU.mult, ALU.mult, extra_reads=[rkv])
                for tb in range(5):
                    c0 = tb * 512; n = min(512, T - c0)
                    S.mm(pR, pR[:, 0:n], cst2, bones, TMP, TMP[:, c0:c0 + n])
                    S.tt("dve", BON, BON[:, c0:c0 + n], AT, AT[:, c0:c0 + n], pR, pR[:, 0:n], ALU.mult)

                for d in range(2):
                    if not (d == 0):
                        S.dma(KT, KT[:], kT.bufs(*rows, 0, T), kT.ap[rows[0]:rows[1], :])
                        S.dma(RT, RT[:], rT.bufs(*rows, 0, T), rT.ap[rows[0]:rows[1], :])
                    if d == 1:
                        S.dma(AR, AR[:], arT[1].bufs(*rows, 0, T), arT[1].ap[rows[0]:rows[1], :])
                    S.dma(LW, LW[:], lwT[d].bufs(*rows, 0, T), lwT[d].ap[rows[0]:rows[1], :])
                    for ch_ in range(NCH):
                        S.emit("dve", lambda e, ch_=ch_: e.tensor_tensor_scan(out=CIN[:, ch_ * L:(ch_ + 1) * L], data0=ones,
                                                                             data1=LW[:, ch_ * L:(ch_ + 1) * L], initial=0.0,
                                                                             op0=ALU.mult, op1=ALU.add), [LW, cst], [CIN])
                    if d == 1:
                        S.tt("dve", TMP, TMP[:], LW, LW[:], CIN, CIN[:], ALU.subtract)
                        S.tt("dve", CIN, v3(CIN), TMP, v3(TMP), CIN, v3(CIN)[:, :, L - 1:L].to_broadcast([128, NCH, L]), ALU.add)
                    S.tt("pool", BT, BT[:], KKN, KKN[:], AR, AR[:], ALU.mult)
                    S.ts("dve", TMP, TMP[:], AR, AR[:], -1.0, None, ALU.add)
                    S.ts("dve", TMP, TMP[:], TMP, TMP[:], kav[:, ec:ec + 1], None, ALU.mult, extra_reads=[kav])
                    S.ts("dve", TMP, TMP[:], TMP, TMP[:], 1.0, None, ALU.add)
                    S.tt("pool", KT, KT[:], KT, KT[:], TMP, TMP[:], ALU.mult)
                    S.act(TMP, TMP[:], CIN, CIN[:], AF.Exp, scale=-1.0)
                    S.tt("dve", BT, BT[:], BT, BT[:], TMP, TMP[:], ALU.mult)
                    S.tt("pool", KT, KT[:], KT, KT[:], TMP, TMP[:], ALU.mult)
                    S.act(TMP, TMP[:], CIN, CIN[:], AF.Exp)
                    S.tt("dve", RT, RT[:], RT, RT[:], TMP, TMP[:], ALU.mult)
                    S.tt("pool", TMP, TMP[:], CIN, CIN[:], LW, LW[:], ALU.subtract)
                    S.act(TMP, TMP[:], TMP, TMP[:], AF.Exp)
                    S.stt("dve", AT, AT[:], KKN, KKN[:], -1.0, TMP, TMP[:], ALU.mult, ALU.mult)
                    endc = (L - 1) if d == 0 else 0
                    S.act(GL, GL[:], CIN, v3(CIN)[:, :, endc], AF.Exp)
                    cur = 0
                    for h in range(2):
                        pb_ = 64 * h
                        S.emit("dve", lambda e, h=h, pb_=pb_: e.memset(Sbd[0][pb_:pb_ + 64, pb_:pb_ + 64], 0.0), [], [Sbd[0]])
                    order = list(range(NCH)) if d == 0 else [1, 0] + list(range(NCH - 1, 1, -1))
                    m4c = cst2[:, C2_M4 + d * 512:C2_M4 + (d + 1) * 512].rearrange("p (a t) -> p a t", a=4)
                    mNc = cst2[:, C2_MN + d * 128:C2_MN + (d + 1) * 128]
                    Wfin = {}

                    def pre_head(h, ci, ch):
                        cols = slice(ch * L, ch * L + L)
                        hp = slice(64 * h, 64 * h + 64)
                        m4 = M4[h][ci % 3]
                        pc = pC[h][ci % 2]
                        S.mm(pM[h], pM[h][:, 0:128], BT, BT[hp, cols], AT, AT[hp, cols])
                        S.mm(pM[h], pM[h][:, 128:256], BT, BT[hp, cols], RT, RT[hp, cols])
                        S.mm(pM[h], pM[h][:, 256:384], KT, KT[hp, cols], AT, AT[hp, cols])
                        S.mm(pM[h], pM[h][:, 384:512], KT, KT[hp, cols], RT, RT[hp, cols])
                        S.mm(pc, pc[:, 384:512], AT, AT[hp, cols], BT, BT[hp, cols])
                        S.tt("dve", m4, m4[:], cst2, m4c, pM[h], pM[h][:, :].rearrange("p (a t) -> p a t", a=4), ALU.mult)
                        cur_ = XN[h][ci % 2][0]
                        S.tt("dve", cur_, cur_[:, 0, :], cst2, mNc, pc, pc[:, 384:512], ALU.mult)
                        yield
                        S.copy("act", cur_, cur_[:, 1, :], m4, m4[:, 0, :])
                        Wc = Wm[h][2 * (ci % 3)]
                        S.tt("pool", Wc, Wc[:], m4, m4[:, 0, :], cst, ident, ALU.add)
                        yield
                        pp, wi_ = 0, 0
                        for j in range(6):
                            nxt_ = XN[h][ci % 2][1 - pp]
                            S.mm(pc, pc[:, 0:128], cur_, cur_[:, 1, :], cur_, cur_[:, 0, :])
                            if j < 5:
                                S.mm(pc, pc[:, 128:256], cur_, cur_[:, 0, :], cur_, cur_[:, 1, :])
                            yield
                            if j < 5:
                                S.copy("act", nxt_, nxt_[:, :, :], pc, pc[:, 0:256].rearrange("p (a b) -> p a b", a=2))
                            else:
                                S.copy("act", nxt_, nxt_[:, 0, :], pc, pc[:, 0:128])
                            yield
                            S.mm(pc, pc[:, 256:384], nxt_, nxt_[:, 0, :], Wc, Wc[:])
                            yield
                            Wn = Wm[h][2 * (ci % 3) + 1 - wi_]
                            S.tt("dve", Wn, Wn[:], Wc, Wc[:], pc, pc[:, 256:384], ALU.add)
                            yield
                            cur_, Wc = nxt_, Wn
                            pp, wi_ = 1 - pp, 1 - wi_
                        Wfin[(h, ci)] = Wc

                    def seq(ci, ch):
                        cols = slice(ch * L, ch * L + L)
                        S0, S1 = Sbd[ci % 2], Sbd[(ci + 1) % 2]
                        for h in range(2):
                            hp = slice(64 * h, 64 * h + 64)
                            m4 = M4[h][ci % 3]
                            S.mm(pR, pR[:, 64 * h:64 * h + 64], AT, AT[hp, cols], S0, S0[hp, 64 * h:64 * h + 64], start=True, stop=False)
                            S.mm(pR, pR[:, 64 * h:64 * h + 64], m4, m4[:, 2, :], Vp, Vp[:, ch, 64 * h:64 * h + 64], start=False, stop=True)
                        yield
                        S.copy("act", RHS, RHS[:], pR, pR[:, 0:128])
                        yield
                        for h in range(2):
                            Wh = Wfin[(h, ci)]
                            S.mm(pR, pR[:, 128 + 64 * h:128 + 64 * h + 64], Wh, Wh[:], RHS, RHS[:, 64 * h:64 * h + 64])
                        yield
                        S.copy("act", Us, Us[:], pR, pR[:, 128:256])
                        S.copy("dve", UpA, UpA[:, 0:64], pR, pR[:, 128:192])
                        S.copy("dve", UpB, UpB[:, 64:128], pR, pR[:, 192:256])
                        yield
                        S.mm(pYy, pYy[:, 256:384], S0, S0[:], RT, RT[:, cols], start=True, stop=False)
                        S.mm(pYy, pYy[:, 256:384], VpA, VpA[:, ch, :], M4[0][ci % 3], M4[0][ci % 3][:, 3, :], start=False, stop=False)
                        S.mm(pYy, pYy[:, 256:384], VpB, VpB[:, ch, :], M4[1][ci % 3], M4[1][ci % 3][:, 3, :], start=False, stop=False)
                        yield
                        S.mm(pYy, pYy[:, 256:384], UpA, UpA[:], M4[0][ci % 3], M4[0][ci % 3][:, 1, :], start=False, stop=False)
                        S.mm(pYy, pYy[:, 256:384], UpB, UpB[:], M4[1][ci % 3], M4[1][ci % 3][:, 1, :], start=False, stop=True)
                        yield
                        if d == 0:
                            S.copy("act", YACC, YACC[:, cols], pYy, pYy[:, 256:384])
                        else:
                            S.tt("dve", YACC, YACC[:, cols], YACC, YACC[:, cols], pYy, pYy[:, 256:384], ALU.add)
                        yield
                        S.ts("pool", bk, bk[:, 0, :], BT, BT[:, cols], GL[:, ch:ch + 1], None, ALU.mult, extra_reads=[GL])
                        S.ts("pool", bk, bk[:, 1, :], KT, KT[:, cols], GL[:, ch:ch + 1], None, ALU.mult, extra_reads=[GL])
                        yield
                        S.tr(pS, pS[:, 0:128], bk, bk[:, 0, :], cst, ident)
                        S.tr(pS, pS[:, 128:256], bk, bk[:, 1, :], cst, ident)
                        yield
                        S.copy("act", bkT, bkT[:], pS, pS[:, 0:256].rearrange("p (a b) -> p a b", a=2))
                        yield
                        S.mm(pS, pS[:, 256:384], bkT, bkT[:, 0, :], Us, Us[:], start=True, stop=False)
                        S.mm(pS, pS[:, 256:384], bkT, bkT[:, 1, :], Vp, Vp[:, ch, :], start=False, stop=True)
                        yield
                        for h in range(2):
                            hp = slice(64 * h, 64 * h + 64)
                            S.stt("dve", S1, S1[hp, 64 * h:64 * h + 64], S0, S0[hp, 64 * h:64 * h + 64], GL[hp, ch:ch + 1],
                                  pS, pS[hp, 256 + 64 * h:256 + 64 * h + 64], ALU.mult, ALU.add, extra_reads=[GL])
                        yield

                    def mkpre(ci_):
                        return [pre_head(0, ci_, order[ci_]), pre_head(1, ci_, order[ci_])]

                    def drain(must, others):
                        live = list(must) + list(others)
                        must = set(id(g) for g in must)
                        while must:
                            for g in list(live):
                                try:
                                    next(g)
                                except StopIteration:
                                    live.remove(g)
                                    must.discard(id(g))
                            if not live:
                                break
                        return [g for g in live]

                    nord = len(order)
                    ahead = drain(mkpre(0), mkpre(1) if nord > 1 else [])
                    for ci, ch in enumerate(order):
                        newp = mkpre(ci + 2) if ci + 2 < nord else []
                        ahead = drain([seq(ci, ch)] + ahead, newp)
                for tb in range(5):
                    c0 = tb * 512; n = min(512, T - c0)
                    S.mm(pR, pR[:, 0:n], cst2, bones, YACC, YACC[:, c0:c0 + n])
                    S.stt("dve", TMP, TMP[:, c0:c0 + n], pR, pR[:, 0:n], -1.0 / 64, YACC, YACC[:, c0:c0 + n], ALU.mult, ALU.add)
                    S.act(CIN, CIN[:, c0:c0 + n], TMP, TMP[:, c0:c0 + n], AF.Square)
                    S.mm(pYy, pYy[:, 0:n], cst2, bones, CIN, CIN[:, c0:c0 + n])
                    S.ts("dve", CIN, CIN[:, c0:c0 + n], pYy, pYy[:, 0:n], 1.0 / 64, GN_EPS, ALU.mult, ALU.add)
                    S.act(CIN, CIN[:, c0:c0 + n], CIN, CIN[:, c0:c0 + n], AF.Sqrt)
                    S.emit("dve", lambda e, c0=c0, n=n: e.reciprocal(out=CIN[:, c0:c0 + n], in_=CIN[:, c0:c0 + n]), [CIN], [CIN])
                    S.stt("dve", TMP, TMP[:, c0:c0 + n], TMP, TMP[:, c0:c0 + n], lnw[:, ec:ec + 1], CIN, CIN[:, c0:c0 + n],
                          ALU.mult, ALU.mult, extra_reads=[lnw])
                    S.stt("dve", TMP, TMP[:, c0:c0 + n], TMP, TMP[:, c0:c0 + n], lnb[:, ec:ec + 1], BON, BON[:, c0:c0 + n],
                          ALU.add, ALU.add, extra_reads=[lnb])
                    S.tt("pool", gst, gst[:, c0:c0 + n], TMP, TMP[:, c0:c0 + n], szb, szb[:, c0:c0 + n], ALU.mult)
                S.dma(gT.bufs(*rows, 0, T), gT.ap[rows[0]:rows[1], :], gst, gst[:])
            S.pop()

        def stage_l1_out(modG):
            S.push()
            ps = mkps()
            gb_ = S.sbuf("g1", [128, EC, 512], BF16)
            xb = S.sbuf("xb1", [128, KC, 512])
            sq = S.sbuf("sq1", [128, KC, 512])
            rstd = S.sbuf("rstd1", [128, 512])
            ot = [S.sbuf("ot", [128, D]) for _ in range(2)]
            gw = [S.sbuf("gw1", [128, EC, 128], BF16) for _ in range(2)]
            fng = S.sbuf("fng", [128, KC])
            S.dma(fng, fng[:], W["final_norm_g"], W["final_norm_g"].t.rearrange("(kc p) -> p kc", p=128), slow=True)
            ov = W["l1_out_w"].t.rearrange("(ei p) e -> p ei e", p=128)
            xTv = xT.ap.rearrange("(kc p) t -> p kc t", p=128)
            gTv = gT.ap.rearrange("(e p) t -> p e t", p=128)
            wi = 0
            oi = 0
            for (c0, n, r) in lat_blocks()[1:]:
                S.dma(gb_, gb_[:, :, 0:n], gT.bufs(0, E, 0, T), gTv[:, :, c0:c0 + n])
                S.dma(xb, xb[:, :, 0:n], xT.bufs(0, D, c0, c0 + n), xTv[:, :, c0:c0 + n])
                for dc in range(KC):
                    w = gw[wi % 2]; wi += 1
                    S.dma(w, w[:], W["l1_out_w"], ov[:, :, dc * 128:(dc + 1) * 128], q="pool")
                    pb = ps[dc % 4]
                    for ei in range(EC):
                        S.mm(pb, pb[:, 0:n], w, w[:, ei, :], gb_, gb_[:, ei, 0:n], start=(ei == 0), stop=(ei == EC - 1))
                    S.stt("dve", xb, xb[:, dc, 0:n], pb, pb[:, 0:n], modG[:, dc, r:r + 1], xb, xb[:, dc, 0:n],
                          ALU.mult, ALU.add, extra_reads=[modG])
                S.act(sq, sq[:, :, 0:n], xb, xb[:, :, 0:n], AF.Square)
                pb = ps[4]
                for kc in range(KC):
                    S.mm(pb, pb[:, 0:n], cst, ones, sq, sq[:, kc, 0:n], start=(kc == 0), stop=(kc == KC - 1))
                S.ts("dve", rstd, rstd[:, 0:n], pb, pb[:, 0:n], 1.0 / D, RMS_EPS, ALU.mult, ALU.add)
                S.act(rstd, rstd[:, 0:n], rstd, rstd[:, 0:n], AF.Sqrt)
                S.emit("dve", lambda e, n=n: e.reciprocal(out=rstd[:, 0:n], in_=rstd[:, 0:n]), [rstd], [rstd])
                for kc in range(KC):
                    S.stt("dve", xb, xb[:, kc, 0:n], xb, xb[:, kc, 0:n], fng[:, kc:kc + 1], rstd, rstd[:, 0:n],
                          ALU.mult, ALU.mult, extra_reads=[fng])
                for tt in range(n // 128):
                    o = ot[oi % 2]; oi += 1
                    for q in range(4):
                        pb = ps[5 + (q % 3)]
                        for j in range(4):
                            kc = 4 * q + j
                            S.tr(pb, pb[:, j * 128:(j + 1) * 128], xb, xb[:, kc, tt * 128:(tt + 1) * 128], cst, ident)
                        S.copy("act" if q % 2 == 0 else "dve", o, o[:, q * 512:(q + 1) * 512], pb, pb[:, :])
                    t0 = c0 - TC + tt * 128
                    S.dma(out_t, out_t[t0:t0 + 128, :], o, o[:])
            S.pop()

        dbg = dbg or ""
        stage_transpose_in()
        modA = S.sbuf("modA", [128, KC, 2]); modB = S.sbuf("modB", [128, KC, 2]); modG = S.sbuf("modG", [128, KC, 2])
        l1_only = dbg.startswith("l1")
        if not l1_only:
            stage_ada("l0_", modA, modB, modG)
            S.push()
            hT = S.sbuf("hT", [128, KC, T], BF16)
            stage_norm(hT, modA, modB)
            if dbg == "h0":
                ho = Buf(nc.dram_tensor("dbg_h", [D, T], BF16, kind="ExternalOutput").ap(), "dbg_h")
                S.dma(ho, ho.t.rearrange("(kc p) t -> p kc t", p=128), hT, hT[:])
                S.finish([ho])
            stage_inproj_l0(hT)
            S.pop()
        if dbg in ("s5pre", "s5") or dbg.startswith("s5pre:"):
            stage_s5(1 if dbg == "s5" else 0)
            if dbg == "s5":
                o_ = Buf(nc.dram_tensor("dbg_v", [128, T], BF16, kind="ExternalOutput").ap(), "dbg_v")
                S.push(); tb_ = S.sbuf("dbgv", [128, T], BF16)
                S.dma(tb_, tb_[:], vT.bufs(0, 128, 0, T), vT.ap[0:128, :]); S.dma(o_, o_[:, :], tb_, tb_[:]); S.pop()
                S.finish([o_])
        elif dbg != "h0" and not l1_only:
            stage_s5()
            stage_glu_out_l0(modG)
            if dbg == "l0":
                xo = Buf(nc.dram_tensor("dbg_x", [D, T], F32, kind="ExternalOutput").ap(), "dbg_x")
                S.push()
                tb_ = S.sbuf("dbgb", [128, KC, 256])
                xTv = xT.ap.rearrange("(kc p) t -> p kc t", p=128)
                for bi in range(9):
                    S.dma(tb_, tb_[:], xT.bufs(0, D, bi * 256, bi * 256 + 256), xTv[:, :, bi * 256:(bi + 1) * 256])
                    S.dma(xo, xo.t.rearrange("(kc p) t -> p kc t", p=128)[:, :, bi * 256:(bi + 1) * 256], tb_, tb_[:])
                S.pop()
                S.finish([xo])
        if dbg in ("", "l1", "l1scan"):
            stage_ada("l1_", modA, modB, modG)
            S.push()
            hT = S.sbuf("hT1", [128, KC, T], BF16)
            stage_norm(hT, modA, modB)
            stage_l1_proj(hT)
            S.pop()
            if dbg == "l1scan":
                stage_l1_scan(1)
                o_ = Buf(nc.dram_tensor("dbg_g", [128, T], BF16, kind="ExternalOutput").ap(), "dbg_g")
                S.push(); tb_ = S.sbuf("dbgg", [128, T], BF16)
                S.dma(tb_, tb_[:], gT.bufs(0, 128, 0, T), gT.ap[0:128, :]); S.dma(o_, o_[:, :], tb_, tb_[:]); S.pop()
                S.finish([o_])
            else:
                stage_l1_scan()
                stage_l1_out(modG)
                S.finish([out_t])
        print("instructions:", S.nins, S.cnt)
    return nc


def make_consts():
    c = np.zeros((128, C_W), np.float32)
    s = np.arange(128)
    c[:, C_IDENT:C_IDENT + 128] = np.eye(128)
    c[:, C_ONES:C_ONES + 128] = 1.0
    c[:, C_TRIF:C_TRIF + 128] = (s[:, None] <= s[None, :])
    c[:, C_TRIB:C_TRIB + 128] = (s[:, None] >= s[None, :])
    c[127, C_SELF:C_SELF + 128] = 1.0
    c[0, C_SELB:C_SELB + 128] = 1.0
    c[:, C_S1 + 0] = s + 1
    c[:, C_S1 + 1] = -(s + 1)
    c[:, C_S1 + 2] = 128 - s
    c[:, C_S1 + 3] = -(128 - s)
    c[:, C_MBD:C_MBD + 8] = ((s[:, None] // 16) == np.arange(8)[None, :])
    return c


WEIGHT_NAMES = ["l0_norm_g", "l0_ada_w", "l0_ada_b", "l0_in_w", "l0_a_re", "l0_a_im", "l0_log_step", "l0_b_re",
                "l0_b_im", "l0_c_re", "l0_c_im", "l0_d", "l0_glu_w", "l0_glu_b", "l0_out_w", "l1_norm_g", "l1_ada_w",
                "l1_ada_b", "l1_mu", "l1_in_w", "l1_w0", "l1_w1", "l1_w2", "l1_a0", "l1_a1", "l1_a2", "l1_k_k",
                "l1_k_a", "l1_r_k", "l1_ln_w", "l1_ln_b", "l1_out_w", "final_norm_g"]


def make_consts2():
    c = np.zeros((128, C2_W), np.float32)
    s = np.arange(128)
    lt = (s[:, None] < s[None, :]).astype(np.float32)
    le = (s[:, None] <= s[None, :]).astype(np.float32)
    gt = (s[:, None] > s[None, :]).astype(np.float32)
    ge = (s[:, None] >= s[None, :]).astype(np.float32)
    c[:, C2_M4:C2_M4 + 512] = np.concatenate([lt, le, lt, le], 1)
    c[:, C2_M4 + 512:C2_M4 + 1024] = np.concatenate([gt, ge, gt, ge], 1)
    c[:, C2_MN:C2_MN + 128] = gt
    c[:, C2_MN + 128:C2_MN + 256] = lt
    c[:, C2_BONES:C2_BONES + 128] = ((s[:, None] // 64) == (s[None, :] // 64))
    c[:, C2_RESET:C2_RESET + 128] = (s[None, :] != 0)
    return c


def make_in_maps(inputs, cores):
    cst = make_consts()
    cst2 = make_consts2()
    maps = []
    for b in cores:
        m = {"x": np.ascontiguousarray(inputs["x"][b]), "ctx": np.ascontiguousarray(inputs["ctx"][b]),
             "cond": np.ascontiguousarray(np.stack([inputs["c"][b], inputs["c_ctx"]], 0)), "cst": cst, "cst2": cst2}
        for nm in WEIGHT_NAMES:
            m[nm] = np.ascontiguousarray(inputs[nm], dtype=np.float32)
        maps.append(m)
    return maps


def kernel(**inputs):
    nc = build_program()
    maps = make_in_maps(inputs, list(range(8)))
    res = run_bass_kernel_spmd(nc, maps, core_ids=list(range(8)))
    return np.stack([r["out"] for r in res.results], 0).astype(np.float32)
```

```python
import contextlib, math
import numpy as np
import concourse.bass as bass
import concourse.mybir as mybir
from concourse.bass_utils import run_bass_kernel_spmd

F32 = mybir.dt.float32
BF16 = mybir.dt.bfloat16
ALU = mybir.AluOpType
AF = mybir.ActivationFunctionType
AX = mybir.AxisListType

D = 2048
E = 4096
T = 2304
TC = 256
KC = 16
EC = 32
NCH = 18
PI = math.pi
RMS_EPS = 1e-6
GN_EPS = 64e-5

C_IDENT, C_ONES, C_TRIF, C_TRIB, C_SELF, C_SELB = 0, 128, 256, 384, 512, 640
C_S1 = 768
C_MBD = 772
C_W = 780
C2_M4, C2_MN, C2_BONES, C2_RESET = 0, 1024, 1280, 1408
C2_W = 1536


class Ev:
    __slots__ = ("sem", "val", "eng", "key")

    def __init__(self, sem, val, eng, key):
        self.sem, self.val, self.eng, self.key = sem, val, eng, key


class Buf:
    __slots__ = ("t", "w", "r", "name", "excl")

    def __init__(self, t=None, name=""):
        self.t, self.w, self.r, self.name, self.excl = t, None, {}, name, False

    def __getitem__(self, idx):
        return self.t[idx]


class Sched:
    SAME_ENGINE_SYNC = True
    NDMASEM = 24

    def __init__(self, nc, es):
        self.nc = nc
        self.stack = [es]
        self.engs = {"pe": nc.tensor, "act": nc.scalar, "dve": nc.vector, "pool": nc.gpsimd, "sp": nc.sync}
        self.sem = {k: es.enter_context(nc.semaphore("s_" + k)) for k in ("pe", "act", "dve", "pool")}
        self.cnt = {k: 0 for k in self.sem}
        self.dsem = [es.enter_context(nc.semaphore("d%d" % i)) for i in range(2 * self.NDMASEM)]
        self.duse = [0] * (2 * self.NDMASEM)
        self.dnext = {"sp": 0, "pool": 0, "act": 0}
        self.seen = {k: {} for k in self.engs}
        self.nins = 0
        self.uid = 0
        self.scope_bufs = [[]]
        self.pending_free = {}

    def push(self):
        es = contextlib.ExitStack()
        es.__enter__()
        self.stack.append(es)
        self.scope_bufs.append([])

    def pop(self):
        for b in self.scope_bufs.pop():
            evs = list(b.r.values()) + ([b.w] if b.w is not None else [])
            for ev in evs:
                old = self.pending_free.get(ev.key)
                if old is None or old.val < ev.val:
                    self.pending_free[ev.key] = ev
        es = self.stack.pop()
        es.__exit__(None, None, None)

    def _newbuf(self, t, nm):
        b = Buf(t, nm)
        b.r = dict(self.pending_free)
        self.scope_bufs[-1].append(b)
        return b

    def sbuf(self, name, shape, dt=F32):
        self.uid += 1
        nm = "%s_%d" % (name, self.uid)
        return self._newbuf(self.stack[-1].enter_context(self.nc.sbuf_tensor(nm, list(shape), dt)), nm)

    def psum(self, name, shape, dt=F32):
        b = self._newbuf(self.stack[-1].enter_context(self.nc.psum_tensor(name, list(shape), dt)), name)
        b.excl = True
        return b

    def dram(self, name, shape, dt=F32, kind="Internal"):
        return self.nc.dram_tensor(name, list(shape), dt, kind=kind).ap()

    def _wait(self, engname, ev):
        seen = self.seen[engname]
        if seen.get(ev.key, 0) >= ev.val:
            return
        self.engs[engname].wait_ge(ev.sem, ev.val)
        seen[ev.key] = ev.val

    def emit(self, engname, fn, reads=(), writes=(), dma=False):
        waits = []
        for b in reads:
            if b.w is not None:
                waits.append(b.w)
            if b.excl:
                waits.extend(ev for ev in b.r.values() if ev.eng != engname)
        for b in writes:
            if b.w is not None:
                waits.append(b.w)
            waits.extend(b.r.values())
        for ev in waits:
            if ev.eng == engname and not dma:
                if engname in ("pe", "dve", "act") or not self.SAME_ENGINE_SYNC:
                    continue
            self._wait(engname, ev)
        eng = self.engs[engname]
        if dma:
            i = self.dnext[engname] + (self.NDMASEM if engname == "pool" else 0)
            self.dnext[engname] = (self.dnext[engname] + 1) % self.NDMASEM
            s = self.dsem[i]
            if self.duse[i] > 0:
                self._wait(engname, Ev(s, 16 * self.duse[i], "dma", ("d", i)))
            ins = fn(eng)
            self.duse[i] += 1
            ins.then_inc(s, 16)
            ev = Ev(s, 16 * self.duse[i], "dma", ("d", i))
        else:
            ins = fn(eng)
            self.cnt[engname] += 1
            ins.then_inc(self.sem[engname], 1)
            ev = Ev(self.sem[engname], self.cnt[engname], engname, engname)
        for b in reads:
            b.r[ev.key] = ev
        for b in writes:
            b.w = ev
            b.r = {}
        self.nins += 1
        return ev

    def dma(self, out_b, out_ap, in_b, in_ap, q="sp", slow=False):
        outs = out_b if isinstance(out_b, (list, tuple)) else [out_b]
        ins = in_b if isinstance(in_b, (list, tuple)) else [in_b]
        if slow:
            f = lambda e: e.dma_start(out=out_ap, in_=in_ap, allow_slow_non_contiguous=True)
        else:
            f = lambda e: e.dma_start(out=out_ap, in_=in_ap)
        return self.emit(q, f, reads=ins, writes=outs, dma=True)

    def finish(self, bufs):
        for b in bufs:
            if b.w is not None:
                self._wait("sp", b.w)

    def act(self, ob, o, ib, i, func, bias=None, scale=None, eng="act", extra_reads=()):
        kw = {}
        if bias is not None:
            kw["bias"] = bias
        if scale is not None:
            kw["scale"] = scale
        return self.emit(eng, lambda e: e.activation(out=o, in_=i, func=func, **kw), [ib] + list(extra_reads), [ob])

    def tt(self, eng, ob, o, ab, a, bb, b, op):
        return self.emit(eng, lambda e: e.tensor_tensor(out=o, in0=a, in1=b, op=op), [ab, bb], [ob])

    def ts(self, eng, ob, o, ab, a, s1, s2, op0, op1=None, extra_reads=()):
        if op1 is None:
            f = lambda e: e.tensor_scalar(out=o, in0=a, scalar1=s1, scalar2=None, op0=op0)
        else:
            f = lambda e: e.tensor_scalar(out=o, in0=a, scalar1=s1, scalar2=s2, op0=op0, op1=op1)
        return self.emit(eng, f, [ab] + list(extra_reads), [ob])

    def stt(self, eng, ob, o, ab, a, sc, bb, b, op0, op1, extra_reads=()):
        return self.emit(eng, lambda e: e.scalar_tensor_tensor(out=o, in0=a, scalar=sc, in1=b, op0=op0, op1=op1),
                         [ab, bb] + list(extra_reads), [ob])

    def copy(self, eng, ob, o, ib, i):
        if eng == "act":
            return self.emit(eng, lambda e: e.activation(out=o, in_=i, func=AF.Copy), [ib], [ob])
        return self.emit(eng, lambda e: e.tensor_copy(out=o, in_=i), [ib], [ob])

    def mm(self, ob, o, lb, lhsT, rb, rhs, start=True, stop=True):
        return self.emit("pe", lambda e: e.matmul(out=o, lhsT=lhsT, rhs=rhs, start=start, stop=stop), [lb, rb], [ob])

    def tr(self, ob, o, ib, i, identb, ident):
        return self.emit("pe", lambda e: e.transpose(out=o, in_=i, identity=ident), [ib, identb], [ob])


class DramT:
    def __init__(self, S, name, shape, dt, ru, cu, kind="Internal"):
        self.ap = S.dram(name, shape, dt, kind=kind)
        self.ru, self.cu = ru, cu
        self.nr, self.ncu = (shape[0] + ru - 1) // ru, (shape[1] + cu - 1) // cu
        self.units = [[Buf(None, "%s_%d_%d" % (name, i, j)) for j in range(self.ncu)] for i in range(self.nr)]

    def bufs(self, r0, r1, c0, c1):
        out = []
        for i in range(r0 // self.ru, (r1 - 1) // self.ru + 1):
            for j in range(c0 // self.cu, (c1 - 1) // self.cu + 1):
                out.append(self.units[i][j])
        return out


def lat_blocks():
    return [(0, 256, 1)] + [(256 + 512 * i, 512, 0) for i in range(4)]


def build_program(dbg=None):
    dbg = dbg or ""
    nc = bass.Bass("TRN2", target_bir_lowering=False)
    es = contextlib.ExitStack()
    with es:
        S = Sched(nc, es)
        inp = {}

        def ext(name, shape):
            b = Buf(nc.dram_tensor(name, list(shape), F32, kind="ExternalInput").ap(), name)
            inp[name] = b
            return b

        x_in = ext("x", [2048, D])
        ctx_in = ext("ctx", [TC, D])
        cond_in = ext("cond", [2, D])
        cst_in = ext("cst", [128, C_W])
        cst2_in = ext("cst2", [128, C2_W])
        W = {}
        for nm, shp in [
            ("l0_norm_g", [D]), ("l0_ada_w", [D, 3 * D]), ("l0_ada_b", [3 * D]), ("l0_in_w", [D, 2 * E]),
            ("l0_a_re", [2, 256, 64]), ("l0_a_im", [2, 256, 64]), ("l0_log_step", [2, 256]),
            ("l0_b_re", [2, 256, 64, 16]), ("l0_b_im", [2, 256, 64, 16]),
            ("l0_c_re", [2, 256, 16, 64]), ("l0_c_im", [2, 256, 16, 64]),
            ("l0_d", [E]), ("l0_glu_w", [E, E]), ("l0_glu_b", [E]), ("l0_out_w", [E, D]),
            ("l1_norm_g", [D]), ("l1_ada_w", [D, 3 * D]), ("l1_ada_b", [3 * D]), ("l1_mu", [6, D]),
            ("l1_in_w", [4, D, E]), ("l1_w0", [2, E]), ("l1_w1", [2, D, 96]), ("l1_w2", [2, 96, E]),
            ("l1_a0", [2, E]), ("l1_a1", [2, D, 96]), ("l1_a2", [2, 96, E]), ("l1_k_k", [E]), ("l1_k_a", [E]),
            ("l1_r_k", [64, 64]), ("l1_ln_w", [E]), ("l1_ln_b", [E]), ("l1_out_w", [E, D]), ("final_norm_g", [D]),
        ]:
            W[nm] = ext(nm, shp)
        out_t = Buf(nc.dram_tensor("out", [2048, D], F32, kind="ExternalOutput").ap(), "out")

        xT = DramT(S, "xT", [D, T], F32, D, 256)
        uT = DramT(S, "uT", [E, T], F32, 128, T)
        szT = DramT(S, "szT", [E, T], BF16, 128, T)
        vT = DramT(S, "vT", [E, T], BF16, 128, T)
        s5c = DramT(S, "s5c", [4 * 512, 64], F32, 512, 64)

        cst = S.sbuf("cst", [128, C_W])
        S.dma(cst, cst[:], cst_in, cst_in[:, :])
        ident = cst[:, C_IDENT:C_IDENT + 128]
        ones = cst[:, C_ONES:C_ONES + 128]
        def mkps(n=8):
            return [S.psum("ps%d_%d" % (i, S.uid), [128, 512]) for i in range(n)]

        dbg_out = {}

        def stage_transpose_in():
            S.push()
            ps = mkps()
            xin = [S.sbuf("xin", [128, D]) for _ in range(2)]
            xtt = [S.sbuf("xtt", [128, KC, 128]) for _ in range(2)]
            xTv = xT.ap.rearrange("(kc p) t -> p kc t", p=128)
            for i in range(NCH):
                src_b, src = (ctx_in, ctx_in[i * 128:(i + 1) * 128, :]) if i < 2 else (x_in, x_in[(i - 2) * 128:(i - 1) * 128, :])
                xi, xo = xin[i % 2], xtt[i % 2]
                S.dma(xi, xi[:], src_b, src)
                for q in range(4):
                    pb = ps[(i * 4 + q) % 8]
                    for j in range(4):
                        kc = 4 * q + j
                        S.tr(pb, pb[:, j * 128:(j + 1) * 128], xi, xi[:, kc * 128:(kc + 1) * 128], cst, ident)
                    S.copy("act" if q % 2 == 0 else "dve", xo, xo[:, 4 * q:4 * q + 4, :],
                           pb, pb[:, :].rearrange("p (a b) -> p a b", a=4))
                S.dma(xT.bufs(0, D, i * 128, (i + 1) * 128), xTv[:, :, i * 128:(i + 1) * 128], xo, xo[:])
            S.pop()

        def stage_ada(pre, modA, modB, modG):
            S.push()
            ps = mkps()
            condT = S.sbuf("condT", [128, 2, KC])
            for r in range(2):
                S.dma(condT, condT[:, r, :], cond_in, cond_in.t[r, :].rearrange("(kc p) -> p kc", p=128), slow=True)
            S.act(condT, condT[:], condT, condT[:], AF.Silu)
            adab = S.sbuf("adab", [128, 48])
            S.dma(adab, adab[:], W[pre + "ada_b"], W[pre + "ada_b"].t.rearrange("(jc p) -> p jc", p=128), slow=True)
            normg = S.sbuf("normg", [128, KC])
            S.dma(normg, normg[:], W[pre + "norm_g"], W[pre + "norm_g"].t.rearrange("(kc p) -> p kc", p=128), slow=True)
            wb = [S.sbuf("adaw", [128, KC, 512]) for _ in range(2)]
            wv = W[pre + "ada_w"].t.rearrange("(kc p) j -> p kc j", p=128)
            pA = ps[7]
            for grp in range(12):
                w = wb[grp % 2]
                S.dma(w, w[:], W[pre + "ada_w"], wv[:, :, grp * 512:(grp + 1) * 512])
                for jj in range(4):
                    jc = grp * 4 + jj
                    for kc in range(KC):
                        S.mm(pA, pA[:, jc * 2:jc * 2 + 2], w, w[:, kc, jj * 128:(jj + 1) * 128], condT, condT[:, :, kc],
                             start=(kc == 0), stop=(kc == KC - 1))
            mod = S.sbuf("mod", [128, 48, 2])
            S.tt("dve", mod, mod[:], pA, pA[:, 0:96].rearrange("p (j r) -> p j r", r=2),
                 adab, adab[:, :].unsqueeze(2).to_broadcast([128, 48, 2]), ALU.add)
            S.copy("dve", modB, modB[:], mod, mod[:, 0:16, :])
            S.copy("dve", modG, modG[:], mod, mod[:, 32:48, :])
            S.stt("dve", modA, modA[:], mod, mod[:, 16:32, :], 1.0, normg,
                  normg[:, :].unsqueeze(2).to_broadcast([128, KC, 2]), ALU.add, ALU.mult)
            S.pop()

        def stage_norm(hT, modA, modB):
            S.push()
            ps = mkps()
            xb = [S.sbuf("xb", [128, KC, 256]) for _ in range(2)]
            sq = S.sbuf("sq", [128, KC, 256])
            tmp = S.sbuf("ntmp", [128, KC, 256])
            rstd = S.sbuf("rstd", [128, 256])
            xTv = xT.ap.rearrange("(kc p) t -> p kc t", p=128)
            for bi in range(T // 256):
                c0 = bi * 256
                r = 1 if bi == 0 else 0
                x = xb[bi % 2]
                S.dma(x, x[:], xT.bufs(0, D, c0, c0 + 256), xTv[:, :, c0:c0 + 256])
                S.act(sq, sq[:], x, x[:], AF.Square)
                pb = ps[bi % 2]
                for kc in range(KC):
                    S.mm(pb, pb[:, 0:256], cst, ones, sq, sq[:, kc, :], start=(kc == 0), stop=(kc == KC - 1))
                S.ts("dve", rstd, rstd[:], pb, pb[:, 0:256], 1.0 / D, RMS_EPS, ALU.mult, ALU.add)
                S.act(rstd, rstd[:], rstd, rstd[:], AF.Sqrt)
                S.emit("dve", lambda e: e.reciprocal(out=rstd[:], in_=rstd[:]), [rstd], [rstd])
                for kc in range(KC):
                    S.stt("dve", tmp, tmp[:, kc, :], x, x[:, kc, :], modA[:, kc, r:r + 1], rstd, rstd[:],
                          ALU.mult, ALU.mult, extra_reads=[modA])
                    S.act(hT, hT[:, kc, c0:c0 + 256], tmp, tmp[:, kc, :], AF.Identity, bias=modB[:, kc, r:r + 1],
                          extra_reads=[modB])
            S.pop()

        def proj_chunk(hT, wt, pbanks, evac):
            for tb in range(5):
                c0 = tb * 512
                n = min(512, T - c0)
                pb = pbanks[tb % len(pbanks)]
                for kc in range(KC):
                    S.mm(pb, pb[:, 0:n], wt, wt[:, kc, :], hT, hT[:, kc, c0:c0 + n], start=(kc == 0), stop=(kc == KC - 1))
                evac(tb, c0, n, pb)

        def stage_inproj_l0(hT):
            S.push()
            ps = mkps()
            wb = [S.sbuf("inw", [128, KC, 128], BF16) for _ in range(2)]
            ust = [S.sbuf("ust", [128, T]) for _ in range(2)]
            zst = [S.sbuf("zst", [128, T], BF16) for _ in range(2)]
            wv = W["l0_in_w"].t.rearrange("(kc p) e -> p kc e", p=128)
            for ec in range(2 * EC):
                w = wb[ec % 2]
                S.dma(w, w[:], W["l0_in_w"], wv[:, :, ec * 128:(ec + 1) * 128], q="pool")
                if ec < EC:
                    st = ust[ec % 2]
                    proj_chunk(hT, w, ps[0:4], lambda tb, c0, n, pb: S.copy("act" if tb % 2 else "dve", st, st[:, c0:c0 + n], pb, pb[:, 0:n]))
                    S.dma(uT.bufs(ec * 128, ec * 128 + 128, 0, T), uT.ap[ec * 128:(ec + 1) * 128, :], st, st[:])
                else:
                    st = zst[ec % 2]
                    proj_chunk(hT, w, ps[0:4], lambda tb, c0, n, pb: S.act(st, st[:, c0:c0 + n], pb, pb[:, 0:n], AF.Silu))
                    r0 = (ec - EC) * 128
                    S.dma(szT.bufs(r0, r0 + 128, 0, T), szT.ap[r0:r0 + 128, :], st, st[:])
            S.pop()

        MAGIC = 12582912.0

        def sin_arg(eng, ob, o, ib, i, mul, add, tb, t):
            if isinstance(mul, float):
                S.ts(eng, ob, o, ib, i, mul, add, ALU.mult, ALU.add)
            else:
                S.ts(eng, ob, o, ib, i, mul, None, ALU.mult, extra_reads=[cst])
                if add != 0.0:
                    S.ts(eng, ob, o, ob, o, add, None, ALU.add)
            S.ts(eng, tb, t, ob, o, 1.0 / (2 * PI), MAGIC, ALU.mult, ALU.add)
            S.ts(eng, tb, t, tb, t, -MAGIC, None, ALU.add)
            S.stt(eng, ob, o, tb, t, -2 * PI, ob, o, ALU.mult, ALU.add)

        SIN_SC = 0.999999

        def rr(*gens):
            gens = list(gens)
            while gens:
                for g in list(gens):
                    try:
                        next(g)
                        yield
                    except StopIteration:
                        gens.remove(g)

        def stage_s5(ngb=EC):
            S.push()
            pBu = S.psum("pBu", [128, 1024]); pG = S.psum("pG", [128, 1024]); pT = S.psum("pT", [128, 1024])
            pG2 = S.psum("pG2", [128, 1024])
            pY = pG2
            pX = pG
            Bbar = [S.sbuf("Bbar%d" % c, [128, 64, 64]) for c in range(2)]
            Cn = S.sbuf("Cn", [128, 64, 128], BF16)
            dsk = S.sbuf("dsk", [128, EC])
            S.dma(dsk, dsk[:], W["l0_d"], W["l0_d"].t.rearrange("(gb p) -> p gb", p=128), slow=True)
            tri = [S.sbuf("tri%d" % d, [128, 128], BF16) for d in range(2)]
            S.copy("dve", tri[0], tri[0][:], cst, cst[:, C_TRIF:C_TRIF + 128])
            S.copy("dve", tri[1], tri[1][:], cst, cst[:, C_TRIB:C_TRIB + 128])
            sel = [cst[:, C_SELF:C_SELF + 128], cst[:, C_SELB:C_SELB + 128]]
            stop_k = int(dbg.split(":")[1]) if (dbg and ":" in dbg) else 99

            def ckpt(k):
                if k != stop_k:
                    return False
                o_ = Buf(nc.dram_tensor("dbg_ck", [128, EC], F32, kind="ExternalOutput").ap(), "dbg_ck")
                S.dma(o_, o_[:, :], dsk, dsk[:]); S.finish([o_])
                return True
            if ckpt(1):
                S.pop(); return
            S.push()
            cv = lambda ap3: ap3.rearrange("d g n -> (d g) n").rearrange("(q p) n -> p q n", p=128)
            are = S.sbuf("are", [128, 4, 64]); aim = S.sbuf("aim", [128, 4, 64])
            S.dma(are, are[:], W["l0_a_re"], cv(W["l0_a_re"].t))
            S.dma(aim, aim[:], W["l0_a_im"], cv(W["l0_a_im"].t))
            ls = S.sbuf("ls", [128, 4])
            S.dma(ls, ls[:], W["l0_log_step"], W["l0_log_step"].t.rearrange("d g -> (d g)").rearrange("(q p) -> p q", p=128), slow=True)
            S.act(ls, ls[:], ls, ls[:], AF.Exp)
            dtb = ls[:, :].unsqueeze(2).to_broadcast([128, 4, 64])
            dtar = S.sbuf("dtar", [128, 4, 64]); dtai = S.sbuf("dtai", [128, 4, 64])
            S.tt("dve", dtar, dtar[:], are, are[:], ls, dtb, ALU.mult)
            S.tt("dve", dtai, dtai[:], aim, aim[:], ls, dtb, ALU.mult)
            mag = S.sbuf("mag", [128, 4, 64]); sn = S.sbuf("sn", [128, 4, 64]); cs = S.sbuf("cs", [128, 4, 64])
            S.act(mag, mag[:], dtar, dtar[:], AF.Exp)
            rt = S.sbuf("rt", [128, 4, 64])
            sin_arg("dve", sn, sn[:], dtai, dtai[:], 1.0, 0.0, rt, rt[:])
            S.act(sn, sn[:], sn, sn[:], AF.Sin, scale=SIN_SC)
            sin_arg("dve", cs, cs[:], dtai, dtai[:], 1.0, 0.5 * PI, rt, rt[:])
            S.act(cs, cs[:], cs, cs[:], AF.Sin, scale=SIN_SC)
            nr = S.sbuf("nr", [128, 4, 64]); ni = S.sbuf("ni", [128, 4, 64])
            S.tt("dve", nr, nr[:], mag, mag[:], cs, cs[:], ALU.mult)
            S.ts("dve", nr, nr[:], nr, nr[:], -1.0, None, ALU.add)
            S.tt("dve", ni, ni[:], mag, mag[:], sn, sn[:], ALU.mult)
            den = S.sbuf("den", [128, 4, 64]); t0 = S.sbuf("t0", [128, 4, 64])
            S.tt("dve", den, den[:], are, are[:], are, are[:], ALU.mult)
            S.tt("dve", t0, t0[:], aim, aim[:], aim, aim[:], ALU.mult)
            S.tt("dve", den, den[:], den, den[:], t0, t0[:], ALU.add)
            S.emit("dve", lambda e: e.reciprocal(out=den[:], in_=den[:]), [den], [den])
            fre = S.sbuf("fre", [128, 4, 64]); fim = S.sbuf("fim", [128, 4, 64])
            S.tt("dve", fre, fre[:], nr, nr[:], are, are[:], ALU.mult)
            S.tt("dve", t0, t0[:], ni, ni[:], aim, aim[:], ALU.mult)
            S.tt("dve", fre, fre[:], fre, fre[:], t0, t0[:], ALU.add)
            S.tt("dve", fre, fre[:], fre, fre[:], den, den[:], ALU.mult)
            S.tt("dve", fim, fim[:], ni, ni[:], are, are[:], ALU.mult)
            S.tt("dve", t0, t0[:], nr, nr[:], aim, aim[:], ALU.mult)
            S.tt("dve", fim, fim[:], fim, fim[:], t0, t0[:], ALU.subtract)
            S.tt("dve", fim, fim[:], fim, fim[:], den, den[:], ALU.mult)
            if ckpt(2):
                S.pop(); S.pop(); return
            for k, tsrc in enumerate((dtar, dtai, fre, fim)):
                S.dma(s5c.bufs(k * 512, k * 512 + 512, 0, 64),
                      s5c.ap[k * 512:(k + 1) * 512, :].rearrange("(q p) n -> p q n", p=128), tsrc, tsrc[:])
            if ckpt(3):
                S.pop(); S.pop(); return
            S.pop()
            S.push()
            Bn = S.sbuf("Bn", [128, 512, 16])
            for q in range(16):
                S.dma(Bn, Bn[0:64, q * 32:(q + 1) * 32, :], W["l0_b_re"], W["l0_b_re"].t.rearrange("d g n j -> n (d g) j")[:, q * 32:(q + 1) * 32, :])
                S.dma(Bn, Bn[64:128, q * 32:(q + 1) * 32, :], W["l0_b_im"], W["l0_b_im"].t.rearrange("d g n j -> n (d g) j")[:, q * 32:(q + 1) * 32, :])
            Bre = S.sbuf("Bre", [128, 64, 64]); Bim = S.sbuf("Bim", [128, 64, 64])
            Fre = S.sbuf("Fre", [128, 64, 64]); Fim = S.sbuf("Fim", [128, 64, 64])
            for gp in range(8):
                for dst, k in ((Fre, 2), (Fim, 3)):
                    S.dma(dst, dst[gp * 16:(gp + 1) * 16, :, :], s5c.bufs(k * 512, k * 512 + 512, 0, 64),
                          bass.AP(s5c.ap.tensor, s5c.ap.offset + (k * 512 + gp) * 64, [[0, 16], [512, 64], [1, 64]]))
            if stop_k == 4:
                S.finish([Bn, Fre, Fim])
            if stop_k == 41:
                S.finish([Bn])
            if stop_k == 42:
                S.finish([Fre, Fim])
            if ckpt(4) or ckpt(41) or ckpt(42):
                S.pop(); S.pop(); return
            for q in range(16):
                pb = pY if q % 2 else pX
                for j in range(4):
                    dgb = q * 4 + j
                    S.tr(pb, pb[:, j * 128:(j + 1) * 128], Bn, Bn[:, dgb * 8:(dgb + 1) * 8, :].rearrange('p a b -> p (a b)'), cst, ident)
                pv = pb[:, 0:512].rearrange("p (a c n) -> p a c n", a=4, c=2)
                S.copy("act", Bre, Bre[:, q * 4:q * 4 + 4, :], pb, pv[:, :, 0, :])
                S.copy("dve", Bim, Bim[:, q * 4:q * 4 + 4, :], pb, pv[:, :, 1, :])
            if stop_k == 45:
                S.finish([Bre, Bim])
            if ckpt(45):
                S.pop(); S.pop(); return
            tA = S.sbuf("tA", [128, 64, 64])
            S.tt("dve", Bbar[0], Bbar[0][:], Fre, Fre[:], Bre, Bre[:], ALU.mult)
            S.tt("pool", tA, tA[:], Fim, Fim[:], Bim, Bim[:], ALU.mult)
            S.tt("dve", Bbar[0], Bbar[0][:], Bbar[0], Bbar[0][:], tA, tA[:], ALU.subtract)
            S.tt("dve", Bbar[1], Bbar[1][:], Fre, Fre[:], Bim, Bim[:], ALU.mult)
            S.tt("pool", tA, tA[:], Fim, Fim[:], Bre, Bre[:], ALU.mult)
            S.tt("dve", Bbar[1], Bbar[1][:], Bbar[1], Bbar[1][:], tA, tA[:], ALU.add)
            if ckpt(5):
                S.pop(); S.pop(); return
            S.pop()
            S.push()
            Cnat = S.sbuf("Cnat", [128, 64, 2, 64])
            for q in range(4):
                S.dma(Cnat, Cnat[:, q * 16:(q + 1) * 16, 0, :], W["l0_c_re"], W["l0_c_re"].t.rearrange("d (gb gp) i n -> (gp i) (d gb) n", gp=8)[:, q * 16:(q + 1) * 16, :])
                S.dma(Cnat, Cnat[:, q * 16:(q + 1) * 16, 1, :], W["l0_c_im"], W["l0_c_im"].t.rearrange("d (gb gp) i n -> (gp i) (d gb) n", gp=8)[:, q * 16:(q + 1) * 16, :])
            for q in range(16):
                pb = pY if q % 2 else pX
                for j in range(4):
                    dgb = q * 4 + j
                    S.tr(pb, pb[:, j * 128:(j + 1) * 128], Cnat, Cnat[:, dgb, :, :].rearrange('p a b -> p (a b)'), cst, ident)
                pv = pb[:, 0:512].rearrange("p (a m) -> p a m", a=4)
                S.copy("act", Cn, Cn[0:64, q * 4:q * 4 + 4, :], pb, pv[0:64, :, :])
                S.ts("dve", Cn, Cn[64:128, q * 4:q * 4 + 4, :], pb, pv[64:128, :, :], -1.0, None, ALU.mult)
            S.pop()

            u32 = [S.sbuf("u32", [128, T]) for _ in range(2)]
            ubf = S.sbuf("ubf", [128, T], BF16)
            yacc = S.sbuf("yacc", [128, T])
            vst = [S.sbuf("vst", [128, T], BF16) for _ in range(2)]
            ex = S.sbuf("ex", [128, 512]); ex2 = S.sbuf("ex2", [128, 512])
            v3 = lambda t: t[:, :].rearrange("p (g n) -> p g n", g=8)
            v4 = lambda ap2: ap2.rearrange("p (g c n) -> p g c n", g=8, c=2)
            flat = lambda ap4: ap4.rearrange("p g c n -> p (g c n)")
            DB = []
            for d in range(2):
                B_ = {}
                for nm_ in ("drebc", "dimbc", "magm", "magp", "snT", "csT", "Lmr", "Lmi", "Lpr", "Lpi"):
                    B_[nm_] = S.sbuf(nm_, [128, 512])
                B_["BBD"] = S.sbuf("BBD", [128, 8, 2, 64], BF16)
                B_["CBD"] = S.sbuf("CBD", [128, 8, 128], BF16)
                S.emit("dve", lambda e, c_=B_["CBD"]: e.memset(c_[:], 0.0), [], [B_["CBD"]])
                B_["tt"] = [S.sbuf("s5t%d" % k, [128, 8, 64]) for k in range(4)]
                B_["X"] = [S.sbuf("X", [128, 8, 2, 64], BF16) for _ in range(2)]
                B_["Hc"] = [S.sbuf("Hc", [128, 8, 2, 64]) for _ in range(2)]
                B_["HTc"] = [S.sbuf("HTc", [128, 8, 128], BF16) for _ in range(2)]
                B_["pBu"] = pBu if d == 0 else pT
                B_["pG"] = pG if d == 0 else pG2
                DB.append(B_)

            def s5_dir(gb, d, u):
                B_ = DB[d]
                drebc, dimbc, magm, magp, snT, csT = (B_[k] for k in ("drebc", "dimbc", "magm", "magp", "snT", "csT"))
                Lmr, Lmi, Lpr, Lpi, BBD, CBD, tt_, X, Hc, HTc = (B_[k] for k in ("Lmr", "Lmi", "Lpr", "Lpi", "BBD", "CBD", "tt", "X", "Hc", "HTc"))
                pBu_, pG_ = B_["pBu"], B_["pG"]
                Bu4 = v4(pBu_[:, :]); G4 = v4(pG_[:, :])
                s1 = cst[:, C_S1 + 2 * d:C_S1 + 2 * d + 1]
                ns1 = cst[:, C_S1 + 2 * d + 1:C_S1 + 2 * d + 2]
                row0 = d * 256 + gb * 8
                for dst, k in ((drebc, 0), (dimbc, 1)):
                    S.dma(dst, dst[:], s5c.bufs(k * 512, k * 512 + 512, 0, 64),
                          bass.AP(s5c.ap.tensor, s5c.ap.offset + (k * 512 + row0) * 64, [[0, 128], [1, 512]]))
                S.act(magm, magm[:], drebc, drebc[:], AF.Exp, scale=ns1, extra_reads=[cst])
                S.act(magp, magp[:], drebc, drebc[:], AF.Exp, scale=s1, extra_reads=[cst])
                yield
                sin_arg("dve", snT, snT[:], dimbc, dimbc[:], s1, 0.0, Lmr, Lmr[:])
                S.act(snT, snT[:], snT, snT[:], AF.Sin, scale=SIN_SC)
                yield
                sin_arg("dve", csT, csT[:], dimbc, dimbc[:], s1, 0.5 * PI, Lmr, Lmr[:])
                S.act(csT, csT[:], csT, csT[:], AF.Sin, scale=SIN_SC)
                yield
                S.tt("dve", Lmr, Lmr[:], magm, magm[:], csT, csT[:], ALU.mult)
                S.stt("dve", Lmi, Lmi[:], magm, magm[:], -1.0, snT, snT[:], ALU.mult, ALU.mult)
                S.tt("pool", Lpr, Lpr[:], magp, magp[:], csT, csT[:], ALU.mult)
                S.tt("pool", Lpi, Lpi[:], magp, magp[:], snT, snT[:], ALU.mult)
                yield
                for c in range(2):
                    for g2 in range(8):
                        S.ts("dve" if g2 % 2 else "pool", BBD, BBD[:, g2, c, :], Bbar[c], Bbar[c][:, d * 32 + gb, :],
                             cst[:, C_MBD + g2:C_MBD + g2 + 1], None, ALU.mult, extra_reads=[cst])
                    yield
                cfull = CBD[:]
                S.copy("pool", CBD, bass.AP(cfull.tensor, cfull.offset, [[cfull.ap[0][0], 128], [144, 8], [1, 16]]),
                       Cn, Cn[:, d * 32 + gb, :].rearrange("p (g i) -> p g i", g=8))
                order = list(range(NCH)) if d == 0 else [1, 0] + list(range(NCH - 1, 1, -1))

                def stA(ci, ch):
                    c0 = ch * 128
                    Xc = X[ci % 2]
                    for h in range(2):
                        S.mm(pBu_, pBu_[:, h * 512:(h + 1) * 512], ubf, ubf[:, c0:c0 + 128], BBD, flat(BBD[:, 4 * h:4 * h + 4, :, :]))
                    yield
                    S.tt("dve", tt_[0], tt_[0][:], Lmr, v3(Lmr), pBu_, Bu4[:, :, 0, :], ALU.mult)
                    S.tt("dve", tt_[1], tt_[1][:], Lmi, v3(Lmi), pBu_, Bu4[:, :, 1, :], ALU.mult)
                    S.tt("pool", Xc, Xc[:, :, 0, :], tt_[0], tt_[0][:], tt_[1], tt_[1][:], ALU.subtract)
                    yield
                    S.tt("dve", tt_[0], tt_[0][:], Lmr, v3(Lmr), pBu_, Bu4[:, :, 1, :], ALU.mult)
                    S.tt("dve", tt_[1], tt_[1][:], Lmi, v3(Lmi), pBu_, Bu4[:, :, 0, :], ALU.mult)
                    S.tt("pool", Xc, Xc[:, :, 1, :], tt_[0], tt_[0][:], tt_[1], tt_[1][:], ALU.add)
                    yield

                def stB(ci, ch):
                    c0 = ch * 128
                    Xc = X[ci % 2]
                    hp, hc = Hc[(ci + 1) % 2], Hc[ci % 2]
                    first = ci == 0
                    for h in range(2):
                        S.mm(pG_, pG_[:, h * 512:(h + 1) * 512], tri[d], tri[d][:], Xc, flat(Xc[:, 4 * h:4 * h + 4, :, :]), start=True, stop=first)
                        if not first:
                            S.mm(pG_, pG_[:, h * 512:(h + 1) * 512], cst, sel[d], hp, flat(hp[:, 4 * h:4 * h + 4, :, :]), start=False, stop=True)
                    yield
                    S.tt("dve", tt_[2], tt_[2][:], Lpr, v3(Lpr), pG_, G4[:, :, 0, :], ALU.mult)
                    S.tt("dve", tt_[3], tt_[3][:], Lpi, v3(Lpi), pG_, G4[:, :, 1, :], ALU.mult)
                    S.tt("pool", hc, hc[:, :, 0, :], tt_[2], tt_[2][:], tt_[3], tt_[3][:], ALU.subtract)
                    yield
                    S.tt("dve", tt_[2], tt_[2][:], Lpr, v3(Lpr), pG_, G4[:, :, 1, :], ALU.mult)
                    S.tt("dve", tt_[3], tt_[3][:], Lpi, v3(Lpi), pG_, G4[:, :, 0, :], ALU.mult)
                    S.tt("pool", hc, hc[:, :, 1, :], tt_[2], tt_[2][:], tt_[3], tt_[3][:], ALU.add)
                    yield
                    ht = HTc[ci % 2]
                    for gp in range(8):
                        S.tr(pG_, pG_[:, gp * 128:(gp + 1) * 128], hc, hc[:, gp, :, :].rearrange("p c n -> p (c n)"), cst, ident)
                    yield
                    S.copy("act", ht, ht[:, :, :], pG_, pG_[:, :].rearrange("p (a b) -> p a b", a=8))
                    yield
                    for gp in range(8):
                        S.mm(pG_, pG_[:, 0:128], CBD, CBD[:, gp, :], ht, ht[:, gp, :], start=(gp == 0), stop=(gp == 7))
                    yield
                    S.tt("dve", yacc, yacc[:, c0:c0 + 128], yacc, yacc[:, c0:c0 + 128], pG_, pG_[:, 0:128], ALU.add)
                    yield

                yield from stA(0, order[0])
                for ci, ch in enumerate(order):
                    gens = [stB(ci, ch)]
                    if ci + 1 < NCH:
                        gens.append(stA(ci + 1, order[ci + 1]))
                    yield from rr(*gens)

            for gb in range(ngb):
                u = u32[gb % 2]
                S.dma(u, u[:], uT.bufs(gb * 128, gb * 128 + 128, 0, T), uT.ap[gb * 128:(gb + 1) * 128, :])
                S.copy("pool", ubf, ubf[:], u, u[:])
                S.emit("pool", lambda e: e.memset(yacc[:], 0.0), [], [yacc])
                for _ in rr(s5_dir(gb, 0, u), s5_dir(gb, 1, u)):
                    pass
                v = vst[gb % 2]
                for tb in range(5):
                    c0 = tb * 512
                    n = min(512, T - c0)
                    S.stt("dve", ex, ex[:, 0:n], u, u[:, c0:c0 + n], dsk[:, gb:gb + 1], yacc, yacc[:, c0:c0 + n],
                          ALU.mult, ALU.add, extra_reads=[dsk])
                    S.act(ex2, ex2[:, 0:n], ex, ex[:, 0:n], AF.Square)
                    S.ts("dve", ex2, ex2[:, 0:n], ex2, ex2[:, 0:n], 0.044715, 1.0, ALU.mult, ALU.add)
                    S.tt("pool", ex2, ex2[:, 0:n], ex2, ex2[:, 0:n], ex, ex[:, 0:n], ALU.mult)
                    S.act(ex2, ex2[:, 0:n], ex2, ex2[:, 0:n], AF.Sigmoid, scale=2.0 * math.sqrt(2.0 / PI))
                    S.tt("pool", v, v[:, c0:c0 + n], ex, ex[:, 0:n], ex2, ex2[:, 0:n], ALU.mult)
                S.dma(vT.bufs(gb * 128, gb * 128 + 128, 0, T), vT.ap[gb * 128:(gb + 1) * 128, :], v, v[:])
            S.pop()

        def stage_glu_out_l0(modG):
            S.push()
            ps = mkps()
            vb = S.sbuf("vb", [128, EC, 512], BF16)
            zb = S.sbuf("zb", [128, EC, 512], BF16)
            gb_ = S.sbuf("gb", [128, EC, 512], BF16)
            xb = S.sbuf("xbo", [128, KC, 512])
            sg = S.sbuf("sg", [128, 512])
            gw = [S.sbuf("gw", [128, EC, 128], BF16) for _ in range(2)]
            glub = S.sbuf("glub", [128, EC])
            S.dma(glub, glub[:], W["l0_glu_b"], W["l0_glu_b"].t.rearrange("(e p) -> p e", p=128), slow=True)
            gv = W["l0_glu_w"].t.rearrange("(ei p) e -> p ei e", p=128)
            ov = W["l0_out_w"].t.rearrange("(ei p) e -> p ei e", p=128)
            xTv = xT.ap.rearrange("(kc p) t -> p kc t", p=128)
            vTv = vT.ap.rearrange("(e p) t -> p e t", p=128)
            zTv = szT.ap.rearrange("(e p) t -> p e t", p=128)
            wi = 0
            for (c0, n, r) in lat_blocks():
                S.dma(vb, vb[:, :, 0:n], vT.bufs(0, E, 0, T), vTv[:, :, c0:c0 + n])
                S.dma(zb, zb[:, :, 0:n], szT.bufs(0, E, 0, T), zTv[:, :, c0:c0 + n])
                S.dma(xb, xb[:, :, 0:n], xT.bufs(0, D, c0, c0 + n), xTv[:, :, c0:c0 + n])
                for eo in range(EC):
                    w = gw[wi % 2]; wi += 1
                    S.dma(w, w[:], W["l0_glu_w"], gv[:, :, eo * 128:(eo + 1) * 128], q="pool")
                    pb = ps[eo % 4]
                    for ei in range(EC):
                        S.mm(pb, pb[:, 0:n], w, w[:, ei, :], vb, vb[:, ei, 0:n], start=(ei == 0), stop=(ei == EC - 1))
                    S.act(sg, sg[:, 0:n], pb, pb[:, 0:n], AF.Sigmoid, bias=glub[:, eo:eo + 1], extra_reads=[glub])
                    S.tt("dve", sg, sg[:, 0:n], sg, sg[:, 0:n], vb, vb[:, eo, 0:n], ALU.mult)
                    S.tt("dve", gb_, gb_[:, eo, 0:n], sg, sg[:, 0:n], zb, zb[:, eo, 0:n], ALU.mult)
                for dc in range(KC):
                    w = gw[wi % 2]; wi += 1
                    S.dma(w, w[:], W["l0_out_w"], ov[:, :, dc * 128:(dc + 1) * 128], q="pool")
                    pb = ps[4 + dc % 4]
                    for ei in range(EC):
                        S.mm(pb, pb[:, 0:n], w, w[:, ei, :], gb_, gb_[:, ei, 0:n], start=(ei == 0), stop=(ei == EC - 1))
                    S.stt("dve", xb, xb[:, dc, 0:n], pb, pb[:, 0:n], modG[:, dc, r:r + 1], xb, xb[:, dc, 0:n],
                          ALU.mult, ALU.add, extra_reads=[modG])
                S.dma(xT.bufs(0, D, c0, c0 + n), xTv[:, :, c0:c0 + n], xb, xb[:, :, 0:n])
            S.pop()

        cst2 = S.sbuf("cst2", [128, C2_W])
        S.dma(cst2, cst2[:], cst2_in, cst2_in[:, :])
        bones = cst2[:, C2_BONES:C2_BONES + 128]
        rT = DramT(S, "rT", [E, T], F32, 128, T)
        kT = DramT(S, "kT", [E, T], F32, 128, T)
        v1T = DramT(S, "v1T", [E, T], F32, 128, T)
        vtm = DramT(S, "vtm", [T, E], F32, T, 128)
        lwT = [DramT(S, "lwT%d" % d, [E, T], F32, 128, T) for d in range(2)]
        arT = [DramT(S, "arT%d" % d, [E, T], F32, 128, T) for d in range(2)]
        gT = vT

        def colvec(name, src_ap_1d, n):
            t_ = S.sbuf(name, [128, n])
            return t_

        def stage_l1_proj(hT):
            S.push()
            ps = mkps()
            xi = S.sbuf("xi", [128, KC, T], BF16)
            mu = S.sbuf("mu", [128, 6, KC])
            for i in range(6):
                S.dma(mu, mu[:, i, :], W["l1_mu"], W["l1_mu"].t[i, :].rearrange("(kc p) -> p kc", p=128), slow=True)
            omu = S.sbuf("omu", [128, 6, KC])
            S.ts("dve", omu, omu[:], mu, mu[:], -1.0, 1.0, ALU.mult, ALU.add)
            wb = [S.sbuf("l1w", [128, KC, 128], BF16) for _ in range(2)]
            st32 = [S.sbuf("st32", [128, T])] * 2
            stbf = [S.sbuf("stbf", [128, T], BF16)] * 2

            def build_xi(i):
                for kc in range(KC):
                    eng = "dve" if kc % 2 else "pool"
                    S.ts(eng, xi, xi[:, kc, :], hT, hT[:, kc, :], omu[:, i, kc:kc + 1], None, ALU.mult, extra_reads=[omu])
                    m = mu[:, i, kc:kc + 1]
                    xl = xi[:, kc, TC:T].rearrange("p (r c) -> p r c", c=64)
                    hl = hT[:, kc, TC:T].rearrange("p (r c) -> p r c", c=64)
                    q = kc // 4
                    if q == 0:
                        o_, s_ = xl[:, :, 1:64], hl[:, :, 0:63]
                    elif q == 1:
                        o_, s_ = xl[:, :, 0:63], hl[:, :, 1:64]
                    elif q == 2:
                        o_, s_ = xl[:, 1:32, :], hl[:, 0:31, :]
                    else:
                        o_, s_ = xl[:, 0:31, :], hl[:, 1:32, :]
                    S.stt("dve", xi, o_, hT, s_, m, xi, o_, ALU.mult, ALU.add, extra_reads=[mu])
                    if kc < 8:
                        o_, s_ = xi[:, kc, 1:TC], hT[:, kc, 0:TC - 1]
                    else:
                        o_, s_ = xi[:, kc, 0:TC - 1], hT[:, kc, 1:TC]
                    S.stt("dve", xi, o_, hT, s_, m, xi, o_, ALU.mult, ALU.add, extra_reads=[mu])

            cnt = [0]

            def fm_proj(widx, dst, silu=False):
                wv = W["l1_in_w"].t[widx].rearrange("(kc p) e -> p kc e", p=128)
                for ec in range(EC):
                    w = wb[cnt[0] % 2]
                    S.dma(w, w[:], W["l1_in_w"], wv[:, :, ec * 128:(ec + 1) * 128], q="pool")
                    if silu:
                        st = stbf[cnt[0] % 2]
                        proj_chunk(xi, w, ps[0:4], lambda tb, c0, n, pb: S.act(st, st[:, c0:c0 + n], pb, pb[:, 0:n], AF.Silu))
                    else:
                        st = st32[cnt[0] % 2]
                        proj_chunk(xi, w, ps[0:4], lambda tb, c0, n, pb: S.copy("act" if tb % 2 else "dve", st, st[:, c0:c0 + n], pb, pb[:, 0:n]))
                    S.dma(dst.bufs(ec * 128, ec * 128 + 128, 0, T), dst.ap[ec * 128:(ec + 1) * 128, :], st, st[:])
                    cnt[0] += 1

            build_xi(0); fm_proj(0, rT)
            build_xi(1); fm_proj(1, kT)
            build_xi(2); fm_proj(2, v1T)
            S.push()
            wt = [S.sbuf("wvt", [128, KC, 512], BF16)] * 2
            vs = [S.sbuf("vs", [128, 512]) for _ in range(2)]
            wv = W["l1_in_w"].t[2].rearrange("(kc p) e -> p kc e", p=128)
            for cb in range(8):
                w = wt[cb % 2]
                S.dma(w, w[:], W["l1_in_w"], wv[:, :, cb * 512:(cb + 1) * 512], q="pool")
                for tt in range(NCH):
                    pb = ps[4 + tt % 4]
                    for kc in range(KC):
                        S.mm(pb, pb[:, :], xi, xi[:, kc, tt * 128:(tt + 1) * 128], w, w[:, kc, :], start=(kc == 0), stop=(kc == KC - 1))
                    o = vs[tt % 2]
                    S.copy("act" if tt % 2 else "dve", o, o[:], pb, pb[:, :])
                    S.dma(vtm.bufs(0, T, cb * 512, cb * 512 + 512), vtm.ap[tt * 128:(tt + 1) * 128, cb * 512:(cb + 1) * 512], o, o[:])
            S.pop()
            build_xi(3); fm_proj(3, szT, silu=True)
            S.push()
            l1 = S.sbuf("lora1", [128, KC, 96], BF16)
            l2 = S.sbuf("lora2", [96, E], BF16)
            lmid = S.sbuf("lmid", [96, T], BF16)
            b0 = S.sbuf("lb0", [128, EC])
            for (i, n1, n2, n0, dst, is_w) in ((4, "l1_w1", "l1_w2", "l1_w0", lwT, True), (5, "l1_a1", "l1_a2", "l1_a0", arT, False)):
                build_xi(i)
                for d in range(2):
                    S.dma(l1, l1[:], W[n1], W[n1].t[d].rearrange("(kc p) r -> p kc r", p=128), q="pool")
                    S.dma(l2, l2[:], W[n2], W[n2].t[d], q="pool")
                    S.dma(b0, b0[:], W[n0], W[n0].t[d, :].rearrange("(e p) -> p e", p=128), slow=True)
                    for tb in range(5):
                        c0 = tb * 512
                        n = min(512, T - c0)
                        pb = ps[tb % 4]
                        for kc in range(KC):
                            S.mm(pb, pb[0:96, 0:n], l1, l1[:, kc, :], xi, xi[:, kc, c0:c0 + n], start=(kc == 0), stop=(kc == KC - 1))
                        S.act(lmid, lmid[:, c0:c0 + n], pb, pb[0:96, 0:n], AF.Tanh if is_w else AF.Copy)
                    for ec in range(EC):
                        st = st32[cnt[0] % 2]
                        cnt[0] += 1
                        for tb in range(5):
                            c0 = tb * 512
                            n = min(512, T - c0)
                            pb = ps[4 + tb % 4]
                            S.mm(pb, pb[:, 0:n], l2, l2[:, ec * 128:(ec + 1) * 128], lmid, lmid[:, c0:c0 + n])
                            S.act(st, st[:, c0:c0 + n], pb, pb[:, 0:n], AF.Sigmoid, bias=b0[:, ec:ec + 1], extra_reads=[b0])
                        if is_w:
                            S.ts("dve", st, st[:], st, st[:], -math.exp(-0.5), None, ALU.mult)
                        S.dma(dst[d].bufs(ec * 128, ec * 128 + 128, 0, T), dst[d].ap[ec * 128:(ec + 1) * 128, :], st, st[:])
            S.pop()
            S.pop()

        def stage_l1_scan(nec=EC):
            S.push()
            L = 128
            def vec(nm, src):
                t_ = S.sbuf(nm, [128, EC])
                S.dma(t_, t_[:], W[src], W[src].t.rearrange("(e p) -> p e", p=128), slow=True)
                return t_
            kkv = vec("kkv", "l1_k_k"); kav = vec("kav", "l1_k_a"); lnw = vec("lnw", "l1_ln_w"); lnb = vec("lnb", "l1_ln_b")
            rkv = S.sbuf("rkv", [128, EC])
            S.dma(rkv, rkv[:], W["l1_r_k"], W["l1_r_k"].t.rearrange("(e h) k -> (h k) e", h=2), slow=True)
            KKN = S.sbuf("KKN", [128, T]); BON = S.sbuf("BON", [128, T]); YACC = S.sbuf("YACC", [128, T])
            RT = S.sbuf("RT", [128, T]); KT = S.sbuf("KT", [128, T]); AR = S.sbuf("AR", [128, T]); LW = S.sbuf("LW", [128, T])
            CIN = S.sbuf("CIN", [128, T]); TMP = S.sbuf("TMP", [128, T]); TMP2 = S.sbuf("TMP2", [128, T]); TMP3 = S.sbuf("TMP3", [128, T]); AT = S.sbuf("AT", [128, T]); BT = S.sbuf("BT", [128, T])
            GL = S.sbuf("GL", [128, NCH])
            Vp = S.sbuf("Vp", [128, NCH, 128])
            VpA = S.sbuf("VpA", [128, NCH, 128]); VpB = S.sbuf("VpB", [128, NCH, 128])
            S.emit("pool", lambda e: e.memset(VpA[:], 0.0), [], [VpA])
            S.emit("pool", lambda e: e.memset(VpB[:], 0.0), [], [VpB])
            szb = S.sbuf("szb", [128, T], BF16)
            gst = S.sbuf("gst", [128, T], BF16)
            Sbd = [S.sbuf("Sbd", [128, 128]) for _ in range(2)]
            for s_ in Sbd:
                S.emit("dve", lambda e, s_=s_: e.memset(s_[:], 0.0), [], [s_])
            M4 = [[S.sbuf("M4", [128, 4, 128]) for _ in range(3)] for _ in range(2)]
            XN = [[[S.sbuf("XN", [128, 2, 128]) for _ in range(2)] for _ in range(2)] for _ in range(2)]
            Wm = [[S.sbuf("Wm", [128, 128]) for _ in range(6)] for _ in range(2)]
            RHS = S.sbuf("RHS", [128, 128]); Us = S.sbuf("Us", [128, 128])
            UpA = S.sbuf("UpA", [128, 128]); UpB = S.sbuf("UpB", [128, 128])
            S.emit("dve", lambda e: e.memset(UpA[:], 0.0), [], [UpA])
            S.emit("dve", lambda e: e.memset(UpB[:], 0.0), [], [UpB])
            bk = S.sbuf("bk", [128, 2, 128]); bkT = S.sbuf("bkT", [128, 2, 128])
            pM = [S.psum("pM%d" % h, [128, 512]) for h in range(2)]
            pC = [[S.psum("pC%d%d" % (h, q), [128, 512]) for q in range(2)] for h in range(2)]
            pR = S.psum("pR", [128, 512]); pS = S.psum("pS", [128, 512])
            pYy = pR
            v3 = lambda t_: t_[:, :].rearrange("p (c l) -> p c l", l=L)

            for ec in range(nec):
                rows = (ec * 128, ec * 128 + 128)
                S.dma(KT, KT[:], kT.bufs(*rows, 0, T), kT.ap[rows[0]:rows[1], :])
                S.dma(RT, RT[:], rT.bufs(*rows, 0, T), rT.ap[rows[0]:rows[1], :])
                S.dma(AT, AT[:], v1T.bufs(*rows, 0, T), v1T.ap[rows[0]:rows[1], :])
                S.dma(AR, AR[:], arT[0].bufs(*rows, 0, T), arT[0].ap[rows[0]:rows[1], :])
                S.dma(LW, LW[:], arT[1].bufs(*rows, 0, T), arT[1].ap[rows[0]:rows[1], :])
                S.dma(Vp, Vp[:], vtm.bufs(0, T, rows[0], rows[1]), vtm.ap.rearrange("(c p) e -> p c e", p=128)[:, :, rows[0]:rows[1]])
                S.dma(szb, szb[:], szT.bufs(*rows, 0, T), szT.ap[rows[0]:rows[1], :])
                S.copy("pool", VpA, VpA[:, :, 0:64], Vp, Vp[:, :, 0:64])
                S.copy("pool", VpB, VpB[:, :, 64:128], Vp, Vp[:, :, 64:128])
                S.ts("dve", KKN, KKN[:], KT, KT[:], kkv[:, ec:ec + 1], None, ALU.mult, extra_reads=[kkv])
                S.act(TMP, TMP[:], KKN, KKN[:], AF.Square)
                for tb in range(5):
                    c0 = tb * 512; n = min(512, T - c0)
                    S.mm(pR, pR[:, 0:n], cst2, bones, TMP, TMP[:, c0:c0 + n])
                    S.act(CIN, CIN[:, c0:c0 + n], pR, pR[:, 0:n], AF.Sqrt)
                S.ts("dve", CIN, CIN[:], CIN, CIN[:], 1e-12, None, ALU.max)
                S.emit("dve", lambda e: e.reciprocal(out=CIN[:], in_=CIN[:]), [CIN], [CIN])
                S.tt("dve", KKN, KKN[:], KKN, KKN[:], CIN, CIN[:], ALU.mult)
                S.tt("pool", TMP, TMP[:], AR, AR[:], LW, LW[:], ALU.add)
                S.ts("dve", TMP, TMP[:], TMP, TMP[:], -2.0, None, ALU.add)
                S.ts("dve", TMP, TMP[:], TMP, TMP[:], kav[:, ec:ec + 1], None, ALU.mult, extra_reads=[kav])
                S.ts("dve", TMP, TMP[:], TMP, TMP[:], 2.0, None, ALU.add)
                S.tt("pool", TMP, TMP[:], TMP, TMP[:], KT, KT[:], ALU.mult)
                S.stt("dve", TMP, TMP[:], TMP, TMP[:], rkv[:, ec:ec + 1], RT, RT[:], ALU.mult, ALU.mult, extra_reads=[rkv])
                for tb in range(5):
                    c0 = tb * 512; n = min(512, T - c0)
                    S.mm(pR, pR[:, 0:n], cst2, bones, TMP, TMP[:, c0:c0 + n])
                    S.tt("dve", BON, BON[:, c0:c0 + n], AT, AT[:, c0:c0 + n], pR, pR[:, 0:n], ALU.mult)

                for d in range(2):
                    if not (d == 0):
                        S.dma(KT, KT[:], kT.bufs(*rows, 0, T), kT.ap[rows[0]:rows[1], :])
                        S.dma(RT, RT[:], rT.bufs(*rows, 0, T), rT.ap[rows[0]:rows[1], :])
                    if d == 1:
                        S.dma(AR, AR[:], arT[1].bufs(*rows, 0, T), arT[1].ap[rows[0]:rows[1], :])
                    S.dma(LW, LW[:], lwT[d].bufs(*rows, 0, T), lwT[d].ap[rows[0]:rows[1], :])
                    for ch_ in range(NCH):
                        S.emit("dve", lambda e, ch_=ch_: e.tensor_tensor_scan(out=CIN[:, ch_ * L:(ch_ + 1) * L], data0=ones,
                                                                             data1=LW[:, ch_ * L:(ch_ + 1) * L], initial=0.0,
                                                                             op0=ALU.mult, op1=ALU.add), [LW, cst], [CIN])
                    if d == 1:
                        S.tt("dve", TMP, TMP[:], LW, LW[:], CIN, CIN[:], ALU.subtract)
                        S.tt("dve", CIN, v3(CIN), TMP, v3(TMP), CIN, v3(CIN)[:, :, L - 1:L].to_broadcast([128, NCH, L]), ALU.add)
                    S.tt("pool", BT, BT[:], KKN, KKN[:], AR, AR[:], ALU.mult)
                    S.ts("dve", TMP, TMP[:], AR, AR[:], -1.0, None, ALU.add)
                    S.ts("dve", TMP, TMP[:], TMP, TMP[:], kav[:, ec:ec + 1], None, ALU.mult, extra_reads=[kav])
                    S.ts("dve", TMP, TMP[:], TMP, TMP[:], 1.0, None, ALU.add)
                    S.tt("pool", KT, KT[:], KT, KT[:], TMP, TMP[:], ALU.mult)
                    S.act(TMP, TMP[:], CIN, CIN[:], AF.Exp, scale=-1.0)
                    S.act(TMP2, TMP2[:], CIN, CIN[:], AF.Exp)
                    S.tt("pool", TMP3, TMP3[:], CIN, CIN[:], LW, LW[:], ALU.subtract)
                    S.act(TMP3, TMP3[:], TMP3, TMP3[:], AF.Exp)
                    S.tt("dve", BT, BT[:], BT, BT[:], TMP, TMP[:], ALU.mult)
                    S.tt("pool", KT, KT[:], KT, KT[:], TMP, TMP[:], ALU.mult)
                    S.tt("dve", RT, RT[:], RT, RT[:], TMP2, TMP2[:], ALU.mult)
                    S.stt("dve", AT, AT[:], KKN, KKN[:], -1.0, TMP3, TMP3[:], ALU.mult, ALU.mult)
                    endc = (L - 1) if d == 0 else 0
                    S.act(GL, GL[:], CIN, v3(CIN)[:, :, endc], AF.Exp)
                    cur = 0
                    for h in range(2):
                        pb_ = 64 * h
                        S.emit("dve", lambda e, h=h, pb_=pb_: e.memset(Sbd[0][pb_:pb_ + 64, pb_:pb_ + 64], 0.0), [], [Sbd[0]])
                    order = list(range(NCH)) if d == 0 else [1, 0] + list(range(NCH - 1, 1, -1))
                    m4c = cst2[:, C2_M4 + d * 512:C2_M4 + (d + 1) * 512].rearrange("p (a t) -> p a t", a=4)
                    mNc = cst2[:, C2_MN + d * 128:C2_MN + (d + 1) * 128]
                    Wfin = {}

                    def pre_head(h, ci, ch):
                        cols = slice(ch * L, ch * L + L)
                        hp = slice(64 * h, 64 * h + 64)
                        m4 = M4[h][ci % 3]
                        pc = pC[h][ci % 2]
                        S.mm(pM[h], pM[h][:, 0:128], BT, BT[hp, cols], AT, AT[hp, cols])
                        S.mm(pM[h], pM[h][:, 128:256], BT, BT[hp, cols], RT, RT[hp, cols])
                        S.mm(pM[h], pM[h][:, 256:384], KT, KT[hp, cols], AT, AT[hp, cols])
                        S.mm(pM[h], pM[h][:, 384:512], KT, KT[hp, cols], RT, RT[hp, cols])
                        S.mm(pc, pc[:, 384:512], AT, AT[hp, cols], BT, BT[hp, cols])
                        S.tt("dve", m4, m4[:], cst2, m4c, pM[h], pM[h][:, :].rearrange("p (a t) -> p a t", a=4), ALU.mult)
                        cur_ = XN[h][ci % 2][0]
                        S.tt("dve", cur_, cur_[:, 0, :], cst2, mNc, pc, pc[:, 384:512], ALU.mult)
                        yield
                        S.copy("act", cur_, cur_[:, 1, :], m4, m4[:, 0, :])
                        Wc = Wm[h][2 * (ci % 3)]
                        S.tt("pool", Wc, Wc[:], m4, m4[:, 0, :], cst, ident, ALU.add)
                        yield
                        pp, wi_ = 0, 0
                        for j in range(6):
                            nxt_ = XN[h][ci % 2][1 - pp]
                            S.mm(pc, pc[:, 0:128], cur_, cur_[:, 1, :], cur_, cur_[:, 0, :])
                            if j < 5:
                                S.mm(pc, pc[:, 128:256], cur_, cur_[:, 0, :], cur_, cur_[:, 1, :])
                            yield
                            if j < 5:
                                S.copy("act", nxt_, nxt_[:, :, :], pc, pc[:, 0:256].rearrange("p (a b) -> p a b", a=2))
                            else:
                                S.copy("act", nxt_, nxt_[:, 0, :], pc, pc[:, 0:128])
                            yield
                            S.mm(pc, pc[:, 256:384], nxt_, nxt_[:, 0, :], Wc, Wc[:])
                            yield
                            Wn = Wm[h][2 * (ci % 3) + 1 - wi_]
                            S.tt("dve", Wn, Wn[:], Wc, Wc[:], pc, pc[:, 256:384], ALU.add)
                            yield
                            cur_, Wc = nxt_, Wn
                            pp, wi_ = 1 - pp, 1 - wi_
                        Wfin[(h, ci)] = Wc

                    def seq(ci, ch):
                        cols = slice(ch * L, ch * L + L)
                        S0, S1 = Sbd[ci % 2], Sbd[(ci + 1) % 2]
                        for h in range(2):
                            hp = slice(64 * h, 64 * h + 64)
                            m4 = M4[h][ci % 3]
                            S.mm(pR, pR[:, 64 * h:64 * h + 64], AT, AT[hp, cols], S0, S0[hp, 64 * h:64 * h + 64], start=True, stop=False)
                            S.mm(pR, pR[:, 64 * h:64 * h + 64], m4, m4[:, 2, :], Vp, Vp[:, ch, 64 * h:64 * h + 64], start=False, stop=True)
                        yield
                        S.copy("act", RHS, RHS[:], pR, pR[:, 0:128])
                        yield
                        for h in range(2):
                            Wh = Wfin[(h, ci)]
                            S.mm(pR, pR[:, 128 + 64 * h:128 + 64 * h + 64], Wh, Wh[:], RHS, RHS[:, 64 * h:64 * h + 64])
                        yield
                        S.copy("act", Us, Us[:], pR, pR[:, 128:256])
                        S.copy("dve", UpA, UpA[:, 0:64], pR, pR[:, 128:192])
                        S.copy("dve", UpB, UpB[:, 64:128], pR, pR[:, 192:256])
                        yield
                        S.mm(pYy, pYy[:, 256:384], S0, S0[:], RT, RT[:, cols], start=True, stop=False)
                        S.mm(pYy, pYy[:, 256:384], VpA, VpA[:, ch, :], M4[0][ci % 3], M4[0][ci % 3][:, 3, :], start=False, stop=False)
                        S.mm(pYy, pYy[:, 256:384], VpB, VpB[:, ch, :], M4[1][ci % 3], M4[1][ci % 3][:, 3, :], start=False, stop=False)
                        yield
                        S.mm(pYy, pYy[:, 256:384], UpA, UpA[:], M4[0][ci % 3], M4[0][ci % 3][:, 1, :], start=False, stop=False)
                        S.mm(pYy, pYy[:, 256:384], UpB, UpB[:], M4[1][ci % 3], M4[1][ci % 3][:, 1, :], start=False, stop=True)
                        yield
                        if d == 0:
                            S.copy("act", YACC, YACC[:, cols], pYy, pYy[:, 256:384])
                        else:
                            S.tt("dve", YACC, YACC[:, cols], YACC, YACC[:, cols], pYy, pYy[:, 256:384], ALU.add)
                        yield
                        S.ts("pool", bk, bk[:, 0, :], BT, BT[:, cols], GL[:, ch:ch + 1], None, ALU.mult, extra_reads=[GL])
                        S.ts("pool", bk, bk[:, 1, :], KT, KT[:, cols], GL[:, ch:ch + 1], None, ALU.mult, extra_reads=[GL])
                        yield
                        S.tr(pS, pS[:, 0:128], bk, bk[:, 0, :], cst, ident)
                        S.tr(pS, pS[:, 128:256], bk, bk[:, 1, :], cst, ident)
                        yield
                        S.copy("act", bkT, bkT[:], pS, pS[:, 0:256].rearrange("p (a b) -> p a b", a=2))
                        yield
                        S.mm(pS, pS[:, 256:384], bkT, bkT[:, 0, :], Us, Us[:], start=True, stop=False)
                        S.mm(pS, pS[:, 256:384], bkT, bkT[:, 1, :], Vp, Vp[:, ch, :], start=False, stop=True)
                        yield
                        for h in range(2):
                            hp = slice(64 * h, 64 * h + 64)
                            S.stt("dve", S1, S1[hp, 64 * h:64 * h + 64], S0, S0[hp, 64 * h:64 * h + 64], GL[hp, ch:ch + 1],
                                  pS, pS[hp, 256 + 64 * h:256 + 64 * h + 64], ALU.mult, ALU.add, extra_reads=[GL])
                        yield

                    def mkpre(ci_):
                        return [pre_head(0, ci_, order[ci_]), pre_head(1, ci_, order[ci_])]

                    def drain(must, others):
                        live = list(must) + list(others)
                        must = set(id(g) for g in must)
                        while must:
                            for g in list(live):
                                try:
                                    next(g)
                                except StopIteration:
                                    live.remove(g)
                                    must.discard(id(g))
                            if not live:
                                break
                        return [g for g in live]

                    nord = len(order)
                    ahead = drain(mkpre(0), mkpre(1) if nord > 1 else [])
                    for ci, ch in enumerate(order):
                        newp = mkpre(ci + 2) if ci + 2 < nord else []
                        ahead = drain([seq(ci, ch)] + ahead, newp)
                for tb in range(5):
                    c0 = tb * 512; n = min(512, T - c0)
                    S.mm(pR, pR[:, 0:n], cst2, bones, YACC, YACC[:, c0:c0 + n])
                    S.stt("dve", TMP, TMP[:, c0:c0 + n], pR, pR[:, 0:n], -1.0 / 64, YACC, YACC[:, c0:c0 + n], ALU.mult, ALU.add)
                    S.act(CIN, CIN[:, c0:c0 + n], TMP, TMP[:, c0:c0 + n], AF.Square)
                    S.mm(pYy, pYy[:, 0:n], cst2, bones, CIN, CIN[:, c0:c0 + n])
                    S.ts("dve", CIN, CIN[:, c0:c0 + n], pYy, pYy[:, 0:n], 1.0 / 64, GN_EPS, ALU.mult, ALU.add)
                    S.act(CIN, CIN[:, c0:c0 + n], CIN, CIN[:, c0:c0 + n], AF.Sqrt)
                    S.emit("dve", lambda e, c0=c0, n=n: e.reciprocal(out=CIN[:, c0:c0 + n], in_=CIN[:, c0:c0 + n]), [CIN], [CIN])
                    S.stt("dve", TMP, TMP[:, c0:c0 + n], TMP, TMP[:, c0:c0 + n], lnw[:, ec:ec + 1], CIN, CIN[:, c0:c0 + n],
                          ALU.mult, ALU.mult, extra_reads=[lnw])
                    S.stt("dve", TMP, TMP[:, c0:c0 + n], TMP, TMP[:, c0:c0 + n], lnb[:, ec:ec + 1], BON, BON[:, c0:c0 + n],
                          ALU.add, ALU.add, extra_reads=[lnb])
                    S.tt("pool", gst, gst[:, c0:c0 + n], TMP, TMP[:, c0:c0 + n], szb, szb[:, c0:c0 + n], ALU.mult)
                S.dma(gT.bufs(*rows, 0, T), gT.ap[rows[0]:rows[1], :], gst, gst[:])
            S.pop()

        def stage_l1_out(modG):
            S.push()
            ps = mkps()
            gb_ = S.sbuf("g1", [128, EC, 512], BF16)
            xb = S.sbuf("xb1", [128, KC, 512])
            sq = S.sbuf("sq1", [128, KC, 512])
            rstd = S.sbuf("rstd1", [128, 512])
            ot = [S.sbuf("ot", [128, D]) for _ in range(2)]
            gw = [S.sbuf("gw1", [128, EC, 128], BF16) for _ in range(2)]
            fng = S.sbuf("fng", [128, KC])
            S.dma(fng, fng[:], W["final_norm_g"], W["final_norm_g"].t.rearrange("(kc p) -> p kc", p=128), slow=True)
            ov = W["l1_out_w"].t.rearrange("(ei p) e -> p ei e", p=128)
            xTv = xT.ap.rearrange("(kc p) t -> p kc t", p=128)
            gTv = gT.ap.rearrange("(e p) t -> p e t", p=128)
            wi = 0
            oi = 0
            for (c0, n, r) in lat_blocks()[1:]:
                S.dma(gb_, gb_[:, :, 0:n], gT.bufs(0, E, 0, T), gTv[:, :, c0:c0 + n])
                S.dma(xb, xb[:, :, 0:n], xT.bufs(0, D, c0, c0 + n), xTv[:, :, c0:c0 + n])
                for dc in range(KC):
                    w = gw[wi % 2]; wi += 1
                    S.dma(w, w[:], W["l1_out_w"], ov[:, :, dc * 128:(dc + 1) * 128], q="pool")
                    pb = ps[dc % 4]
                    for ei in range(EC):
                        S.mm(pb, pb[:, 0:n], w, w[:, ei, :], gb_, gb_[:, ei, 0:n], start=(ei == 0), stop=(ei == EC - 1))
                    S.stt("dve", xb, xb[:, dc, 0:n], pb, pb[:, 0:n], modG[:, dc, r:r + 1], xb, xb[:, dc, 0:n],
                          ALU.mult, ALU.add, extra_reads=[modG])
                S.act(sq, sq[:, :, 0:n], xb, xb[:, :, 0:n], AF.Square)
                pb = ps[4]
                for kc in range(KC):
                    S.mm(pb, pb[:, 0:n], cst, ones, sq, sq[:, kc, 0:n], start=(kc == 0), stop=(kc == KC - 1))
                S.ts("dve", rstd, rstd[:, 0:n], pb, pb[:, 0:n], 1.0 / D, RMS_EPS, ALU.mult, ALU.add)
                S.act(rstd, rstd[:, 0:n], rstd, rstd[:, 0:n], AF.Sqrt)
                S.emit("dve", lambda e, n=n: e.reciprocal(out=rstd[:, 0:n], in_=rstd[:, 0:n]), [rstd], [rstd])
                for kc in range(KC):
                    S.stt("dve", xb, xb[:, kc, 0:n], xb, xb[:, kc, 0:n], fng[:, kc:kc + 1], rstd, rstd[:, 0:n],
                          ALU.mult, ALU.mult, extra_reads=[fng])
                for tt in range(n // 128):
                    o = ot[oi % 2]; oi += 1
                    for q in range(4):
                        pb = ps[5 + (q % 3)]
                        for j in range(4):
                            kc = 4 * q + j
                            S.tr(pb, pb[:, j * 128:(j + 1) * 128], xb, xb[:, kc, tt * 128:(tt + 1) * 128], cst, ident)
                        S.copy("act" if q % 2 == 0 else "dve", o, o[:, q * 512:(q + 1) * 512], pb, pb[:, :])
                    t0 = c0 - TC + tt * 128
                    S.dma(out_t, out_t[t0:t0 + 128, :], o, o[:])
            S.pop()

        dbg = dbg or ""
        stage_transpose_in()
        modA = S.sbuf("modA", [128, KC, 2]); modB = S.sbuf("modB", [128, KC, 2]); modG = S.sbuf("modG", [128, KC, 2])
        l1_only = dbg.startswith("l1")
        if not l1_only:
            stage_ada("l0_", modA, modB, modG)
            S.push()
            hT = S.sbuf("hT", [128, KC, T], BF16)
            stage_norm(hT, modA, modB)
            if dbg == "h0":
                ho = Buf(nc.dram_tensor("dbg_h", [D, T], BF16, kind="ExternalOutput").ap(), "dbg_h")
                S.dma(ho, ho.t.rearrange("(kc p) t -> p kc t", p=128), hT, hT[:])
                S.finish([ho])
            stage_inproj_l0(hT)
            S.pop()
        if dbg in ("s5pre", "s5") or dbg.startswith("s5pre:"):
            stage_s5(1 if dbg == "s5" else 0)
            if dbg == "s5":
                o_ = Buf(nc.dram_tensor("dbg_v", [128, T], BF16, kind="ExternalOutput").ap(), "dbg_v")
                S.push(); tb_ = S.sbuf("dbgv", [128, T], BF16)
                S.dma(tb_, tb_[:], vT.bufs(0, 128, 0, T), vT.ap[0:128, :]); S.dma(o_, o_[:, :], tb_, tb_[:]); S.pop()
                S.finish([o_])
        elif dbg != "h0" and not l1_only:
            stage_s5()
            stage_glu_out_l0(modG)
            if dbg == "l0":
                xo = Buf(nc.dram_tensor("dbg_x", [D, T], F32, kind="ExternalOutput").ap(), "dbg_x")
                S.push()
                tb_ = S.sbuf("dbgb", [128, KC, 256])
                xTv = xT.ap.rearrange("(kc p) t -> p kc t", p=128)
                for bi in range(9):
                    S.dma(tb_, tb_[:], xT.bufs(0, D, bi * 256, bi * 256 + 256), xTv[:, :, bi * 256:(bi + 1) * 256])
                    S.dma(xo, xo.t.rearrange("(kc p) t -> p kc t", p=128)[:, :, bi * 256:(bi + 1) * 256], tb_, tb_[:])
                S.pop()
                S.finish([xo])
        if dbg in ("", "l1", "l1scan"):
            stage_ada("l1_", modA, modB, modG)
            S.push()
            hT = S.sbuf("hT1", [128, KC, T], BF16)
            stage_norm(hT, modA, modB)
            stage_l1_proj(hT)
            S.pop()
            if dbg == "l1scan":
                stage_l1_scan(1)
                o_ = Buf(nc.dram_tensor("dbg_g", [128, T], BF16, kind="ExternalOutput").ap(), "dbg_g")
                S.push(); tb_ = S.sbuf("dbgg", [128, T], BF16)
                S.dma(tb_, tb_[:], gT.bufs(0, 128, 0, T), gT.ap[0:128, :]); S.dma(o_, o_[:, :], tb_, tb_[:]); S.pop()
                S.finish([o_])
            else:
                stage_l1_scan()
                stage_l1_out(modG)
                S.finish([out_t])
        print("instructions:", S.nins, S.cnt)
    return nc


def make_consts():
    c = np.zeros((128, C_W), np.float32)
    s = np.arange(128)
    c[:, C_IDENT:C_IDENT + 128] = np.eye(128)
    c[:, C_ONES:C_ONES + 128] = 1.0
    c[:, C_TRIF:C_TRIF + 128] = (s[:, None] <= s[None, :])
    c[:, C_TRIB:C_TRIB + 128] = (s[:, None] >= s[None, :])
    c[127, C_SELF:C_SELF + 128] = 1.0
    c[0, C_SELB:C_SELB + 128] = 1.0
    c[:, C_S1 + 0] = s + 1
    c[:, C_S1 + 1] = -(s + 1)
    c[:, C_S1 + 2] = 128 - s
    c[:, C_S1 + 3] = -(128 - s)
    c[:, C_MBD:C_MBD + 8] = ((s[:, None] // 16) == np.arange(8)[None, :])
    return c


WEIGHT_NAMES = ["l0_norm_g", "l0_ada_w", "l0_ada_b", "l0_in_w", "l0_a_re", "l0_a_im", "l0_log_step", "l0_b_re",
                "l0_b_im", "l0_c_re", "l0_c_im", "l0_d", "l0_glu_w", "l0_glu_b", "l0_out_w", "l1_norm_g", "l1_ada_w",
                "l1_ada_b", "l1_mu", "l1_in_w", "l1_w0", "l1_w1", "l1_w2", "l1_a0", "l1_a1", "l1_a2", "l1_k_k",
                "l1_k_a", "l1_r_k", "l1_ln_w", "l1_ln_b", "l1_out_w", "final_norm_g"]


def make_consts2():
    c = np.zeros((128, C2_W), np.float32)
    s = np.arange(128)
    lt = (s[:, None] < s[None, :]).astype(np.float32)
    le = (s[:, None] <= s[None, :]).astype(np.float32)
    gt = (s[:, None] > s[None, :]).astype(np.float32)
    ge = (s[:, None] >= s[None, :]).astype(np.float32)
    c[:, C2_M4:C2_M4 + 512] = np.concatenate([lt, le, lt, le], 1)
    c[:, C2_M4 + 512:C2_M4 + 1024] = np.concatenate([gt, ge, gt, ge], 1)
    c[:, C2_MN:C2_MN + 128] = gt
    c[:, C2_MN + 128:C2_MN + 256] = lt
    c[:, C2_BONES:C2_BONES + 128] = ((s[:, None] // 64) == (s[None, :] // 64))
    c[:, C2_RESET:C2_RESET + 128] = (s[None, :] != 0)
    return c


def make_in_maps(inputs, cores):
    cst = make_consts()
    cst2 = make_consts2()
    maps = []
    for b in cores:
        m = {"x": np.ascontiguousarray(inputs["x"][b]), "ctx": np.ascontiguousarray(inputs["ctx"][b]),
             "cond": np.ascontiguousarray(np.stack([inputs["c"][b], inputs["c_ctx"]], 0)), "cst": cst, "cst2": cst2}
        for nm in WEIGHT_NAMES:
            m[nm] = np.ascontiguousarray(inputs[nm], dtype=np.float32)
        maps.append(m)
    return maps


def kernel(**inputs):
    nc = build_program()
    maps = make_in_maps(inputs, list(range(8)))
    res = run_bass_kernel_spmd(nc, maps, core_ids=list(range(8)))
    return np.stack([r["out"] for r in res.results], 0).astype(np.float32)
```

```python
import contextlib, math
import numpy as np
import concourse.bass as bass
import concourse.mybir as mybir
from concourse.bass_utils import run_bass_kernel_spmd

F32 = mybir.dt.float32
BF16 = mybir.dt.bfloat16
ALU = mybir.AluOpType
AF = mybir.ActivationFunctionType
AX = mybir.AxisListType

D = 2048
E = 4096
T = 2304
TC = 256
KC = 16
EC = 32
NCH = 18
PI = math.pi
RMS_EPS = 1e-6
GN_EPS = 64e-5

C_IDENT, C_ONES, C_TRIF, C_TRIB, C_SELF, C_SELB = 0, 128, 256, 384, 512, 640
C_S1 = 768
C_MBD = 772
C_W = 780
C2_M4, C2_MN, C2_BONES, C2_RESET = 0, 1024, 1280, 1408
C2_W = 1536


class Ev:
    __slots__ = ("sem", "val", "eng", "key")

    def __init__(self, sem, val, eng, key):
        self.sem, self.val, self.eng, self.key = sem, val, eng, key


class Buf:
    __slots__ = ("t", "w", "r", "name", "excl")

    def __init__(self, t=None, name=""):
        self.t, self.w, self.r, self.name, self.excl = t, None, {}, name, False

    def __getitem__(self, idx):
        return self.t[idx]


class Sched:
    SAME_ENGINE_SYNC = True
    NDMASEM = 24

    def __init__(self, nc, es):
        self.nc = nc
        self.stack = [es]
        self.engs = {"pe": nc.tensor, "act": nc.scalar, "dve": nc.vector, "pool": nc.gpsimd, "sp": nc.sync}
        self.sem = {k: es.enter_context(nc.semaphore("s_" + k)) for k in ("pe", "act", "dve", "pool")}
        self.cnt = {k: 0 for k in self.sem}
        self.dsem = [es.enter_context(nc.semaphore("d%d" % i)) for i in range(2 * self.NDMASEM)]
        self.duse = [0] * (2 * self.NDMASEM)
        self.dnext = {"sp": 0, "pool": 0, "act": 0}
        self.seen = {k: {} for k in self.engs}
        self.nins = 0
        self.uid = 0
        self.scope_bufs = [[]]
        self.pending_free = {}

    def push(self):
        es = contextlib.ExitStack()
        es.__enter__()
        self.stack.append(es)
        self.scope_bufs.append([])

    def pop(self):
        for b in self.scope_bufs.pop():
            evs = list(b.r.values()) + ([b.w] if b.w is not None else [])
            for ev in evs:
                old = self.pending_free.get(ev.key)
                if old is None or old.val < ev.val:
                    self.pending_free[ev.key] = ev
        es = self.stack.pop()
        es.__exit__(None, None, None)

    def _newbuf(self, t, nm):
        b = Buf(t, nm)
        b.r = dict(self.pending_free)
        self.scope_bufs[-1].append(b)
        return b

    def sbuf(self, name, shape, dt=F32):
        self.uid += 1
        nm = "%s_%d" % (name, self.uid)
        return self._newbuf(self.stack[-1].enter_context(self.nc.sbuf_tensor(nm, list(shape), dt)), nm)

    def psum(self, name, shape, dt=F32):
        b = self._newbuf(self.stack[-1].enter_context(self.nc.psum_tensor(name, list(shape), dt)), name)
        b.excl = True
        return b

    def dram(self, name, shape, dt=F32, kind="Internal"):
        return self.nc.dram_tensor(name, list(shape), dt, kind=kind).ap()

    def _wait(self, engname, ev):
        seen = self.seen[engname]
        if seen.get(ev.key, 0) >= ev.val:
            return
        self.engs[engname].wait_ge(ev.sem, ev.val)
        seen[ev.key] = ev.val

    def emit(self, engname, fn, reads=(), writes=(), dma=False):
        waits = []
        for b in reads:
            if b.w is not None:
                waits.append(b.w)
            if b.excl:
                waits.extend(ev for ev in b.r.values() if ev.eng != engname)
        for b in writes:
            if b.w is not None:
                waits.append(b.w)
            waits.extend(b.r.values())
        for ev in waits:
            if ev.eng == engname and not dma:
                if engname in ("pe", "dve", "act") or not self.SAME_ENGINE_SYNC:
                    continue
            self._wait(engname, ev)
        eng = self.engs[engname]
        if dma:
            i = self.dnext[engname] + (self.NDMASEM if engname == "pool" else 0)
            self.dnext[engname] = (self.dnext[engname] + 1) % self.NDMASEM
            s = self.dsem[i]
            if self.duse[i] > 0:
                self._wait(engname, Ev(s, 16 * self.duse[i], "dma", ("d", i)))
            ins = fn(eng)
            self.duse[i] += 1
            ins.then_inc(s, 16)
            ev = Ev(s, 16 * self.duse[i], "dma", ("d", i))
        else:
            ins = fn(eng)
            self.cnt[engname] += 1
            ins.then_inc(self.sem[engname], 1)
            ev = Ev(self.sem[engname], self.cnt[engname], engname, engname)
        for b in reads:
            b.r[ev.key] = ev
        for b in writes:
            b.w = ev
            b.r = {}
        self.nins += 1
        return ev

    def dma(self, out_b, out_ap, in_b, in_ap, q="sp", slow=False):
        outs = out_b if isinstance(out_b, (list, tuple)) else [out_b]
        ins = in_b if isinstance(in_b, (list, tuple)) else [in_b]
        if slow:
            f = lambda e: e.dma_start(out=out_ap, in_=in_ap, allow_slow_non_contiguous=True)
        else:
            f = lambda e: e.dma_start(out=out_ap, in_=in_ap)
        return self.emit(q, f, reads=ins, writes=outs, dma=True)

    def finish(self, bufs):
        for b in bufs:
            if b.w is not None:
                self._wait("sp", b.w)

    def act(self, ob, o, ib, i, func, bias=None, scale=None, eng="act", extra_reads=()):
        kw = {}
        if bias is not None:
            kw["bias"] = bias
        if scale is not None:
            kw["scale"] = scale
        return self.emit(eng, lambda e: e.activation(out=o, in_=i, func=func, **kw), [ib] + list(extra_reads), [ob])

    def tt(self, eng, ob, o, ab, a, bb, b, op):
        return self.emit(eng, lambda e: e.tensor_tensor(out=o, in0=a, in1=b, op=op), [ab, bb], [ob])

    def ts(self, eng, ob, o, ab, a, s1, s2, op0, op1=None, extra_reads=()):
        if op1 is None:
            f = lambda e: e.tensor_scalar(out=o, in0=a, scalar1=s1, scalar2=None, op0=op0)
        else:
            f = lambda e: e.tensor_scalar(out=o, in0=a, scalar1=s1, scalar2=s2, op0=op0, op1=op1)
        return self.emit(eng, f, [ab] + list(extra_reads), [ob])

    def stt(self, eng, ob, o, ab, a, sc, bb, b, op0, op1, extra_reads=()):
        return self.emit(eng, lambda e: e.scalar_tensor_tensor(out=o, in0=a, scalar=sc, in1=b, op0=op0, op1=op1),
                         [ab, bb] + list(extra_reads), [ob])

    def copy(self, eng, ob, o, ib, i):
        if eng == "act":
            return self.emit(eng, lambda e: e.activation(out=o, in_=i, func=AF.Copy), [ib], [ob])
        return self.emit(eng, lambda e: e.tensor_copy(out=o, in_=i), [ib], [ob])

    def mm(self, ob, o, lb, lhsT, rb, rhs, start=True, stop=True):
        return self.emit("pe", lambda e: e.matmul(out=o, lhsT=lhsT, rhs=rhs, start=start, stop=stop), [lb, rb], [ob])

    def tr(self, ob, o, ib, i, identb, ident):
        return self.emit("pe", lambda e: e.transpose(out=o, in_=i, identity=ident), [ib, identb], [ob])


class DramT:
    def __init__(self, S, name, shape, dt, ru, cu, kind="Internal"):
        self.ap = S.dram(name, shape, dt, kind=kind)
        self.ru, self.cu = ru, cu
        self.nr, self.ncu = (shape[0] + ru - 1) // ru, (shape[1] + cu - 1) // cu
        self.units = [[Buf(None, "%s_%d_%d" % (name, i, j)) for j in range(self.ncu)] for i in range(self.nr)]

    def bufs(self, r0, r1, c0, c1):
        out = []
        for i in range(r0 // self.ru, (r1 - 1) // self.ru + 1):
            for j in range(c0 // self.cu, (c1 - 1) // self.cu + 1):
                out.append(self.units[i][j])
        return out


def lat_blocks():
    return [(0, 256, 1)] + [(256 + 512 * i, 512, 0) for i in range(4)]


def build_program(dbg=None):
    dbg = dbg or ""
    nc = bass.Bass("TRN2", target_bir_lowering=False)
    es = contextlib.ExitStack()
    with es:
        S = Sched(nc, es)
        inp = {}

        def ext(name, shape):
            b = Buf(nc.dram_tensor(name, list(shape), F32, kind="ExternalInput").ap(), name)
            inp[name] = b
            return b

        x_in = ext("x", [2048, D])
        ctx_in = ext("ctx", [TC, D])
        cond_in = ext("cond", [2, D])
        cst_in = ext("cst", [128, C_W])
        cst2_in = ext("cst2", [128, C2_W])
        W = {}
        for nm, shp in [
            ("l0_norm_g", [D]), ("l0_ada_w", [D, 3 * D]), ("l0_ada_b", [3 * D]), ("l0_in_w", [D, 2 * E]),
            ("l0_a_re", [2, 256, 64]), ("l0_a_im", [2, 256, 64]), ("l0_log_step", [2, 256]),
            ("l0_b_re", [2, 256, 64, 16]), ("l0_b_im", [2, 256, 64, 16]),
            ("l0_c_re", [2, 256, 16, 64]), ("l0_c_im", [2, 256, 16, 64]),
            ("l0_d", [E]), ("l0_glu_w", [E, E]), ("l0_glu_b", [E]), ("l0_out_w", [E, D]),
            ("l1_norm_g", [D]), ("l1_ada_w", [D, 3 * D]), ("l1_ada_b", [3 * D]), ("l1_mu", [6, D]),
            ("l1_in_w", [4, D, E]), ("l1_w0", [2, E]), ("l1_w1", [2, D, 96]), ("l1_w2", [2, 96, E]),
            ("l1_a0", [2, E]), ("l1_a1", [2, D, 96]), ("l1_a2", [2, 96, E]), ("l1_k_k", [E]), ("l1_k_a", [E]),
            ("l1_r_k", [64, 64]), ("l1_ln_w", [E]), ("l1_ln_b", [E]), ("l1_out_w", [E, D]), ("final_norm_g", [D]),
        ]:
            W[nm] = ext(nm, shp)
        out_t = Buf(nc.dram_tensor("out", [2048, D], F32, kind="ExternalOutput").ap(), "out")

        xT = DramT(S, "xT", [D, T], F32, D, 256)
        uT = DramT(S, "uT", [E, T], F32, 128, T)
        szT = DramT(S, "szT", [E, T], BF16, 128, T)
        vT = DramT(S, "vT", [E, T], BF16, 128, T)
        s5c = DramT(S, "s5c", [4 * 512, 64], F32, 512, 64)

        cst = S.sbuf("cst", [128, C_W])
        S.dma(cst, cst[:], cst_in, cst_in[:, :])
        ident = cst[:, C_IDENT:C_IDENT + 128]
        ones = cst[:, C_ONES:C_ONES + 128]
        def mkps(n=8):
            return [S.psum("ps%d_%d" % (i, S.uid), [128, 512]) for i in range(n)]

        dbg_out = {}

        def stage_transpose_in():
            S.push()
            ps = mkps()
            xin = [S.sbuf("xin", [128, D]) for _ in range(2)]
            xtt = [S.sbuf("xtt", [128, KC, 128]) for _ in range(2)]
            xTv = xT.ap.rearrange("(kc p) t -> p kc t", p=128)
            for i in range(NCH):
                src_b, src = (ctx_in, ctx_in[i * 128:(i + 1) * 128, :]) if i < 2 else (x_in, x_in[(i - 2) * 128:(i - 1) * 128, :])
                xi, xo = xin[i % 2], xtt[i % 2]
                S.dma(xi, xi[:], src_b, src)
                for q in range(4):
                    pb = ps[(i * 4 + q) % 8]
                    for j in range(4):
                        kc = 4 * q + j
                        S.tr(pb, pb[:, j * 128:(j + 1) * 128], xi, xi[:, kc * 128:(kc + 1) * 128], cst, ident)
                    S.copy("act" if q % 2 == 0 else "dve", xo, xo[:, 4 * q:4 * q + 4, :],
                           pb, pb[:, :].rearrange("p (a b) -> p a b", a=4))
                S.dma(xT.bufs(0, D, i * 128, (i + 1) * 128), xTv[:, :, i * 128:(i + 1) * 128], xo, xo[:])
            S.pop()

        def stage_ada(pre, modA, modB, modG):
            S.push()
            ps = mkps()
            condT = S.sbuf("condT", [128, 2, KC])
            for r in range(2):
                S.dma(condT, condT[:, r, :], cond_in, cond_in.t[r, :].rearrange("(kc p) -> p kc", p=128), slow=True)
            S.act(condT, condT[:], condT, condT[:], AF.Silu)
            adab = S.sbuf("adab", [128, 48])
            S.dma(adab, adab[:], W[pre + "ada_b"], W[pre + "ada_b"].t.rearrange("(jc p) -> p jc", p=128), slow=True)
            normg = S.sbuf("normg", [128, KC])
            S.dma(normg, normg[:], W[pre + "norm_g"], W[pre + "norm_g"].t.rearrange("(kc p) -> p kc", p=128), slow=True)
            wb = [S.sbuf("adaw", [128, KC, 512]) for _ in range(2)]
            wv = W[pre + "ada_w"].t.rearrange("(kc p) j -> p kc j", p=128)
            pA = ps[7]
            for grp in range(12):
                w = wb[grp % 2]
                S.dma(w, w[:], W[pre + "ada_w"], wv[:, :, grp * 512:(grp + 1) * 512])
                for jj in range(4):
                    jc = grp * 4 + jj
                    for kc in range(KC):
                        S.mm(pA, pA[:, jc * 2:jc * 2 + 2], w, w[:, kc, jj * 128:(jj + 1) * 128], condT, condT[:, :, kc],
                             start=(kc == 0), stop=(kc == KC - 1))
            mod = S.sbuf("mod", [128, 48, 2])
            S.tt("dve", mod, mod[:], pA, pA[:, 0:96].rearrange("p (j r) -> p j r", r=2),
                 adab, adab[:, :].unsqueeze(2).to_broadcast([128, 48, 2]), ALU.add)
            S.copy("dve", modB, modB[:], mod, mod[:, 0:16, :])
            S.copy("dve", modG, modG[:], mod, mod[:, 32:48, :])
            S.stt("dve", modA, modA[:], mod, mod[:, 16:32, :], 1.0, normg,
                  normg[:, :].unsqueeze(2).to_broadcast([128, KC, 2]), ALU.add, ALU.mult)
            S.pop()

        def stage_norm(hT, modA, modB):
            S.push()
            ps = mkps()
            xb = [S.sbuf("xb", [128, KC, 256]) for _ in range(2)]
            sq = S.sbuf("sq", [128, KC, 256])
            tmp = S.sbuf("ntmp", [128, KC, 256])
            rstd = S.sbuf("rstd", [128, 256])
            xTv = xT.ap.rearrange("(kc p) t -> p kc t", p=128)
            for bi in range(T // 256):
                c0 = bi * 256
                r = 1 if bi == 0 else 0
                x = xb[bi % 2]
                S.dma(x, x[:], xT.bufs(0, D, c0, c0 + 256), xTv[:, :, c0:c0 + 256])
                S.act(sq, sq[:], x, x[:], AF.Square)
                pb = ps[bi % 2]
                for kc in range(KC):
                    S.mm(pb, pb[:, 0:256], cst, ones, sq, sq[:, kc, :], start=(kc == 0), stop=(kc == KC - 1))
                S.ts("dve", rstd, rstd[:], pb, pb[:, 0:256], 1.0 / D, RMS_EPS, ALU.mult, ALU.add)
                S.act(rstd, rstd[:], rstd, rstd[:], AF.Sqrt)
                S.emit("dve", lambda e: e.reciprocal(out=rstd[:], in_=rstd[:]), [rstd], [rstd])
                for kc in range(KC):
                    S.stt("dve", tmp, tmp[:, kc, :], x, x[:, kc, :], modA[:, kc, r:r + 1], rstd, rstd[:],
                          ALU.mult, ALU.mult, extra_reads=[modA])
                    S.act(hT, hT[:, kc, c0:c0 + 256], tmp, tmp[:, kc, :], AF.Identity, bias=modB[:, kc, r:r + 1],
                          extra_reads=[modB])
            S.pop()

        def proj_chunk(hT, wt, pbanks, evac):
            for tb in range(5):
                c0 = tb * 512
                n = min(512, T - c0)
                pb = pbanks[tb % len(pbanks)]
                for kc in range(KC):
                    S.mm(pb, pb[:, 0:n], wt, wt[:, kc, :], hT, hT[:, kc, c0:c0 + n], start=(kc == 0), stop=(kc == KC - 1))
                evac(tb, c0, n, pb)

        def stage_inproj_l0(hT):
            S.push()
            ps = mkps()
            wb = [S.sbuf("inw", [128, KC, 128], BF16) for _ in range(2)]
            ust = [S.sbuf("ust", [128, T]) for _ in range(2)]
            zst = [S.sbuf("zst", [128, T], BF16) for _ in range(2)]
            wv = W["l0_in_w"].t.rearrange("(kc p) e -> p kc e", p=128)
            for ec in range(2 * EC):
                w = wb[ec % 2]
                S.dma(w, w[:], W["l0_in_w"], wv[:, :, ec * 128:(ec + 1) * 128], q="pool")
                if ec < EC:
                    st = ust[ec % 2]
                    proj_chunk(hT, w, ps[0:4], lambda tb, c0, n, pb: S.copy("act" if tb % 2 else "dve", st, st[:, c0:c0 + n], pb, pb[:, 0:n]))
                    S.dma(uT.bufs(ec * 128, ec * 128 + 128, 0, T), uT.ap[ec * 128:(ec + 1) * 128, :], st, st[:])
                else:
                    st = zst[ec % 2]
                    proj_chunk(hT, w, ps[0:4], lambda tb, c0, n, pb: S.act(st, st[:, c0:c0 + n], pb, pb[:, 0:n], AF.Silu))
                    r0 = (ec - EC) * 128
                    S.dma(szT.bufs(r0, r0 + 128, 0, T), szT.ap[r0:r0 + 128, :], st, st[:])
            S.pop()

        MAGIC = 12582912.0

        def sin_arg(eng, ob, o, ib, i, mul, add, tb, t):
            if isinstance(mul, float):
                S.ts(eng, ob, o, ib, i, mul, add, ALU.mult, ALU.add)
            else:
                S.ts(eng, ob, o, ib, i, mul, None, ALU.mult, extra_reads=[cst])
                if add != 0.0:
                    S.ts(eng, ob, o, ob, o, add, None, ALU.add)
            S.ts(eng, tb, t, ob, o, 1.0 / (2 * PI), MAGIC, ALU.mult, ALU.add)
            S.ts(eng, tb, t, tb, t, -MAGIC, None, ALU.add)
            S.stt(eng, ob, o, tb, t, -2 * PI, ob, o, ALU.mult, ALU.add)

        SIN_SC = 0.999999

        def rr(*gens):
            gens = list(gens)
            while gens:
                for g in list(gens):
                    try:
                        next(g)
                        yield
                    except StopIteration:
                        gens.remove(g)

        def stage_s5(ngb=EC):
            S.push()
            pBu = S.psum("pBu", [128, 1024]); pG = S.psum("pG", [128, 1024]); pT = S.psum("pT", [128, 1024])
            pG2 = S.psum("pG2", [128, 1024])
            pY = pG2
            pX = pG
            Bbar = [S.sbuf("Bbar%d" % c, [128, 64, 64]) for c in range(2)]
            Cn = S.sbuf("Cn", [128, 64, 128], BF16)
            dsk = S.sbuf("dsk", [128, EC])
            S.dma(dsk, dsk[:], W["l0_d"], W["l0_d"].t.rearrange("(gb p) -> p gb", p=128), slow=True)
            tri = [S.sbuf("tri%d" % d, [128, 128], BF16) for d in range(2)]
            S.copy("dve", tri[0], tri[0][:], cst, cst[:, C_TRIF:C_TRIF + 128])
            S.copy("dve", tri[1], tri[1][:], cst, cst[:, C_TRIB:C_TRIB + 128])
            sel = [cst[:, C_SELF:C_SELF + 128], cst[:, C_SELB:C_SELB + 128]]
            stop_k = int(dbg.split(":")[1]) if (dbg and ":" in dbg) else 99

            def ckpt(k):
                if k != stop_k:
                    return False
                o_ = Buf(nc.dram_tensor("dbg_ck", [128, EC], F32, kind="ExternalOutput").ap(), "dbg_ck")
                S.dma(o_, o_[:, :], dsk, dsk[:]); S.finish([o_])
                return True
            if ckpt(1):
                S.pop(); return
            S.push()
            cv = lambda ap3: ap3.rearrange("d g n -> (d g) n").rearrange("(q p) n -> p q n", p=128)
            are = S.sbuf("are", [128, 4, 64]); aim = S.sbuf("aim", [128, 4, 64])
            S.dma(are, are[:], W["l0_a_re"], cv(W["l0_a_re"].t))
            S.dma(aim, aim[:], W["l0_a_im"], cv(W["l0_a_im"].t))
            ls = S.sbuf("ls", [128, 4])
            S.dma(ls, ls[:], W["l0_log_step"], W["l0_log_step"].t.rearrange("d g -> (d g)").rearrange("(q p) -> p q", p=128), slow=True)
            S.act(ls, ls[:], ls, ls[:], AF.Exp)
            dtb = ls[:, :].unsqueeze(2).to_broadcast([128, 4, 64])
            dtar = S.sbuf("dtar", [128, 4, 64]); dtai = S.sbuf("dtai", [128, 4, 64])
            S.tt("dve", dtar, dtar[:], are, are[:], ls, dtb, ALU.mult)
            S.tt("dve", dtai, dtai[:], aim, aim[:], ls, dtb, ALU.mult)
            mag = S.sbuf("mag", [128, 4, 64]); sn = S.sbuf("sn", [128, 4, 64]); cs = S.sbuf("cs", [128, 4, 64])
            S.act(mag, mag[:], dtar, dtar[:], AF.Exp)
            rt = S.sbuf("rt", [128, 4, 64])
            sin_arg("dve", sn, sn[:], dtai, dtai[:], 1.0, 0.0, rt, rt[:])
            S.act(sn, sn[:], sn, sn[:], AF.Sin, scale=SIN_SC)
            sin_arg("dve", cs, cs[:], dtai, dtai[:], 1.0, 0.5 * PI, rt, rt[:])
            S.act(cs, cs[:], cs, cs[:], AF.Sin, scale=SIN_SC)
            nr = S.sbuf("nr", [128, 4, 64]); ni = S.sbuf("ni", [128, 4, 64])
            S.tt("dve", nr, nr[:], mag, mag[:], cs, cs[:], ALU.mult)
            S.ts("dve", nr, nr[:], nr, nr[:], -1.0, None, ALU.add)
            S.tt("dve", ni, ni[:], mag, mag[:], sn, sn[:], ALU.mult)
            den = S.sbuf("den", [128, 4, 64]); t0 = S.sbuf("t0", [128, 4, 64])
            S.tt("dve", den, den[:], are, are[:], are, are[:], ALU.mult)
            S.tt("dve", t0, t0[:], aim, aim[:], aim, aim[:], ALU.mult)
            S.tt("dve", den, den[:], den, den[:], t0, t0[:], ALU.add)
            S.emit("dve", lambda e: e.reciprocal(out=den[:], in_=den[:]), [den], [den])
            fre = S.sbuf("fre", [128, 4, 64]); fim = S.sbuf("fim", [128, 4, 64])
            S.tt("dve", fre, fre[:], nr, nr[:], are, are[:], ALU.mult)
            S.tt("dve", t0, t0[:], ni, ni[:], aim, aim[:], ALU.mult)
            S.tt("dve", fre, fre[:], fre, fre[:], t0, t0[:], ALU.add)
            S.tt("dve", fre, fre[:], fre, fre[:], den, den[:], ALU.mult)
            S.tt("dve", fim, fim[:], ni, ni[:], are, are[:], ALU.mult)
            S.tt("dve", t0, t0[:], nr, nr[:], aim, aim[:], ALU.mult)
            S.tt("dve", fim, fim[:], fim, fim[:], t0, t0[:], ALU.subtract)
            S.tt("dve", fim, fim[:], fim, fim[:], den, den[:], ALU.mult)
            if ckpt(2):
                S.pop(); S.pop(); return
            for k, tsrc in enumerate((dtar, dtai, fre, fim)):
                S.dma(s5c.bufs(k * 512, k * 512 + 512, 0, 64),
                      s5c.ap[k * 512:(k + 1) * 512, :].rearrange("(q p) n -> p q n", p=128), tsrc, tsrc[:])
            if ckpt(3):
                S.pop(); S.pop(); return
            S.pop()
            S.push()
            Bn = S.sbuf("Bn", [128, 512, 16])
            for q in range(16):
                S.dma(Bn, Bn[0:64, q * 32:(q + 1) * 32, :], W["l0_b_re"], W["l0_b_re"].t.rearrange("d g n j -> n (d g) j")[:, q * 32:(q + 1) * 32, :])
                S.dma(Bn, Bn[64:128, q * 32:(q + 1) * 32, :], W["l0_b_im"], W["l0_b_im"].t.rearrange("d g n j -> n (d g) j")[:, q * 32:(q + 1) * 32, :])
            Bre = S.sbuf("Bre", [128, 64, 64]); Bim = S.sbuf("Bim", [128, 64, 64])
            Fre = S.sbuf("Fre", [128, 64, 64]); Fim = S.sbuf("Fim", [128, 64, 64])
            for gp in range(8):
                for dst, k in ((Fre, 2), (Fim, 3)):
                    S.dma(dst, dst[gp * 16:(gp + 1) * 16, :, :], s5c.bufs(k * 512, k * 512 + 512, 0, 64),
                          bass.AP(s5c.ap.tensor, s5c.ap.offset + (k * 512 + gp) * 64, [[0, 16], [512, 64], [1, 64]]))
            if stop_k == 4:
                S.finish([Bn, Fre, Fim])
            if stop_k == 41:
                S.finish([Bn])
            if stop_k == 42:
                S.finish([Fre, Fim])
            if ckpt(4) or ckpt(41) or ckpt(42):
                S.pop(); S.pop(); return
            for q in range(16):
                pb = pY if q % 2 else pX
                for j in range(4):
                    dgb = q * 4 + j
                    S.tr(pb, pb[:, j * 128:(j + 1) * 128], Bn, Bn[:, dgb * 8:(dgb + 1) * 8, :].rearrange('p a b -> p (a b)'), cst, ident)
                pv = pb[:, 0:512].rearrange("p (a c n) -> p a c n", a=4, c=2)
                S.copy("act", Bre, Bre[:, q * 4:q * 4 + 4, :], pb, pv[:, :, 0, :])
                S.copy("dve", Bim, Bim[:, q * 4:q * 4 + 4, :], pb, pv[:, :, 1, :])
            if stop_k == 45:
                S.finish([Bre, Bim])
            if ckpt(45):
                S.pop(); S.pop(); return
            tA = S.sbuf("tA", [128, 64, 64])
            S.tt("dve", Bbar[0], Bbar[0][:], Fre, Fre[:], Bre, Bre[:], ALU.mult)
            S.tt("pool", tA, tA[:], Fim, Fim[:], Bim, Bim[:], ALU.mult)
            S.tt("dve", Bbar[0], Bbar[0][:], Bbar[0], Bbar[0][:], tA, tA[:], ALU.subtract)
            S.tt("dve", Bbar[1], Bbar[1][:], Fre, Fre[:], Bim, Bim[:], ALU.mult)
            S.tt("pool", tA, tA[:], Fim, Fim[:], Bre, Bre[:], ALU.mult)
            S.tt("dve", Bbar[1], Bbar[1][:], Bbar[1], Bbar[1][:], tA, tA[:], ALU.add)
            if ckpt(5):
                S.pop(); S.pop(); return
            S.pop()
            S.push()
            Cnat = S.sbuf("Cnat", [128, 64, 2, 64])
            for q in range(4):
                S.dma(Cnat, Cnat[:, q * 16:(q + 1) * 16, 0, :], W["l0_c_re"], W["l0_c_re"].t.rearrange("d (gb gp) i n -> (gp i) (d gb) n", gp=8)[:, q * 16:(q + 1) * 16, :])
                S.dma(Cnat, Cnat[:, q * 16:(q + 1) * 16, 1, :], W["l0_c_im"], W["l0_c_im"].t.rearrange("d (gb gp) i n -> (gp i) (d gb) n", gp=8)[:, q * 16:(q + 1) * 16, :])
            for q in range(16):
                pb = pY if q % 2 else pX
                for j in range(4):
                    dgb = q * 4 + j
                    S.tr(pb, pb[:, j * 128:(j + 1) * 128], Cnat, Cnat[:, dgb, :, :].rearrange('p a b -> p (a b)'), cst, ident)
                pv = pb[:, 0:512].rearrange("p (a m) -> p a m", a=4)
                S.copy("act", Cn, Cn[0:64, q * 4:q * 4 + 4, :], pb, pv[0:64, :, :])
                S.ts("dve", Cn, Cn[64:128, q * 4:q * 4 + 4, :], pb, pv[64:128, :, :], -1.0, None, ALU.mult)
            S.pop()

            u32 = [S.sbuf("u32", [128, T]) for _ in range(2)]
            ubf = S.sbuf("ubf", [128, T], BF16)
            yacc = S.sbuf("yacc", [128, T])
            vst = [S.sbuf("vst", [128, T], BF16) for _ in range(2)]
            ex = S.sbuf("ex", [128, 512]); ex2 = S.sbuf("ex2", [128, 512])
            v3 = lambda t: t[:, :].rearrange("p (g n) -> p g n", g=8)
            v4 = lambda ap2: ap2.rearrange("p (g c n) -> p g c n", g=8, c=2)
            flat = lambda ap4: ap4.rearrange("p g c n -> p (g c n)")
            DB = []
            for d in range(2):
                B_ = {}
                for nm_ in ("drebc", "dimbc", "magm", "magp", "snT", "csT", "Lmr", "Lmi", "Lpr", "Lpi"):
                    B_[nm_] = S.sbuf(nm_, [128, 512])
                B_["BBD"] = S.sbuf("BBD", [128, 8, 2, 64], BF16)
                B_["CBD"] = S.sbuf("CBD", [128, 8, 128], BF16)
                S.emit("dve", lambda e, c_=B_["CBD"]: e.memset(c_[:], 0.0), [], [B_["CBD"]])
                B_["tt"] = [S.sbuf("s5t%d" % k, [128, 8, 64]) for k in range(4)]
                B_["X"] = [S.sbuf("X", [128, 8, 2, 64], BF16) for _ in range(2)]
                B_["Hc"] = [S.sbuf("Hc", [128, 8, 2, 64]) for _ in range(2)]
                B_["HTc"] = [S.sbuf("HTc", [128, 8, 128], BF16) for _ in range(2)]
                B_["pBu"] = pBu if d == 0 else pT
                B_["pG"] = pG if d == 0 else pG2
                DB.append(B_)

            def s5_dir(gb, d, u):
                B_ = DB[d]
                drebc, dimbc, magm, magp, snT, csT = (B_[k] for k in ("drebc", "dimbc", "magm", "magp", "snT", "csT"))
                Lmr, Lmi, Lpr, Lpi, BBD, CBD, tt_, X, Hc, HTc = (B_[k] for k in ("Lmr", "Lmi", "Lpr", "Lpi", "BBD", "CBD", "tt", "X", "Hc", "HTc"))
                pBu_, pG_ = B_["pBu"], B_["pG"]
                Bu4 = v4(pBu_[:, :]); G4 = v4(pG_[:, :])
                s1 = cst[:, C_S1 + 2 * d:C_S1 + 2 * d + 1]
                ns1 = cst[:, C_S1 + 2 * d + 1:C_S1 + 2 * d + 2]
                row0 = d * 256 + gb * 8
                for dst, k in ((drebc, 0), (dimbc, 1)):
                    S.dma(dst, dst[:], s5c.bufs(k * 512, k * 512 + 512, 0, 64),
                          bass.AP(s5c.ap.tensor, s5c.ap.offset + (k * 512 + row0) * 64, [[0, 128], [1, 512]]))
                S.act(magm, magm[:], drebc, drebc[:], AF.Exp, scale=ns1, extra_reads=[cst])
                S.act(magp, magp[:], drebc, drebc[:], AF.Exp, scale=s1, extra_reads=[cst])
                yield
                sin_arg("dve", snT, snT[:], dimbc, dimbc[:], s1, 0.0, Lmr, Lmr[:])
                S.act(snT, snT[:], snT, snT[:], AF.Sin, scale=SIN_SC)
                yield
                sin_arg("dve", csT, csT[:], dimbc, dimbc[:], s1, 0.5 * PI, Lmr, Lmr[:])
                S.act(csT, csT[:], csT, csT[:], AF.Sin, scale=SIN_SC)
                yield
                S.tt("dve", Lmr, Lmr[:], magm, magm[:], csT, csT[:], ALU.mult)
                S.stt("dve", Lmi, Lmi[:], magm, magm[:], -1.0, snT, snT[:], ALU.mult, ALU.mult)
                S.tt("pool", Lpr, Lpr[:], magp, magp[:], csT, csT[:], ALU.mult)
                S.tt("pool", Lpi, Lpi[:], magp, magp[:], snT, snT[:], ALU.mult)
                yield
                for c in range(2):
                    for g2 in range(8):
                        S.ts("dve" if g2 % 2 else "pool", BBD, BBD[:, g2, c, :], Bbar[c], Bbar[c][:, d * 32 + gb, :],
                             cst[:, C_MBD + g2:C_MBD + g2 + 1], None, ALU.mult, extra_reads=[cst])
                    yield
                cfull = CBD[:]
                S.copy("pool", CBD, bass.AP(cfull.tensor, cfull.offset, [[cfull.ap[0][0], 128], [144, 8], [1, 16]]),
                       Cn, Cn[:, d * 32 + gb, :].rearrange("p (g i) -> p g i", g=8))
                order = list(range(NCH)) if d == 0 else [1, 0] + list(range(NCH - 1, 1, -1))

                def stA(ci, ch):
                    c0 = ch * 128
                    Xc = X[ci % 2]
                    for h in range(2):
                        S.mm(pBu_, pBu_[:, h * 512:(h + 1) * 512], ubf, ubf[:, c0:c0 + 128], BBD, flat(BBD[:, 4 * h:4 * h + 4, :, :]))
                    yield
                    S.tt("dve", tt_[0], tt_[0][:], Lmr, v3(Lmr), pBu_, Bu4[:, :, 0, :], ALU.mult)
                    S.tt("dve", tt_[1], tt_[1][:], Lmi, v3(Lmi), pBu_, Bu4[:, :, 1, :], ALU.mult)
                    S.tt("pool", Xc, Xc[:, :, 0, :], tt_[0], tt_[0][:], tt_[1], tt_[1][:], ALU.subtract)
                    yield
                    S.tt("dve", tt_[0], tt_[0][:], Lmr, v3(Lmr), pBu_, Bu4[:, :, 1, :], ALU.mult)
                    S.tt("dve", tt_[1], tt_[1][:], Lmi, v3(Lmi), pBu_, Bu4[:, :, 0, :], ALU.mult)
                    S.tt("pool", Xc, Xc[:, :, 1, :], tt_[0], tt_[0][:], tt_[1], tt_[1][:], ALU.add)
                    yield

                def stB(ci, ch):
                    c0 = ch * 128
                    Xc = X[ci % 2]
                    hp, hc = Hc[(ci + 1) % 2], Hc[ci % 2]
                    first = ci == 0
                    for h in range(2):
                        S.mm(pG_, pG_[:, h * 512:(h + 1) * 512], tri[d], tri[d][:], Xc, flat(Xc[:, 4 * h:4 * h + 4, :, :]), start=True, stop=first)
                        if not first:
                            S.mm(pG_, pG_[:, h * 512:(h + 1) * 512], cst, sel[d], hp, flat(hp[:, 4 * h:4 * h + 4, :, :]), start=False, stop=True)
                    yield
                    S.tt("dve", tt_[2], tt_[2][:], Lpr, v3(Lpr), pG_, G4[:, :, 0, :], ALU.mult)
                    S.tt("dve", tt_[3], tt_[3][:], Lpi, v3(Lpi), pG_, G4[:, :, 1, :], ALU.mult)
                    S.tt("pool", hc, hc[:, :, 0, :], tt_[2], tt_[2][:], tt_[3], tt_[3][:], ALU.subtract)
                    yield
                    S.tt("dve", tt_[2], tt_[2][:], Lpr, v3(Lpr), pG_, G4[:, :, 1, :], ALU.mult)
                    S.tt("dve", tt_[3], tt_[3][:], Lpi, v3(Lpi), pG_, G4[:, :, 0, :], ALU.mult)
                    S.tt("pool", hc, hc[:, :, 1, :], tt_[2], tt_[2][:], tt_[3], tt_[3][:], ALU.add)
                    yield
                    ht = HTc[ci % 2]
                    for gp in range(8):
                        S.tr(pG_, pG_[:, gp * 128:(gp + 1) * 128], hc, hc[:, gp, :, :].rearrange("p c n -> p (c n)"), cst, ident)
                    yield
                    S.copy("act", ht, ht[:, :, :], pG_, pG_[:, :].rearrange("p (a b) -> p a b", a=8))
                    yield
                    for gp in range(8):
                        S.mm(pG_, pG_[:, 0:128], CBD, CBD[:, gp, :], ht, ht[:, gp, :], start=(gp == 0), stop=(gp == 7))
                    yield
                    S.tt("dve", yacc, yacc[:, c0:c0 + 128], yacc, yacc[:, c0:c0 + 128], pG_, pG_[:, 0:128], ALU.add)
                    yield

                yield from stA(0, order[0])
                for ci, ch in enumerate(order):
                    gens = [stB(ci, ch)]
                    if ci + 1 < NCH:
                        gens.append(stA(ci + 1, order[ci + 1]))
                    yield from rr(*gens)

            for gb in range(ngb):
                u = u32[gb % 2]
                S.dma(u, u[:], uT.bufs(gb * 128, gb * 128 + 128, 0, T), uT.ap[gb * 128:(gb + 1) * 128, :])
                S.copy("pool", ubf, ubf[:], u, u[:])
                S.emit("pool", lambda e: e.memset(yacc[:], 0.0), [], [yacc])
                for _ in rr(s5_dir(gb, 0, u), s5_dir(gb, 1, u)):
                    pass
                v = vst[gb % 2]
                for tb in range(5):
                    c0 = tb * 512
                    n = min(512, T - c0)
                    S.stt("dve", ex, ex[:, 0:n], u, u[:, c0:c0 + n], dsk[:, gb:gb + 1], yacc, yacc[:, c0:c0 + n],
                          ALU.mult, ALU.add, extra_reads=[dsk])
                    S.act(ex2, ex2[:, 0:n], ex, ex[:, 0:n], AF.Square)
                    S.ts("dve", ex2, ex2[:, 0:n], ex2, ex2[:, 0:n], 0.044715, 1.0, ALU.mult, ALU.add)
                    S.tt("pool", ex2, ex2[:, 0:n], ex2, ex2[:, 0:n], ex, ex[:, 0:n], ALU.mult)
                    S.act(ex2, ex2[:, 0:n], ex2, ex2[:, 0:n], AF.Sigmoid, scale=2.0 * math.sqrt(2.0 / PI))
                    S.tt("pool", v, v[:, c0:c0 + n], ex, ex[:, 0:n], ex2, ex2[:, 0:n], ALU.mult)
                S.dma(vT.bufs(gb * 128, gb * 128 + 128, 0, T), vT.ap[gb * 128:(gb + 1) * 128, :], v, v[:])
            S.pop()

        def stage_glu_out_l0(modG):
            S.push()
            ps = mkps()
            HW_ = T // 2
            vb = S.sbuf("vb", [128, EC, HW_], BF16)
            gb_ = S.sbuf("gb", [128, EC, HW_], BF16)
            zb = [S.sbuf("zb", [128, HW_], BF16) for _ in range(2)]
            xb = [S.sbuf("xbo", [128, HW_]) for _ in range(2)]
            sg = S.sbuf("sg", [128, 512])
            gw = [S.sbuf("gw", [128, EC, 128], BF16) for _ in range(2)]
            glub = S.sbuf("glub", [128, EC])
            S.dma(glub, glub[:], W["l0_glu_b"], W["l0_glu_b"].t.rearrange("(e p) -> p e", p=128), slow=True)
            gv = W["l0_glu_w"].t.rearrange("(ei p) e -> p ei e", p=128)
            ov = W["l0_out_w"].t.rearrange("(ei p) e -> p ei e", p=128)
            vTv = vT.ap.rearrange("(e p) t -> p e t", p=128)
            wi = 0
            pi_ = 0
            for half in range(2):
                h0 = half * HW_
                subs = [(0, 256, 1), (256, 512, 0), (768, 384, 0)] if half == 0 else [(0, 512, 0), (512, 512, 0), (1024, 128, 0)]
                for q in range(4):
                    S.dma(vb, vb[:, q * 8:(q + 1) * 8, :], vT.bufs(0, E, 0, T), vTv[:, q * 8:(q + 1) * 8, h0:h0 + HW_])
                for eo in range(EC):
                    w = gw[wi % 2]; wi += 1
                    S.dma(w, w[:], W["l0_glu_w"], gv[:, :, eo * 128:(eo + 1) * 128], q="pool")
                    z = zb[eo % 2]
                    S.dma(z, z[:], szT.bufs(eo * 128, eo * 128 + 128, 0, T), szT.ap[eo * 128:(eo + 1) * 128, h0:h0 + HW_])
                    for (c0, n, r) in subs:
                        pb = ps[pi_ % 4]; pi_ += 1
                        for ei in range(EC):
                            S.mm(pb, pb[:, 0:n], w, w[:, ei, :], vb, vb[:, ei, c0:c0 + n], start=(ei == 0), stop=(ei == EC - 1))
                        S.act(sg, sg[:, 0:n], pb, pb[:, 0:n], AF.Sigmoid, bias=glub[:, eo:eo + 1], extra_reads=[glub])
                        S.tt("dve", sg, sg[:, 0:n], sg, sg[:, 0:n], vb, vb[:, eo, c0:c0 + n], ALU.mult)
                        S.tt("pool", gb_, gb_[:, eo, c0:c0 + n], sg, sg[:, 0:n], z, z[:, c0:c0 + n], ALU.mult)
                for dc in range(KC):
                    w = gw[wi % 2]; wi += 1
                    S.dma(w, w[:], W["l0_out_w"], ov[:, :, dc * 128:(dc + 1) * 128], q="pool")
                    x_ = xb[dc % 2]
                    S.dma(x_, x_[:], xT.bufs(0, D, h0, h0 + HW_), xT.ap[dc * 128:(dc + 1) * 128, h0:h0 + HW_])
                    for (c0, n, r) in subs:
                        pb = ps[4 + pi_ % 4]; pi_ += 1
                        for ei in range(EC):
                            S.mm(pb, pb[:, 0:n], w, w[:, ei, :], gb_, gb_[:, ei, c0:c0 + n], start=(ei == 0), stop=(ei == EC - 1))
                        S.stt("dve", x_, x_[:, c0:c0 + n], pb, pb[:, 0:n], modG[:, dc, r:r + 1], x_, x_[:, c0:c0 + n],
                              ALU.mult, ALU.add, extra_reads=[modG])
                    S.dma(xT.bufs(0, D, h0, h0 + HW_), xT.ap[dc * 128:(dc + 1) * 128, h0:h0 + HW_], x_, x_[:])
            S.pop()

        cst2 = S.sbuf("cst2", [128, C2_W])
        S.dma(cst2, cst2[:], cst2_in, cst2_in[:, :])
        bones = cst2[:, C2_BONES:C2_BONES + 128]
        rT = DramT(S, "rT", [E, T], F32, 128, T)
        kT = DramT(S, "kT", [E, T], F32, 128, T)
        v1T = DramT(S, "v1T", [E, T], F32, 128, T)
        vtm = DramT(S, "vtm", [T, E], F32, T, 128)
        lwT = [DramT(S, "lwT%d" % d, [E, T], F32, 128, T) for d in range(2)]
        arT = [DramT(S, "arT%d" % d, [E, T], F32, 128, T) for d in range(2)]
        gT = vT

        def colvec(name, src_ap_1d, n):
            t_ = S.sbuf(name, [128, n])
            return t_

        def stage_l1_proj(hT):
            S.push()
            ps = mkps()
            xi = S.sbuf("xi", [128, KC, T], BF16)
            mu = S.sbuf("mu", [128, 6, KC])
            for i in range(6):
                S.dma(mu, mu[:, i, :], W["l1_mu"], W["l1_mu"].t[i, :].rearrange("(kc p) -> p kc", p=128), slow=True)
            omu = S.sbuf("omu", [128, 6, KC])
            S.ts("dve", omu, omu[:], mu, mu[:], -1.0, 1.0, ALU.mult, ALU.add)
            wb = [S.sbuf("l1w", [128, KC, 128], BF16) for _ in range(2)]
            st32 = [S.sbuf("st32", [128, T]) for _ in range(2)]
            stbf = [S.sbuf("stbf", [128, T], BF16)] * 2

            def build_xi(i):
                for kc in range(KC):
                    eng = "dve" if kc % 2 else "pool"
                    S.ts(eng, xi, xi[:, kc, :], hT, hT[:, kc, :], omu[:, i, kc:kc + 1], None, ALU.mult, extra_reads=[omu])
                    m = mu[:, i, kc:kc + 1]
                    xl = xi[:, kc, TC:T].rearrange("p (r c) -> p r c", c=64)
                    hl = hT[:, kc, TC:T].rearrange("p (r c) -> p r c", c=64)
                    q = kc // 4
                    if q == 0:
                        o_, s_ = xl[:, :, 1:64], hl[:, :, 0:63]
                    elif q == 1:
                        o_, s_ = xl[:, :, 0:63], hl[:, :, 1:64]
                    elif q == 2:
                        o_, s_ = xl[:, 1:32, :], hl[:, 0:31, :]
                    else:
                        o_, s_ = xl[:, 0:31, :], hl[:, 1:32, :]
                    S.stt("dve", xi, o_, hT, s_, m, xi, o_, ALU.mult, ALU.add, extra_reads=[mu])
                    if kc < 8:
                        o_, s_ = xi[:, kc, 1:TC], hT[:, kc, 0:TC - 1]
                    else:
                        o_, s_ = xi[:, kc, 0:TC - 1], hT[:, kc, 1:TC]
                    S.stt("dve", xi, o_, hT, s_, m, xi, o_, ALU.mult, ALU.add, extra_reads=[mu])

            cnt = [0]

            def fm_proj(widx, dst, silu=False):
                wv = W["l1_in_w"].t[widx].rearrange("(kc p) e -> p kc e", p=128)
                for ec in range(EC):
                    w = wb[cnt[0] % 2]
                    S.dma(w, w[:], W["l1_in_w"], wv[:, :, ec * 128:(ec + 1) * 128], q="pool")
                    if silu:
                        st = stbf[cnt[0] % 2]
                        proj_chunk(xi, w, ps[0:4], lambda tb, c0, n, pb: S.act(st, st[:, c0:c0 + n], pb, pb[:, 0:n], AF.Silu))
                    else:
                        st = st32[cnt[0] % 2]
                        proj_chunk(xi, w, ps[0:4], lambda tb, c0, n, pb: S.copy("act" if tb % 2 else "dve", st, st[:, c0:c0 + n], pb, pb[:, 0:n]))
                    S.dma(dst.bufs(ec * 128, ec * 128 + 128, 0, T), dst.ap[ec * 128:(ec + 1) * 128, :], st, st[:])
                    cnt[0] += 1

            build_xi(0); fm_proj(0, rT)
            build_xi(1); fm_proj(1, kT)
            build_xi(2); fm_proj(2, v1T)
            S.push()
            wt = [S.sbuf("wvt", [128, KC, 256], BF16)] * 2
            vs = [S.sbuf("vs", [128, 256]) for _ in range(2)]
            wv = W["l1_in_w"].t[2].rearrange("(kc p) e -> p kc e", p=128)
            for cb in range(16):
                w = wt[cb % 2]
                S.dma(w, w[:], W["l1_in_w"], wv[:, :, cb * 256:(cb + 1) * 256], q="pool")
                for tt in range(NCH):
                    pb = ps[4 + tt % 4]
                    for kc in range(KC):
                        S.mm(pb, pb[:, 0:256], xi, xi[:, kc, tt * 128:(tt + 1) * 128], w, w[:, kc, :], start=(kc == 0), stop=(kc == KC - 1))
                    o = vs[tt % 2]
                    S.copy("act" if tt % 2 else "dve", o, o[:], pb, pb[:, 0:256])
                    S.dma(vtm.bufs(0, T, cb * 256, cb * 256 + 256), vtm.ap[tt * 128:(tt + 1) * 128, cb * 256:(cb + 1) * 256], o, o[:])
            S.pop()
            build_xi(3); fm_proj(3, szT, silu=True)
            S.push()
            l1 = S.sbuf("lora1", [128, KC, 96], BF16)
            l2 = S.sbuf("lora2", [96, E], BF16)
            lmid = S.sbuf("lmid", [96, T], BF16)
            b0 = S.sbuf("lb0", [128, EC])
            for (i, n1, n2, n0, dst, is_w) in ((4, "l1_w1", "l1_w2", "l1_w0", lwT, True), (5, "l1_a1", "l1_a2", "l1_a0", arT, False)):
                build_xi(i)
                for d in range(2):
                    S.dma(l1, l1[:], W[n1], W[n1].t[d].rearrange("(kc p) r -> p kc r", p=128), q="pool")
                    S.dma(l2, l2[:], W[n2], W[n2].t[d], q="pool")
                    S.dma(b0, b0[:], W[n0], W[n0].t[d, :].rearrange("(e p) -> p e", p=128), slow=True)
                    for tb in range(5):
                        c0 = tb * 512
                        n = min(512, T - c0)
                        pb = ps[tb % 4]
                        for kc in range(KC):
                            S.mm(pb, pb[0:96, 0:n], l1, l1[:, kc, :], xi, xi[:, kc, c0:c0 + n], start=(kc == 0), stop=(kc == KC - 1))
                        S.act(lmid, lmid[:, c0:c0 + n], pb, pb[0:96, 0:n], AF.Tanh if is_w else AF.Copy)
                    for ec in range(EC):
                        st = st32[cnt[0] % 2]
                        cnt[0] += 1
                        for tb in range(5):
                            c0 = tb * 512
                            n = min(512, T - c0)
                            pb = ps[4 + tb % 4]
                            S.mm(pb, pb[:, 0:n], l2, l2[:, ec * 128:(ec + 1) * 128], lmid, lmid[:, c0:c0 + n])
                            S.act(st, st[:, c0:c0 + n], pb, pb[:, 0:n], AF.Sigmoid, bias=b0[:, ec:ec + 1], extra_reads=[b0])
                        if is_w:
                            S.ts("dve", st, st[:], st, st[:], -math.exp(-0.5), None, ALU.mult)
                        S.dma(dst[d].bufs(ec * 128, ec * 128 + 128, 0, T), dst[d].ap[ec * 128:(ec + 1) * 128, :], st, st[:])
            S.pop()
            S.pop()

        def stage_l1_scan(nec=EC):
            S.push()
            L = 128
            def vec(nm, src):
                t_ = S.sbuf(nm, [128, EC])
                S.dma(t_, t_[:], W[src], W[src].t.rearrange("(e p) -> p e", p=128), slow=True)
                return t_
            kkv = vec("kkv", "l1_k_k"); kav = vec("kav", "l1_k_a"); lnw = vec("lnw", "l1_ln_w"); lnb = vec("lnb", "l1_ln_b")
            rkv = S.sbuf("rkv", [128, EC])
            S.dma(rkv, rkv[:], W["l1_r_k"], W["l1_r_k"].t.rearrange("(e h) k -> (h k) e", h=2), slow=True)
            KKN = S.sbuf("KKN", [128, T]); BON = S.sbuf("BON", [128, T]); YACC = S.sbuf("YACC", [128, T])
            RT = S.sbuf("RT", [128, T]); KT = S.sbuf("KT", [128, T]); AR = S.sbuf("AR", [128, T]); LW = S.sbuf("LW", [128, T])
            CIN = S.sbuf("CIN", [128, T]); TMP = S.sbuf("TMP", [128, T]); TMP2 = S.sbuf("TMP2", [128, T]); TMP3 = S.sbuf("TMP3", [128, T]); AT = S.sbuf("AT", [128, T]); BT = S.sbuf("BT", [128, T])
            GL = S.sbuf("GL", [128, NCH])
            Vp = S.sbuf("Vp", [128, NCH, 128])
            VpA = S.sbuf("VpA", [128, NCH, 128]); VpB = S.sbuf("VpB", [128, NCH, 128])
            S.emit("pool", lambda e: e.memset(VpA[:], 0.0), [], [VpA])
            S.emit("pool", lambda e: e.memset(VpB[:], 0.0), [], [VpB])
            szb = S.sbuf("szb", [128, T], BF16)
            gst = S.sbuf("gst", [128, T], BF16)
            Sbd = [S.sbuf("Sbd", [128, 128]) for _ in range(2)]
            for s_ in Sbd:
                S.emit("dve", lambda e, s_=s_: e.memset(s_[:], 0.0), [], [s_])
            M4 = [[S.sbuf("M4", [128, 4, 128]) for _ in range(3)] for _ in range(2)]
            XN = [[[S.sbuf("XN", [128, 2, 128]) for _ in range(2)] for _ in range(2)] for _ in range(2)]
            Wm = [[S.sbuf("Wm", [128, 128]) for _ in range(6)] for _ in range(2)]
            RHS = S.sbuf("RHS", [128, 128]); Us = S.sbuf("Us", [128, 128])
            UpA = S.sbuf("UpA", [128, 128]); UpB = S.sbuf("UpB", [128, 128])
            S.emit("dve", lambda e: e.memset(UpA[:], 0.0), [], [UpA])
            S.emit("dve", lambda e: e.memset(UpB[:], 0.0), [], [UpB])
            bk = S.sbuf("bk", [128, 2, 128]); bkT = S.sbuf("bkT", [128, 2, 128])
            pM = [S.psum("pM%d" % h, [128, 512]) for h in range(2)]
            pC = [[S.psum("pC%d%d" % (h, q), [128, 512]) for q in range(2)] for h in range(2)]
            pR = S.psum("pR", [128, 512]); pS = S.psum("pS", [128, 512])
            pYy = pR
            v3 = lambda t_: t_[:, :].rearrange("p (c l) -> p c l", l=L)

            for ec in range(nec):
                rows = (ec * 128, ec * 128 + 128)
                S.dma(KT, KT[:], kT.bufs(*rows, 0, T), kT.ap[rows[0]:rows[1], :])
                S.dma(RT, RT[:], rT.bufs(*rows, 0, T), rT.ap[rows[0]:rows[1], :])
                S.dma(AT, AT[:], v1T.bufs(*rows, 0, T), v1T.ap[rows[0]:rows[1], :])
                S.dma(AR, AR[:], arT[0].bufs(*rows, 0, T), arT[0].ap[rows[0]:rows[1], :])
                S.dma(LW, LW[:], arT[1].bufs(*rows, 0, T), arT[1].ap[rows[0]:rows[1], :])
                S.dma(Vp, Vp[:], vtm.bufs(0, T, rows[0], rows[1]), vtm.ap.rearrange("(c p) e -> p c e", p=128)[:, :, rows[0]:rows[1]])
                S.dma(szb, szb[:], szT.bufs(*rows, 0, T), szT.ap[rows[0]:rows[1], :])
                S.copy("pool", VpA, VpA[:, :, 0:64], Vp, Vp[:, :, 0:64])
                S.copy("pool", VpB, VpB[:, :, 64:128], Vp, Vp[:, :, 64:128])
                S.ts("dve", KKN, KKN[:], KT, KT[:], kkv[:, ec:ec + 1], None, ALU.mult, extra_reads=[kkv])
                S.act(TMP, TMP[:], KKN, KKN[:], AF.Square)
                for tb in range(5):
                    c0 = tb * 512; n = min(512, T - c0)
                    S.mm(pR, pR[:, 0:n], cst2, bones, TMP, TMP[:, c0:c0 + n])
                    S.act(CIN, CIN[:, c0:c0 + n], pR, pR[:, 0:n], AF.Sqrt)
                S.ts("dve", CIN, CIN[:], CIN, CIN[:], 1e-12, None, ALU.max)
                S.emit("dve", lambda e: e.reciprocal(out=CIN[:], in_=CIN[:]), [CIN], [CIN])
                S.tt("dve", KKN, KKN[:], KKN, KKN[:], CIN, CIN[:], ALU.mult)
                S.tt("pool", TMP, TMP[:], AR, AR[:], LW, LW[:], ALU.add)
                S.ts("dve", TMP, TMP[:], TMP, TMP[:], -2.0, None, ALU.add)
                S.ts("dve", TMP, TMP[:], TMP, TMP[:], kav[:, ec:ec + 1], None, ALU.mult, extra_reads=[kav])
                S.ts("dve", TMP, TMP[:], TMP, TMP[:], 2.0, None, ALU.add)
                S.tt("pool", TMP, TMP[:], TMP, TMP[:], KT, KT[:], ALU.mult)
                S.stt("dve", TMP, TMP[:], TMP, TMP[:], rkv[:, ec:ec + 1], RT, RT[:], ALU.mult, ALU.mult, extra_reads=[rkv])
                for tb in range(5):
                    c0 = tb * 512; n = min(512, T - c0)
                    S.mm(pR, pR[:, 0:n], cst2, bones, TMP, TMP[:, c0:c0 + n])
                    S.tt("dve", BON, BON[:, c0:c0 + n], AT, AT[:, c0:c0 + n], pR, pR[:, 0:n], ALU.mult)

                for d in range(2):
                    if not (d == 0):
                        S.dma(KT, KT[:], kT.bufs(*rows, 0, T), kT.ap[rows[0]:rows[1], :])
                        S.dma(RT, RT[:], rT.bufs(*rows, 0, T), rT.ap[rows[0]:rows[1], :])
                    if d == 1:
                        S.dma(AR, AR[:], arT[1].bufs(*rows, 0, T), arT[1].ap[rows[0]:rows[1], :])
                    S.dma(LW, LW[:], lwT[d].bufs(*rows, 0, T), lwT[d].ap[rows[0]:rows[1], :])
                    for ch_ in range(NCH):
                        S.emit("dve", lambda e, ch_=ch_: e.tensor_tensor_scan(out=CIN[:, ch_ * L:(ch_ + 1) * L], data0=ones,
                                                                             data1=LW[:, ch_ * L:(ch_ + 1) * L], initial=0.0,
                                                                             op0=ALU.mult, op1=ALU.add), [LW, cst], [CIN])
                    if d == 1:
                        S.tt("dve", TMP, TMP[:], LW, LW[:], CIN, CIN[:], ALU.subtract)
                        S.tt("dve", CIN, v3(CIN), TMP, v3(TMP), CIN, v3(CIN)[:, :, L - 1:L].to_broadcast([128, NCH, L]), ALU.add)
                    S.tt("pool", BT, BT[:], KKN, KKN[:], AR, AR[:], ALU.mult)
                    S.ts("dve", TMP, TMP[:], AR, AR[:], -1.0, None, ALU.add)
                    S.ts("dve", TMP, TMP[:], TMP, TMP[:], kav[:, ec:ec + 1], None, ALU.mult, extra_reads=[kav])
                    S.ts("dve", TMP, TMP[:], TMP, TMP[:], 1.0, None, ALU.add)
                    S.tt("pool", KT, KT[:], KT, KT[:], TMP, TMP[:], ALU.mult)
                    S.act(TMP, TMP[:], CIN, CIN[:], AF.Exp, scale=-1.0)
                    S.act(TMP2, TMP2[:], CIN, CIN[:], AF.Exp)
                    S.tt("pool", TMP3, TMP3[:], CIN, CIN[:], LW, LW[:], ALU.subtract)
                    S.act(TMP3, TMP3[:], TMP3, TMP3[:], AF.Exp)
                    S.tt("dve", BT, BT[:], BT, BT[:], TMP, TMP[:], ALU.mult)
                    S.tt("pool", KT, KT[:], KT, KT[:], TMP, TMP[:], ALU.mult)
                    S.tt("dve", RT, RT[:], RT, RT[:], TMP2, TMP2[:], ALU.mult)
                    S.stt("dve", AT, AT[:], KKN, KKN[:], -1.0, TMP3, TMP3[:], ALU.mult, ALU.mult)
                    endc = (L - 1) if d == 0 else 0
                    S.act(GL, GL[:], CIN, v3(CIN)[:, :, endc], AF.Exp)
                    cur = 0
                    for h in range(2):
                        pb_ = 64 * h
                        S.emit("dve", lambda e, h=h, pb_=pb_: e.memset(Sbd[0][pb_:pb_ + 64, pb_:pb_ + 64], 0.0), [], [Sbd[0]])
                    order = list(range(NCH)) if d == 0 else [1, 0] + list(range(NCH - 1, 1, -1))
                    m4c = cst2[:, C2_M4 + d * 512:C2_M4 + (d + 1) * 512].rearrange("p (a t) -> p a t", a=4)
                    mNc = cst2[:, C2_MN + d * 128:C2_MN + (d + 1) * 128]
                    Wfin = {}

                    def pre_head(h, ci, ch):
                        cols = slice(ch * L, ch * L + L)
                        hp = slice(64 * h, 64 * h + 64)
                        m4 = M4[h][ci % 3]
                        pc = pC[h][ci % 2]
                        S.mm(pM[h], pM[h][:, 0:128], BT, BT[hp, cols], AT, AT[hp, cols])
                        S.mm(pM[h], pM[h][:, 128:256], BT, BT[hp, cols], RT, RT[hp, cols])
                        S.mm(pM[h], pM[h][:, 256:384], KT, KT[hp, cols], AT, AT[hp, cols])
                        S.mm(pM[h], pM[h][:, 384:512], KT, KT[hp, cols], RT, RT[hp, cols])
                        S.mm(pc, pc[:, 384:512], AT, AT[hp, cols], BT, BT[hp, cols])
                        S.tt("dve", m4, m4[:], cst2, m4c, pM[h], pM[h][:, :].rearrange("p (a t) -> p a t", a=4), ALU.mult)
                        cur_ = XN[h][ci % 2][0]
                        S.tt("dve", cur_, cur_[:, 0, :], cst2, mNc, pc, pc[:, 384:512], ALU.mult)
                        yield
                        S.copy("act", cur_, cur_[:, 1, :], m4, m4[:, 0, :])
                        Wc = Wm[h][2 * (ci % 3)]
                        S.tt("pool", Wc, Wc[:], m4, m4[:, 0, :], cst, ident, ALU.add)
                        yield
                        pp, wi_ = 0, 0
                        for j in range(6):
                            nxt_ = XN[h][ci % 2][1 - pp]
                            S.mm(pc, pc[:, 0:128], cur_, cur_[:, 1, :], cur_, cur_[:, 0, :])
                            if j < 5:
                                S.mm(pc, pc[:, 128:256], cur_, cur_[:, 0, :], cur_, cur_[:, 1, :])
                            yield
                            if j < 5:
                                S.copy("act", nxt_, nxt_[:, :, :], pc, pc[:, 0:256].rearrange("p (a b) -> p a b", a=2))
                            else:
                                S.copy("act", nxt_, nxt_[:, 0, :], pc, pc[:, 0:128])
                            yield
                            S.mm(pc, pc[:, 256:384], nxt_, nxt_[:, 0, :], Wc, Wc[:])
                            yield
                            Wn = Wm[h][2 * (ci % 3) + 1 - wi_]
                            S.tt("dve", Wn, Wn[:], Wc, Wc[:], pc, pc[:, 256:384], ALU.add)
                            yield
                            cur_, Wc = nxt_, Wn
                            pp, wi_ = 1 - pp, 1 - wi_
                        Wfin[(h, ci)] = Wc

                    def seq(ci, ch):
                        cols = slice(ch * L, ch * L + L)
                        S0, S1 = Sbd[ci % 2], Sbd[(ci + 1) % 2]
                        for h in range(2):
                            hp = slice(64 * h, 64 * h + 64)
                            m4 = M4[h][ci % 3]
                            S.mm(pR, pR[:, 64 * h:64 * h + 64], AT, AT[hp, cols], S0, S0[hp, 64 * h:64 * h + 64], start=True, stop=False)
                            S.mm(pR, pR[:, 64 * h:64 * h + 64], m4, m4[:, 2, :], Vp, Vp[:, ch, 64 * h:64 * h + 64], start=False, stop=True)
                        yield
                        S.copy("act", RHS, RHS[:], pR, pR[:, 0:128])
                        yield
                        for h in range(2):
                            Wh = Wfin[(h, ci)]
                            S.mm(pR, pR[:, 128 + 64 * h:128 + 64 * h + 64], Wh, Wh[:], RHS, RHS[:, 64 * h:64 * h + 64])
                        yield
                        S.copy("act", Us, Us[:], pR, pR[:, 128:256])
                        S.copy("dve", UpA, UpA[:, 0:64], pR, pR[:, 128:192])
                        S.copy("dve", UpB, UpB[:, 64:128], pR, pR[:, 192:256])
                        yield
                        S.mm(pYy, pYy[:, 256:384], S0, S0[:], RT, RT[:, cols], start=True, stop=False)
                        S.mm(pYy, pYy[:, 256:384], VpA, VpA[:, ch, :], M4[0][ci % 3], M4[0][ci % 3][:, 3, :], start=False, stop=False)
                        S.mm(pYy, pYy[:, 256:384], VpB, VpB[:, ch, :], M4[1][ci % 3], M4[1][ci % 3][:, 3, :], start=False, stop=False)
                        yield
                        S.mm(pYy, pYy[:, 256:384], UpA, UpA[:], M4[0][ci % 3], M4[0][ci % 3][:, 1, :], start=False, stop=False)
                        S.mm(pYy, pYy[:, 256:384], UpB, UpB[:], M4[1][ci % 3], M4[1][ci % 3][:, 1, :], start=False, stop=True)
                        yield
                        if d == 0:
                            S.copy("act", YACC, YACC[:, cols], pYy, pYy[:, 256:384])
                        else:
                            S.tt("dve", YACC, YACC[:, cols], YACC, YACC[:, cols], pYy, pYy[:, 256:384], ALU.add)
                        yield
                        S.ts("pool", bk, bk[:, 0, :], BT, BT[:, cols], GL[:, ch:ch + 1], None, ALU.mult, extra_reads=[GL])
                        S.ts("pool", bk, bk[:, 1, :], KT, KT[:, cols], GL[:, ch:ch + 1], None, ALU.mult, extra_reads=[GL])
                        yield
                        S.tr(pS, pS[:, 0:128], bk, bk[:, 0, :], cst, ident)
                        S.tr(pS, pS[:, 128:256], bk, bk[:, 1, :], cst, ident)
                        yield
                        S.copy("act", bkT, bkT[:], pS, pS[:, 0:256].rearrange("p (a b) -> p a b", a=2))
                        yield
                        S.mm(pS, pS[:, 256:384], bkT, bkT[:, 0, :], Us, Us[:], start=True, stop=False)
                        S.mm(pS, pS[:, 256:384], bkT, bkT[:, 1, :], Vp, Vp[:, ch, :], start=False, stop=True)
                        yield
                        for h in range(2):
                            hp = slice(64 * h, 64 * h + 64)
                            S.stt("dve", S1, S1[hp, 64 * h:64 * h + 64], S0, S0[hp, 64 * h:64 * h + 64], GL[hp, ch:ch + 1],
                                  pS, pS[hp, 256 + 64 * h:256 + 64 * h + 64], ALU.mult, ALU.add, extra_reads=[GL])
                        yield

                    def mkpre(ci_):
                        return [pre_head(0, ci_, order[ci_]), pre_head(1, ci_, order[ci_])]

                    def drain(must, others):
                        live = list(must) + list(others)
                        must = set(id(g) for g in must)
                        while must:
                            for g in list(live):
                                try:
                                    next(g)
                                except StopIteration:
                                    live.remove(g)
                                    must.discard(id(g))
                            if not live:
                                break
                        return [g for g in live]

                    nord = len(order)
                    ahead = drain(mkpre(0), mkpre(1) if nord > 1 else [])
                    for ci, ch in enumerate(order):
                        newp = mkpre(ci + 2) if ci + 2 < nord else []
                        ahead = drain([seq(ci, ch)] + ahead, newp)
                for tb in range(5):
                    c0 = tb * 512; n = min(512, T - c0)
                    S.mm(pR, pR[:, 0:n], cst2, bones, YACC, YACC[:, c0:c0 + n])
                    S.stt("dve", TMP, TMP[:, c0:c0 + n], pR, pR[:, 0:n], -1.0 / 64, YACC, YACC[:, c0:c0 + n], ALU.mult, ALU.add)
                    S.act(CIN, CIN[:, c0:c0 + n], TMP, TMP[:, c0:c0 + n], AF.Square)
                    S.mm(pYy, pYy[:, 0:n], cst2, bones, CIN, CIN[:, c0:c0 + n])
                    S.ts("dve", CIN, CIN[:, c0:c0 + n], pYy, pYy[:, 0:n], 1.0 / 64, GN_EPS, ALU.mult, ALU.add)
                    S.act(CIN, CIN[:, c0:c0 + n], CIN, CIN[:, c0:c0 + n], AF.Sqrt)
                    S.emit("dve", lambda e, c0=c0, n=n: e.reciprocal(out=CIN[:, c0:c0 + n], in_=CIN[:, c0:c0 + n]), [CIN], [CIN])
                    S.stt("dve", TMP, TMP[:, c0:c0 + n], TMP, TMP[:, c0:c0 + n], lnw[:, ec:ec + 1], CIN, CIN[:, c0:c0 + n],
                          ALU.mult, ALU.mult, extra_reads=[lnw])
                    S.stt("dve", TMP, TMP[:, c0:c0 + n], TMP, TMP[:, c0:c0 + n], lnb[:, ec:ec + 1], BON, BON[:, c0:c0 + n],
                          ALU.add, ALU.add, extra_reads=[lnb])
                    S.tt("pool", gst, gst[:, c0:c0 + n], TMP, TMP[:, c0:c0 + n], szb, szb[:, c0:c0 + n], ALU.mult)
                S.dma(gT.bufs(*rows, 0, T), gT.ap[rows[0]:rows[1], :], gst, gst[:])
            S.pop()

        def stage_l1_out(modG):
            S.push()
            ps = mkps()
            gb_ = S.sbuf("g1", [128, EC, 512], BF16)
            xb = S.sbuf("xb1", [128, KC, 512])
            sq = S.sbuf("sq1", [128, KC, 512])
            rstd = S.sbuf("rstd1", [128, 512])
            ot = [S.sbuf("ot", [128, D]) for _ in range(2)]
            gw = [S.sbuf("gw1", [128, EC, 128], BF16) for _ in range(2)]
            fng = S.sbuf("fng", [128, KC])
            S.dma(fng, fng[:], W["final_norm_g"], W["final_norm_g"].t.rearrange("(kc p) -> p kc", p=128), slow=True)
            ov = W["l1_out_w"].t.rearrange("(ei p) e -> p ei e", p=128)
            xTv = xT.ap.rearrange("(kc p) t -> p kc t", p=128)
            gTv = gT.ap.rearrange("(e p) t -> p e t", p=128)
            wi = 0
            oi = 0
            for (c0, n, r) in lat_blocks()[1:]:
                S.dma(gb_, gb_[:, :, 0:n], gT.bufs(0, E, 0, T), gTv[:, :, c0:c0 + n])
                S.dma(xb, xb[:, :, 0:n], xT.bufs(0, D, c0, c0 + n), xTv[:, :, c0:c0 + n])
                for dc in range(KC):
                    w = gw[wi % 2]; wi += 1
                    S.dma(w, w[:], W["l1_out_w"], ov[:, :, dc * 128:(dc + 1) * 128], q="pool")
                    pb = ps[dc % 4]
                    for ei in range(EC):
                        S.mm(pb, pb[:, 0:n], w, w[:, ei, :], gb_, gb_[:, ei, 0:n], start=(ei == 0), stop=(ei == EC - 1))
                    S.stt("dve", xb, xb[:, dc, 0:n], pb, pb[:, 0:n], modG[:, dc, r:r + 1], xb, xb[:, dc, 0:n],
                          ALU.mult, ALU.add, extra_reads=[modG])
                S.act(sq, sq[:, :, 0:n], xb, xb[:, :, 0:n], AF.Square)
                pb = ps[4]
                for kc in range(KC):
                    S.mm(pb, pb[:, 0:n], cst, ones, sq, sq[:, kc, 0:n], start=(kc == 0), stop=(kc == KC - 1))
                S.ts("dve", rstd, rstd[:, 0:n], pb, pb[:, 0:n], 1.0 / D, RMS_EPS, ALU.mult, ALU.add)
                S.act(rstd, rstd[:, 0:n], rstd, rstd[:, 0:n], AF.Sqrt)
                S.emit("dve", lambda e, n=n: e.reciprocal(out=rstd[:, 0:n], in_=rstd[:, 0:n]), [rstd], [rstd])
                for kc in range(KC):
                    S.stt("dve", xb, xb[:, kc, 0:n], xb, xb[:, kc, 0:n], fng[:, kc:kc + 1], rstd, rstd[:, 0:n],
                          ALU.mult, ALU.mult, extra_reads=[fng])
                for tt in range(n // 128):
                    o = ot[oi % 2]; oi += 1
                    for q in range(4):
                        pb = ps[5 + (q % 3)]
                        for j in range(4):
                            kc = 4 * q + j
                            S.tr(pb, pb[:, j * 128:(j + 1) * 128], xb, xb[:, kc, tt * 128:(tt + 1) * 128], cst, ident)
                        S.copy("act" if q % 2 == 0 else "dve", o, o[:, q * 512:(q + 1) * 512], pb, pb[:, :])
                    t0 = c0 - TC + tt * 128
                    S.dma(out_t, out_t[t0:t0 + 128, :], o, o[:])
            S.pop()

        dbg = dbg or ""
        stage_transpose_in()
        modA = S.sbuf("modA", [128, KC, 2]); modB = S.sbuf("modB", [128, KC, 2]); modG = S.sbuf("modG", [128, KC, 2])
        l1_only = dbg.startswith("l1")
        if not l1_only:
            stage_ada("l0_", modA, modB, modG)
            S.push()
            hT = S.sbuf("hT", [128, KC, T], BF16)
            stage_norm(hT, modA, modB)
            if dbg == "h0":
                ho = Buf(nc.dram_tensor("dbg_h", [D, T], BF16, kind="ExternalOutput").ap(), "dbg_h")
                S.dma(ho, ho.t.rearrange("(kc p) t -> p kc t", p=128), hT, hT[:])
                S.finish([ho])
            stage_inproj_l0(hT)
            S.pop()
        if dbg in ("s5pre", "s5") or dbg.startswith("s5pre:"):
            stage_s5(1 if dbg == "s5" else 0)
            if dbg == "s5":
                o_ = Buf(nc.dram_tensor("dbg_v", [128, T], BF16, kind="ExternalOutput").ap(), "dbg_v")
                S.push(); tb_ = S.sbuf("dbgv", [128, T], BF16)
                S.dma(tb_, tb_[:], vT.bufs(0, 128, 0, T), vT.ap[0:128, :]); S.dma(o_, o_[:, :], tb_, tb_[:]); S.pop()
                S.finish([o_])
        elif dbg != "h0" and not l1_only:
            stage_s5()
            stage_glu_out_l0(modG)
            if dbg == "l0":
                xo = Buf(nc.dram_tensor("dbg_x", [D, T], F32, kind="ExternalOutput").ap(), "dbg_x")
                S.push()
                tb_ = S.sbuf("dbgb", [128, KC, 256])
                xTv = xT.ap.rearrange("(kc p) t -> p kc t", p=128)
                for bi in range(9):
                    S.dma(tb_, tb_[:], xT.bufs(0, D, bi * 256, bi * 256 + 256), xTv[:, :, bi * 256:(bi + 1) * 256])
                    S.dma(xo, xo.t.rearrange("(kc p) t -> p kc t", p=128)[:, :, bi * 256:(bi + 1) * 256], tb_, tb_[:])
                S.pop()
                S.finish([xo])
        if dbg in ("", "l1", "l1scan"):
            stage_ada("l1_", modA, modB, modG)
            S.push()
            hT = S.sbuf("hT1", [128, KC, T], BF16)
            stage_norm(hT, modA, modB)
            stage_l1_proj(hT)
            S.pop()
            if dbg == "l1scan":
                stage_l1_scan(1)
                o_ = Buf(nc.dram_tensor("dbg_g", [128, T], BF16, kind="ExternalOutput").ap(), "dbg_g")
                S.push(); tb_ = S.sbuf("dbgg", [128, T], BF16)
                S.dma(tb_, tb_[:], gT.bufs(0, 128, 0, T), gT.ap[0:128, :]); S.dma(o_, o_[:, :], tb_, tb_[:]); S.pop()
                S.finish([o_])
            else:
                stage_l1_scan()
                stage_l1_out(modG)
                S.finish([out_t])
        print("instructions:", S.nins, S.cnt)
    return nc


def make_consts():
    c = np.zeros((128, C_W), np.float32)
    s = np.arange(128)
    c[:, C_IDENT:C_IDENT + 128] = np.eye(128)
    c[:, C_ONES:C_ONES + 128] = 1.0
    c[:, C_TRIF:C_TRIF + 128] = (s[:, None] <= s[None, :])
    c[:, C_TRIB:C_TRIB + 128] = (s[:, None] >= s[None, :])
    c[127, C_SELF:C_SELF + 128] = 1.0
    c[0, C_SELB:C_SELB + 128] = 1.0
    c[:, C_S1 + 0] = s + 1
    c[:, C_S1 + 1] = -(s + 1)
    c[:, C_S1 + 2] = 128 - s
    c[:, C_S1 + 3] = -(128 - s)
    c[:, C_MBD:C_MBD + 8] = ((s[:, None] // 16) == np.arange(8)[None, :])
    return c


WEIGHT_NAMES = ["l0_norm_g", "l0_ada_w", "l0_ada_b", "l0_in_w", "l0_a_re", "l0_a_im", "l0_log_step", "l0_b_re",
                "l0_b_im", "l0_c_re", "l0_c_im", "l0_d", "l0_glu_w", "l0_glu_b", "l0_out_w", "l1_norm_g", "l1_ada_w",
                "l1_ada_b", "l1_mu", "l1_in_w", "l1_w0", "l1_w1", "l1_w2", "l1_a0", "l1_a1", "l1_a2", "l1_k_k",
                "l1_k_a", "l1_r_k", "l1_ln_w", "l1_ln_b", "l1_out_w", "final_norm_g"]


def make_consts2():
    c = np.zeros((128, C2_W), np.float32)
    s = np.arange(128)
    lt = (s[:, None] < s[None, :]).astype(np.float32)
    le = (s[:, None] <= s[None, :]).astype(np.float32)
    gt = (s[:, None] > s[None, :]).astype(np.float32)
    ge = (s[:, None] >= s[None, :]).astype(np.float32)
    c[:, C2_M4:C2_M4 + 512] = np.concatenate([lt, le, lt, le], 1)
    c[:, C2_M4 + 512:C2_M4 + 1024] = np.concatenate([gt, ge, gt, ge], 1)
    c[:, C2_MN:C2_MN + 128] = gt
    c[:, C2_MN + 128:C2_MN + 256] = lt
    c[:, C2_BONES:C2_BONES + 128] = ((s[:, None] // 64) == (s[None, :] // 64))
    c[:, C2_RESET:C2_RESET + 128] = (s[None, :] != 0)
    return c


def make_in_maps(inputs, cores):
    cst = make_consts()
    cst2 = make_consts2()
    maps = []
    for b in cores:
        m = {"x": np.ascontiguousarray(inputs["x"][b]), "ctx": np.ascontiguousarray(inputs["ctx"][b]),
             "cond": np.ascontiguousarray(np.stack([inputs["c"][b], inputs["c_ctx"]], 0)), "cst": cst, "cst2": cst2}
        for nm in WEIGHT_NAMES:
            m[nm] = np.ascontiguousarray(inputs[nm], dtype=np.float32)
        maps.append(m)
    return maps


def kernel(**inputs):
    nc = build_program()
    maps = make_in_maps(inputs, list(range(8)))
    res = run_bass_kernel_spmd(nc, maps, core_ids=list(range(8)))
    return np.stack([r["out"] for r in res.results], 0).astype(np.float32)
```
